# Optimizing a Trainium2 kernel written in Bass

```python
import math
import jax, jax.numpy as jnp
from jax import lax
import numpy as np

D_MODEL = 2048
BATCH = 8
SEQ = 2048
DEPTH = 1

D_RNN = D_MODEL
LRU_BLOCKS = 16
LRU_BLOCK = D_RNN // LRU_BLOCKS
LRU_CONV = 4
LRU_C = 8.0
N_HEADS = 16
HEAD_DIM = 128
IDX_HEADS = 16
IDX_DIM = 64
TOPK_MAX = 256
Q_BLOCK = 128
REL_BUCKETS = 32
REL_MAX_DIST = 128
D_FF = 3 * D_MODEL
FFN_CONV = 3
LN_EPS = 1e-5
ALPHA = (2.0 * DEPTH) ** 0.25
BETA = (8.0 * DEPTH) ** -0.25

SPLITS = (D_RNN, D_RNN, N_HEADS * HEAD_DIM, HEAD_DIM, HEAD_DIM,
          IDX_HEADS * IDX_DIM, IDX_DIM, IDX_HEADS, D_MODEL, D_MODEL)
D_IN = sum(SPLITS)
OFFSETS = tuple(int(v) for v in np.cumsum(SPLITS)[:-1])

kernel_name = "hybrid_rglru_dsa_convffn_deepnorm"


def layer_norm(x, g, b):
    xf = x.astype(jnp.float32)
    mu = jnp.mean(xf, axis=-1, keepdims=True)
    var = jnp.mean(jnp.square(xf - mu), axis=-1, keepdims=True)
    y = (xf - mu) * lax.rsqrt(var + LN_EPS) * g.astype(jnp.float32) + b.astype(jnp.float32)
    return y.astype(x.dtype)


def causal_dwconv(x, w, b):
    width = w.shape[0]
    S = x.shape[1]
    xp = jnp.pad(x, ((0, 0), (width - 1, 0), (0, 0)))
    y = b + xp[:, 0:S] * w[0]
    for j in range(1, width):
        y = y + xp[:, j:j + S] * w[j]
    return y


def t5_bucket(rel):
    max_exact = REL_BUCKETS // 2
    nf = jnp.maximum(rel, 1).astype(jnp.float32)
    large = max_exact + (jnp.log(nf / max_exact) / math.log(REL_MAX_DIST / max_exact)
                         * (REL_BUCKETS - max_exact)).astype(jnp.int32)
    large = jnp.minimum(large, REL_BUCKETS - 1)
    return jnp.where(rel < max_exact, rel, large)


def rg_lru(x, gate_a_w, gate_a_b, gate_x_w, gate_x_b, lam):
    B, S, _ = x.shape
    xb = x.reshape(B, S, LRU_BLOCKS, LRU_BLOCK)
    r = jax.nn.sigmoid(jnp.einsum('bsnj,njk->bsnk', xb, gate_a_w).reshape(B, S, D_RNN) + gate_a_b)
    i = jax.nn.sigmoid(jnp.einsum('bsnj,njk->bsnk', xb, gate_x_w).reshape(B, S, D_RNN) + gate_x_b)
    log_a = -LRU_C * r.astype(jnp.float32) * jax.nn.softplus(-lam.astype(jnp.float32))
    a = jnp.exp(log_a)
    mult = jnp.sqrt(-jnp.expm1(2.0 * log_a))
    first = (jnp.arange(S) == 0)[None, :, None]
    mult = jnp.where(first, 1.0, mult)
    u = mult * (i * x).astype(jnp.float32)

    def combine(left, right):
        a1, b1 = left
        a2, b2 = right
        return a1 * a2, a2 * b1 + b2

    _, h = lax.associative_scan(combine, (a, u), axis=1)
    return h.astype(x.dtype)


def sparse_attention(q, k, v, qi, ki, wi, rel_bias, top_k):
    B, S, H, Dh = q.shape
    nb = S // Q_BLOCK
    qi32 = qi.astype(jnp.float32)
    ki32 = ki.astype(jnp.float32)
    wi32 = wi.astype(jnp.float32) * (IDX_HEADS ** -0.5)
    key_pos = jnp.arange(S)

    def to_blocks(t):
        return jnp.moveaxis(t.reshape((B, nb, Q_BLOCK) + t.shape[2:]), 1, 0)

    q_blocks = to_blocks(q)
    qi_blocks = to_blocks(qi32)
    wi_blocks = to_blocks(wi32)
    t_blocks = jnp.arange(S, dtype=jnp.int32).reshape(nb, Q_BLOCK)
    gather = jax.vmap(lambda tb, ib: tb[ib])

    def block_fn(args):
        q_b, qi_b, w_b, t_b = args
        dots = jnp.einsum('bthd,bsd->bths', qi_b, ki32) * (IDX_DIM ** -0.5)
        score = jnp.einsum('bths,bth->bts', jax.nn.relu(dots), w_b)
        causal = key_pos[None, :] <= t_b[:, None]
        score = jnp.where(causal[None], score, -jnp.inf)
        _, idx = lax.top_k(score, top_k)
        valid = idx <= t_b[None, :, None]
        k_sel = gather(k, idx)
        v_sel = gather(v, idx)
        logits = jnp.einsum('bthd,btkd->bhtk', q_b, k_sel).astype(jnp.float32) * (Dh ** -0.5)
        rel = t_b[None, :, None] - idx
        bias = rel_bias[t5_bucket(rel)].astype(jnp.float32)
        logits = logits + jnp.transpose(bias, (0, 3, 1, 2))
        logits = jnp.where(valid[:, None], logits, -jnp.inf)
        p = jax.nn.softmax(logits, axis=-1).astype(v.dtype)
        return jnp.einsum('bhtk,btkd->bthd', p, v_sel)

    out = lax.map(block_fn, (q_blocks, qi_blocks, wi_blocks, t_blocks))
    return jnp.moveaxis(out, 0, 1).reshape(B, S, H * Dh)


def setup_inputs(seed: int = 0) -> dict:
    key = jax.random.key(seed)
    ks = jax.random.split(key, 24)

    def nrm(k, shape, scale):
        return jax.random.normal(k, shape, jnp.float32) * scale

    u = jax.random.uniform(ks[9], (DEPTH, D_RNN), jnp.float32, minval=0.9 ** 2, maxval=0.999 ** 2)
    lru_lambda = -jnp.log(jnp.expm1(-0.5 * jnp.log(u)))
    return {
        "x": nrm(ks[0], (BATCH, SEQ, D_MODEL), 1.0),
        "w_in": nrm(ks[1], (DEPTH, D_MODEL, D_IN), D_MODEL ** -0.5),
        "lru_conv_w": nrm(ks[2], (DEPTH, LRU_CONV, D_RNN), LRU_CONV ** -0.5),
        "lru_conv_b": nrm(ks[3], (DEPTH, D_RNN), 0.01),
        "lru_gate_a_w": nrm(ks[4], (DEPTH, LRU_BLOCKS, LRU_BLOCK, LRU_BLOCK), LRU_BLOCK ** -0.5),
        "lru_gate_a_b": nrm(ks[5], (DEPTH, D_RNN), 0.01),
        "lru_gate_x_w": nrm(ks[6], (DEPTH, LRU_BLOCKS, LRU_BLOCK, LRU_BLOCK), LRU_BLOCK ** -0.5),
        "lru_gate_x_b": nrm(ks[7], (DEPTH, D_RNN), 0.01),
        "lru_lambda": lru_lambda,
        "idx_knorm_g": 1.0 + nrm(ks[10], (DEPTH, IDX_DIM), 0.01),
        "idx_knorm_b": nrm(ks[11], (DEPTH, IDX_DIM), 0.01),
        "rel_bias": nrm(ks[12], (REL_BUCKETS, N_HEADS), 0.1),
        "w_proj_lru": nrm(ks[13], (DEPTH, D_RNN, D_MODEL), D_RNN ** -0.5),
        "w_proj_attn": nrm(ks[14], (DEPTH, N_HEADS * HEAD_DIM, D_MODEL), (N_HEADS * HEAD_DIM) ** -0.5),
        "w_out": nrm(ks[15], (DEPTH, D_MODEL, D_MODEL), D_MODEL ** -0.5 * BETA),
        "ln1_g": 1.0 + nrm(ks[16], (DEPTH, D_MODEL), 0.01),
        "ln1_b": nrm(ks[17], (DEPTH, D_MODEL), 0.01),
        "ffn_w_up": nrm(ks[18], (DEPTH, D_MODEL, 2 * D_FF), D_MODEL ** -0.5),
        "ffn_conv_w": nrm(ks[19], (DEPTH, FFN_CONV, 2 * D_FF), FFN_CONV ** -0.5),
        "ffn_conv_b": nrm(ks[20], (DEPTH, 2 * D_FF), 0.01),
        "ffn_w_down": nrm(ks[21], (DEPTH, D_FF, D_MODEL), D_FF ** -0.5 * BETA),
        "ln2_g": 1.0 + nrm(ks[22], (DEPTH, D_MODEL), 0.01),
        "ln2_b": nrm(ks[23], (DEPTH, D_MODEL), 0.01),
    }


def reference(x, w_in, lru_conv_w, lru_conv_b, lru_gate_a_w, lru_gate_a_b, lru_gate_x_w, lru_gate_x_b,
              lru_lambda, idx_knorm_g, idx_knorm_b, rel_bias, w_proj_lru, w_proj_attn, w_out,
              ln1_g, ln1_b, ffn_w_up, ffn_conv_w, ffn_conv_b, ffn_w_down, ln2_g, ln2_b):
    B, S, _ = x.shape
    top_k = min(TOPK_MAX, S // 4)
    h = x
    for l in range(DEPTH):
        proj = h @ w_in[l]
        lru_x, lru_g, q, k, v, qi, ki, wi, g_lru, g_att = jnp.split(proj, OFFSETS, axis=-1)
        lru_x = causal_dwconv(lru_x, lru_conv_w[l], lru_conv_b[l])
        y_lru = jax.nn.gelu(lru_g, approximate=True) * rg_lru(
            lru_x, lru_gate_a_w[l], lru_gate_a_b[l], lru_gate_x_w[l], lru_gate_x_b[l], lru_lambda[l])
        q = q.reshape(B, S, N_HEADS, HEAD_DIM)
        qi = qi.reshape(B, S, IDX_HEADS, IDX_DIM)
        ki = layer_norm(ki, idx_knorm_g[l], idx_knorm_b[l])
        y_att = sparse_attention(q, k, v, qi, ki, wi, rel_bias, top_k)
        merged = (jax.nn.sigmoid(g_lru) * (y_lru @ w_proj_lru[l])
                  + jax.nn.sigmoid(g_att) * (y_att @ w_proj_attn[l]))
        h = layer_norm(ALPHA * h + merged @ w_out[l], ln1_g[l], ln1_b[l])
        up = causal_dwconv(h @ ffn_w_up[l], ffn_conv_w[l], ffn_conv_b[l])
        up_g, up_v = jnp.split(up, 2, axis=-1)
        ffn = (jax.nn.gelu(up_g, approximate=True) * up_v) @ ffn_w_down[l]
        h = layer_norm(ALPHA * h + ffn, ln2_g[l], ln2_b[l])
    return h
```

```python
import math
import os
import contextlib
import numpy as np
import concourse.bass as bass
import concourse.mybir as mybir
from concourse.bass_utils import run_bass_kernel_spmd

F32 = mybir.dt.float32
BF16 = mybir.dt.bfloat16
AF = mybir.ActivationFunctionType
ALU = mybir.AluOpType

D = 2048
S = 2048
T = 512
NCHUNK = S // T
DFF = 6144
NEG = -1.0e30
ALPHA = 2.0 ** 0.25
LN_EPS = 1e-5
ATT_SCALE = 128.0 ** -0.5
O_LRUX, O_LRUG, O_Q, O_K, O_V, O_QI, O_KI, O_WI, O_GL, O_GA = 0, 2048, 4096, 6144, 6272, 6400, 7424, 7488, 7504, 9552

ENGS = ["pe", "act", "dve", "pool", "sp"]
NDMA_SEMS = 12
NSLOT = 4
STRICT_SYNC = True
NBIS = 22
MASKNEG = -30000.0
TILE_E = 4352


class Tok:
    __slots__ = ("w", "r")

    def __init__(self):
        self.w = None
        self.r = []


class Op:
    __slots__ = ("eng", "fn", "deps", "is_dma", "dslot", "dval", "sig", "cnt", "prev_dma")


class Prog:
    def __init__(self, nc):
        self.nc = nc
        self.ops = {e: [] for e in ENGS}
        self.ndma = {e: 0 for e in ENGS}
        self.dma_hist = {e: [] for e in ENGS}

    def add(self, eng, fn, reads=(), writes=(), dma=False):
        op = Op()
        op.eng, op.fn, op.is_dma = eng, fn, dma
        op.deps, op.sig, op.cnt, op.prev_dma, op.dslot, op.dval = [], False, None, None, None, None
        seen = set()
        rawset = set(id(t.w) for t in reads if t.w is not None)

        def consider(d):
            if d is None or id(d) in seen:
                return
            seen.add(id(d))
            if (not d.is_dma) and (not dma) and d.eng == eng:
                if eng == "pe":
                    return
                if id(d) not in rawset and not STRICT_SYNC:
                    return
            op.deps.append(d)

        for t in reads:
            consider(t.w)
        for t in writes:
            consider(t.w)
            for r in t.r:
                consider(r)
        for t in reads:
            t.r.append(op)
        for t in writes:
            t.w = op
            t.r = []
        if dma:
            m = self.ndma[eng]
            self.ndma[eng] = m + 1
            op.dslot = m % NDMA_SEMS
            op.dval = 16 * (m // NDMA_SEMS + 1)
            if m >= NDMA_SEMS:
                op.prev_dma = self.dma_hist[eng][m - NDMA_SEMS]
            self.dma_hist[eng].append(op)
        self.ops[eng].append(op)
        return op

    def emit(self, out_dma_ops=()):
        nc = self.nc
        for e in ENGS:
            for op in self.ops[e]:
                for d in op.deps:
                    if not d.is_dma:
                        d.sig = True
        for e in ENGS:
            c = 0
            for op in self.ops[e]:
                if (not op.is_dma) and op.sig:
                    c += 1
                    op.cnt = c
        with contextlib.ExitStack() as st:
            csem = {e: st.enter_context(nc.semaphore("c_" + e)) for e in ENGS}
            dsem = {e: [st.enter_context(nc.semaphore("d_%s_%d" % (e, i))) for i in range(NDMA_SEMS)]
                    for e in ENGS if self.ndma[e] > 0}
            block = st.enter_context(nc.Block())
            prog = self

            def run(ename, eng):
                waited = {}

                def wait(key, sem, val):
                    if waited.get(key, 0) >= val:
                        return
                    waited[key] = val
                    eng.wait_ge(sem, val)

                for op in prog.ops[ename]:
                    for d in op.deps:
                        if d.is_dma:
                            wait(("d", d.eng, d.dslot), dsem[d.eng][d.dslot], d.dval)
                        else:
                            wait(("c", d.eng), csem[d.eng], d.cnt)
                    if op.is_dma and op.prev_dma is not None:
                        p = op.prev_dma
                        wait(("d", p.eng, p.dslot), dsem[p.eng][p.dslot], p.dval)
                    ins = op.fn(eng)
                    if op.is_dma:
                        ins.then_inc(dsem[ename][op.dslot], 16)
                    elif op.sig:
                        ins.then_inc(csem[ename], 1)
                if ename == "sp":
                    for d in out_dma_ops:
                        wait(("d", d.eng, d.dslot), dsem[d.eng][d.dslot], d.dval)

            @block.tensor
            def _(eng):
                run("pe", eng)

            @block.scalar
            def _(eng):
                run("act", eng)

            @block.vector
            def _(eng):
                run("dve", eng)

            @block.gpsimd
            def _(eng):
                run("pool", eng)

            @block.sync
            def _(eng):
                run("sp", eng)


def tile_plan():
    plan = [("tm",)]
    fm = [("w_in", O_K)] + [("w_in", O_QI + 128 * i) for i in range(8)]
    plan.append(("fm", [fm[0], fm[1]]))
    plan.append(("fm", [fm[2], fm[3]]))
    plan.append(("fm", [fm[4], fm[5]]))
    plan.append(("fm", [fm[6], fm[7]]))
    plan.append(("fm", [fm[8]]))
    for n in range(16):
        plan.append(("lru", [("w_in", O_LRUX + 128 * n), ("w_in", O_LRUG + 128 * n)], n))
    for i in range(8):
        plan.append(("fm", [("w_in", O_Q + 256 * i), ("w_in", O_Q + 256 * i + 128)]))
    for n in range(16):
        plan.append(("fm", [("w_proj_lru", 128 * n), ("w_in", O_GL + 128 * n)]))
        plan.append(("fm", [("w_proj_attn", 128 * n), ("w_in", O_GA + 128 * n)]))
    for nc4 in range(4):
        for kh in range(2):
            plan.append(("kn", "w_out", nc4, kh))
    for n in range(48):
        plan.append(("fm", [("ffn_w_up", 128 * n), ("ffn_w_up", DFF + 128 * n)]))
    for nc4 in range(4):
        for kg in range(6):
            plan.append(("kn", "ffn_w_down", nc4, kg))
    return plan


def tile_elems(desc):
    if desc[0] == "tm":
        return 16 * 208
    if desc[0] == "fm":
        return 16 * 128 * len(desc[1])
    if desc[0] == "lru":
        return 4096 + 256
    return 4096


def pack_wstream(mats):
    plan = tile_plan()
    out = np.zeros((len(plan), 128, TILE_E), np.float32)
    w_in = mats["w_in"]
    for i, d in enumerate(plan):
        if d[0] == "tm":
            cols = np.concatenate([np.arange(O_V, O_V + 128), np.arange(O_KI, O_KI + 64), np.arange(O_WI, O_WI + 16)])
            blk = w_in[:, cols].reshape(16, 128, 208).transpose(1, 0, 2)
            out[i, :, :16 * 208] = blk.reshape(128, -1)
        elif d[0] == "fm":
            parts = [mats[m][:, c0:c0 + 128].reshape(16, 128, 128).transpose(1, 0, 2) for (m, c0) in d[1]]
            blk = np.concatenate(parts, axis=2)
            out[i, :, :blk.shape[1] * blk.shape[2]] = blk.reshape(128, -1)
        elif d[0] == "lru":
            parts = [mats[m][:, c0:c0 + 128].reshape(16, 128, 128).transpose(1, 0, 2) for (m, c0) in d[1]]
            blk = np.concatenate(parts, axis=2)
            out[i, :, :4096] = blk.reshape(128, -1)
            n = d[2]
            out[i, :, 4096:4096 + 128] = mats["lru_gate_a_w"][n]
            out[i, :, 4096 + 128:4096 + 256] = mats["lru_gate_x_w"][n]
        else:
            _, m, nc4, kg = d
            blk = mats[m][kg * 1024:(kg + 1) * 1024, nc4 * 512:(nc4 + 1) * 512].reshape(8, 128, 512).transpose(1, 0, 2)
            out[i, :, :4096] = blk.reshape(128, -1)
    return out


PV_LCW, PV_LCB, PV_BA, PV_BX, PV_LAM, PV_FCW, PV_FCB, PV_N = 0, 64, 80, 96, 112, 128, 416, 512


def t5_bucket_np(rel):
    rel = np.asarray(rel)
    nf = np.maximum(rel, 1).astype(np.float32)
    large = 16 + (np.log(nf / np.float32(16)) / np.float32(math.log(128 / 16)) * np.float32(16)).astype(np.int32)
    large = np.minimum(large, 31)
    return np.where(rel < 16, np.maximum(rel, 0), large)


def build_program(dbg=False, nchunks=NCHUNK):
    nc = bass.Bass("TRN2", target_bir_lowering=False)
    plan = tile_plan()
    NT = len(plan)
    x_d = nc.dram_tensor("x", [S, D], F32, kind="ExternalInput").ap()
    ws_d = nc.dram_tensor("wstream", [NT, 128, TILE_E], F32, kind="ExternalInput").ap()
    pvec_d = nc.dram_tensor("pvec", [128, PV_N], F32, kind="ExternalInput").ap()
    lnp_d = nc.dram_tensor("lnp", [4, 128, D], F32, kind="ExternalInput").ap()
    knp_d = nc.dram_tensor("knp", [128, 2, 128], F32, kind="ExternalInput").ap()
    btab_d = nc.dram_tensor("btab", [128, 16, 2, 128], F32, kind="ExternalInput").ap()
    rb31_d = nc.dram_tensor("rb31", [128, 16], F32, kind="ExternalInput").ap()
    out_d = nc.dram_tensor("out", [S, D], F32, kind="ExternalOutput").ap()
    dbg_d = {}
    if dbg:
        for nm, shp, dt_ in [("d_ylru", [128, 16, 512], BF16), ("d_yatt", [128, 16, 512], BF16),
                             ("d_acc", [128, 2048], F32), ("d_maskT", [128, 16, 512], BF16),
                             ("d_h1", [128, 2048], F32), ("d_merged", [128, 16, 512], BF16),
                             ("d_kT", [128, 2048], BF16), ("d_kiT", [128, 2048], BF16), ("d_V", [128, 16, 128], BF16),
                             ("d_qiT", [128, 8, 512], BF16), ("d_xT", [128, 16, 512], BF16), ("d_wis", [128, 4, 16], F32),
                             ("d_act", [128, 48, 512], BF16), ("d_tmp", [128, 10, 516], F32), ("d_cneg", [128, 16], F32)]:
            dbg_d[nm] = nc.dram_tensor(nm, shp, dt_, kind="ExternalOutput").ap()

    P = Prog(nc)
    A = nc.alloc_sbuf_tensor

    def sb(name, shape, dt_):
        return A(name, shape, dt_).ap()

    xT = sb("xT", [128, 16, 512], BF16)
    big = sb("big", [128, 48, 512], BF16)
    h1tm = sb("h1tm", [128, 4, 2048], F32)
    lnp = sb("lnp_s", [128, 2, 2048], F32)
    qiT = sb("qiT", [128, 8, 512], BF16)
    stg = sb("stg", [128, 2048], BF16)
    maskrow = stg
    NTMP = 10
    tmp = sb("tmp", [128, NTMP, 516], F32)
    xcbs = [sb("xcb%d" % i, [128, 512], BF16) for i in range(2)]
    qTh = [tmp[:, j, 0:256].bitcast(BF16) for j in range(2)]
    Eb = [tmp[:, 2 + j, 0:256].bitcast(BF16) for j in range(2)]
    Pm = [tmp[:, 4 + j, 0:256].bitcast(BF16) for j in range(2)]
    rc = tmp[:, 6, 0:512]
    kT = sb("kT", [128, 2048], BF16)
    Vb = sb("Vb", [128, 16, 128], BF16)
    kiT = sb("kiT", [128, 2048], BF16)
    ident = sb("ident", [128, 128], BF16)
    ones = sb("ones", [128, 128], BF16)
    trimask = sb("trimask", [128, 128], F32)
    DBf = sb("DBf", [128, 16, 2, 128], BF16)
    rb31 = sb("rb31_s", [128, 16], F32)
    pvec = sb("pvec_s", [128, PV_N], F32)
    cneg = sb("cneg", [128, 16], F32)
    cneg2 = sb("cneg2", [128, 16], F32)
    hprev = sb("hprev", [128, 16], F32)
    lruc = sb("lruc", [128, 16, 3], F32)
    fcar = sb("fcar", [128, 96, 2], F32)
    wis = sb("wis", [128, 4, 16], F32)
    knp = sb("knp_s", [128, 2, 128], F32)
    kdup = sb("kdup", [128, 128], BF16)
    kn32 = sb("kn32", [128, 128], F32)
    st6 = sb("st6", [128, 4, 6], F32)
    mv = sb("mv", [128, 2], F32)
    rstd = sb("rstd", [128, 1], F32)
    nb = sb("nb", [128, 1], F32)
    m8 = sb("m8", [128, 8], F32)
    thr = sb("thr", [128, 1], F32)
    epsb = sb("epsb", [128, 1], F32)
    bh0 = sb("bh0", [128, 1], F32)
    bmid = sb("bmid", [128, 1], F32)
    bcnt = sb("bcnt", [128, 1], F32)
    bch = sb("bch", [128, 1], F32)
    t_bh0, t_bmid, t_bcnt, t_bch = Tok(), Tok(), Tok(), Tok()
    ring = [sb("ring%d" % i, [128, TILE_E], BF16) for i in range(NSLOT)]
    ps = nc.alloc_psum_tensor("ps", [128, 4096], F32).ap()

    t_xT = [Tok() for _ in range(4)]
    t_big = [Tok() for _ in range(48)]
    t_h1 = [Tok() for _ in range(4)]
    t_lnp = [Tok(), Tok()]
    t_qiT = [Tok() for _ in range(8)]
    t_stg = Tok()
    t_xcbs = [Tok(), Tok()]
    t_B = [Tok() for _ in range(9)]
    t_relu = [Tok(), Tok()]
    t_maskrow = t_stg
    t_tmp = [Tok() for _ in range(NTMP)]
    t_qTh = [t_tmp[0], t_tmp[1]]
    t_Eb = [t_tmp[2], t_tmp[3]]
    t_Pm = [t_tmp[4], t_tmp[5]]
    t_rc = t_tmp[6]
    t_kT = [Tok() for _ in range(4)]
    t_V = [Tok() for _ in range(4)]
    t_kiT = [Tok() for _ in range(4)]
    t_const, t_DB, t_pvec, t_cneg, t_hprev, t_lruc, t_fcar, t_wis, t_knp = [Tok() for _ in range(9)]
    t_kdup, t_kn32, t_st, t_mv, t_rstd, t_nb, t_m8, t_thr = [Tok() for _ in range(8)]
    t_ring = [Tok() for _ in range(NSLOT)]
    t_bank = [Tok() for _ in range(8)]

    def bank_ap(b, n=512):
        return ps[:, b * 512:b * 512 + n]

    bank_rr = [0]

    phx = {"on": False}

    def nbank():
        b = bank_rr[0]
        if phx["on"] and b < 4:
            b = 4
        bank_rr[0] = (b + 1) % 8
        return b

    grp_rr = [0]

    def ngroup():
        g = grp_rr[0]
        grp_rr[0] = 1 - g
        bank_rr[0] = 0 if g == 1 else 4
        return g

    total_tiles = NT * nchunks
    wstate = {"issued": 0, "next": 0}

    def issue_upto(g):
        while wstate["issued"] <= g and wstate["issued"] < total_tiles:
            gi = wstate["issued"]
            i = gi % NT
            slot = gi % NSLOT
            ne = tile_elems(plan[i])
            P.add("pool", lambda e, i=i, slot=slot, ne=ne: e.dma_start(out=ring[slot][:, 0:ne], in_=ws_d[i, :, 0:ne]),
                  writes=[t_ring[slot]], dma=True)
            wstate["issued"] += 1

    def next_tile(expect):
        g = wstate["next"]
        wstate["next"] += 1
        i = g % NT
        assert plan[i][0] == expect, (plan[i], expect)
        issue_upto(g + NSLOT - 1)
        slot = g % NSLOT
        return ring[slot], t_ring[slot], plan[i]

    P.add("sp", lambda e: e.dma_start(out=pvec, in_=pvec_d), writes=[t_pvec], dma=True)
    P.add("sp", lambda e: e.dma_start(out=knp, in_=knp_d), writes=[t_knp], dma=True)
    bstage = h1tm[:, 0:2, :].rearrange("p a f -> p (a f)").rearrange("p (h d t) -> p h d t", h=16, d=2)
    P.add("sp", lambda e: e.dma_start(out=bstage, in_=btab_d), writes=[t_h1[0], t_h1[1]], dma=True)
    P.add("sp", lambda e: e.dma_start(out=rb31, in_=rb31_d), writes=[t_const], dma=True)
    issue_upto(NSLOT - 2)

    zsrc = tmp[:, 9, 0:128]
    gbuf = [sb("gbuf%d" % i, [128, 256], BF16) for i in range(2)]
    t_gbuf = [Tok(), Tok()]
    t_z = t_tmp[9]
    P.add("pool", lambda e: e.memset(zsrc, 0.0), writes=[t_z])
    P.add("pool", lambda e: e.affine_select(out=ident, in_=zsrc, pattern=[[-1, 128]], compare_op=ALU.not_equal, fill=1.0,
                                            base=0, channel_multiplier=1), reads=[t_z], writes=[t_const])
    P.add("pool", lambda e: e.affine_select(out=trimask, in_=zsrc, pattern=[[-1, 128]], compare_op=ALU.is_ge, fill=NEG,
                                            base=0, channel_multiplier=1), reads=[t_z], writes=[t_const])

    def setup_consts(e):
        e.memset(ones, 1.0)
        e.memset(hprev, 0.0)
        e.memset(epsb, LN_EPS)
        e.memset(lruc, 0.0)
        return e.memset(fcar, 0.0)

    P.add("pool", setup_consts, writes=[t_const, t_hprev, t_lruc, t_fcar])
    P.add("act", lambda e: e.activation(out=cneg, in_=pvec[:, PV_LAM:PV_LAM + 16], func=AF.Exp, scale=-1.0),
          reads=[t_pvec], writes=[t_cneg])
    P.add("dve", lambda e: e.tensor_scalar(out=cneg, in0=cneg, scalar1=1.0, scalar2=None, op0=ALU.add),
          reads=[t_cneg], writes=[t_cneg])
    P.add("act", lambda e: e.activation(out=cneg, in_=cneg, func=AF.Ln), reads=[t_cneg], writes=[t_cneg])
    P.add("dve", lambda e: e.tensor_scalar(out=cneg2, in0=cneg, scalar1=-16.0, scalar2=None, op0=ALU.mult),
          reads=[t_cneg], writes=[t_cneg])
    P.add("dve", lambda e: e.tensor_scalar(out=cneg, in0=cneg, scalar1=-8.0, scalar2=None, op0=ALU.mult),
          reads=[t_cneg], writes=[t_cneg])

    def setup_db(e):
        ins = None
        for h in range(16):
            ins = e.tensor_scalar(out=DBf[:, h, :, :], in0=bstage[:, h, :, :], scalar1=rb31[:, h:h + 1],
                                  scalar2=1.0 / ATT_SCALE, op0=ALU.subtract, op1=ALU.mult)
        return ins

    P.add("dve", setup_db, reads=[t_h1[0], t_h1[1], t_const], writes=[t_DB])

    out_ops = []

    def transpose_rows(src_f32, src_tok, dstT, dst_tok, b):
        P.add("act", lambda e: e.activation(out=stg, in_=src_f32, func=AF.Copy), reads=[src_tok], writes=[t_stg])
        for half in range(2):
            bk = nbank()
            pb = bank_ap(bk).bitcast(BF16)

            def fn(e, half=half, pb=pb):
                ins = None
                for j in range(8):
                    kc = half * 8 + j
                    ins = e.transpose(out=pb[:, j * 128:(j + 1) * 128], in_=stg[:, kc * 128:(kc + 1) * 128],
                                      identity=ident)
                return ins

            P.add("pe", fn, reads=[t_stg, t_const], writes=[t_bank[bk]])
            P.add("dve", lambda e, half=half, pb=pb: e.tensor_copy(
                out=dstT[:, half * 8:(half + 1) * 8, b * 128:(b + 1) * 128],
                in_=pb.rearrange("p (j t) -> p j t", t=128)), reads=[t_bank[bk]], writes=[dst_tok])

    def proj_fm(wt3, wtok, col0, rhsT, rhs_toks, bk):
        def fn(e):
            ins = None
            for kc in range(16):
                ins = e.matmul(bank_ap(bk), lhsT=wt3[:, kc, col0:col0 + 128], rhs=rhsT[:, kc, :],
                               start=(kc == 0), stop=(kc == 15))
            return ins

        P.add("pe", fn, reads=[wtok] + list(rhs_toks), writes=[t_bank[bk]])

    def layer_norm_inplace(z, ztok, gi):
        def stats(e):
            ins = None
            for k in range(4):
                ins = e.bn_stats(out=st6[:, k, :], in_=z[:, k * 512:(k + 1) * 512])
            return ins

        P.add("dve", stats, reads=[ztok], writes=[t_st])
        P.add("dve", lambda e: e.bn_aggr(out=mv, in_=st6.rearrange("p a b -> p (a b)")), reads=[t_st], writes=[t_mv])
        P.add("act", lambda e: e.activation(out=rstd, in_=mv[:, 1:2], func=AF.Sqrt, bias=epsb, scale=1.0),
              reads=[t_mv, t_const], writes=[t_rstd])
        P.add("dve", lambda e: e.reciprocal(out=rstd, in_=rstd), reads=[t_rstd], writes=[t_rstd])
        P.add("dve", lambda e: e.scalar_tensor_tensor(out=nb, in0=mv[:, 0:1], scalar=-1.0, in1=rstd,
                                                      op0=ALU.mult, op1=ALU.mult), reads=[t_mv, t_rstd], writes=[t_nb])
        P.add("dve", lambda e: e.tensor_scalar(out=z, in0=z, scalar1=rstd, scalar2=nb, op0=ALU.mult, op1=ALU.add),
              reads=[ztok, t_rstd, t_nb], writes=[ztok])
        P.add("dve", lambda e: e.tensor_tensor(out=z, in0=z, in1=lnp[:, 0, :], op=ALU.mult),
              reads=[ztok, t_lnp[0]], writes=[ztok])
        P.add("dve", lambda e: e.tensor_tensor(out=z, in0=z, in1=lnp[:, 1, :], op=ALU.add),
              reads=[ztok, t_lnp[1]], writes=[ztok])

    for c in range(nchunks):
        t0 = c * T
        NKB = 4 * (c + 1)
        LC = NKB * 128
        for b in range(4):
            P.add("sp", lambda e, b=b, t0=t0: e.dma_start(out=h1tm[:, 3, :], in_=x_d[t0 + b * 128:t0 + (b + 1) * 128, :]),
                  writes=[t_h1[3]], dma=True)
            transpose_rows(h1tm[:, 3, :], t_h1[3], xT, t_xT[b], b)

        wt, wtok, _ = next_tile("tm")
        wt3 = wt[:, 0:16 * 208].rearrange("p (k n) -> p k n", n=208)
        for b in range(4):
            jb = 4 * c + b
            bk = nbank()

            def fn(e, b=b, bk=bk, wt3=wt3):
                ins = None
                for kc in range(16):
                    ins = e.matmul(bank_ap(bk, 208), lhsT=xT[:, kc, b * 128:(b + 1) * 128], rhs=wt3[:, kc, :],
                                   start=(kc == 0), stop=(kc == 15))
                return ins

            P.add("pe", fn, reads=[wtok, t_xT[b]], writes=[t_bank[bk]])
            pb = bank_ap(bk, 208)
            P.add("act", lambda e, pb=pb, jb=jb: e.activation(out=Vb[:, jb, :], in_=pb[:, 0:128], func=AF.Copy),
                  reads=[t_bank[bk]], writes=[t_V[c]])
            P.add("act", lambda e, pb=pb, b=b: e.activation(out=wis[:, b, :], in_=pb[:, 192:208], func=AF.Copy,
                                                            scale=0.25 * 0.125),
                  reads=[t_bank[bk]], writes=[t_wis])
            P.add("dve", lambda e, pb=pb: e.bn_stats(out=st6[:, 0, :], in_=pb[:, 128:192]), reads=[t_bank[bk]], writes=[t_st])
            P.add("dve", lambda e: e.bn_aggr(out=mv, in_=st6[:, 0, :]), reads=[t_st], writes=[t_mv])
            P.add("act", lambda e: e.activation(out=rstd, in_=mv[:, 1:2], func=AF.Sqrt, bias=epsb, scale=1.0),
                  reads=[t_mv, t_const], writes=[t_rstd])
            P.add("dve", lambda e: e.reciprocal(out=rstd, in_=rstd), reads=[t_rstd], writes=[t_rstd])
            P.add("dve", lambda e: e.scalar_tensor_tensor(out=nb, in0=mv[:, 0:1], scalar=-1.0, in1=rstd,
                                                          op0=ALU.mult, op1=ALU.mult), reads=[t_mv, t_rstd], writes=[t_nb])

            def kn_fn(e, pb=pb):
                e.tensor_scalar(out=kn32[:, 0:64], in0=pb[:, 128:192], scalar1=rstd, scalar2=nb, op0=ALU.mult, op1=ALU.add)
                return e.tensor_scalar(out=kn32[:, 64:128], in0=pb[:, 128:192], scalar1=rstd, scalar2=nb,
                                       op0=ALU.mult, op1=ALU.add)

            P.add("dve", kn_fn, reads=[t_bank[bk], t_rstd, t_nb], writes=[t_kn32])
            P.add("dve", lambda e: e.tensor_tensor(out=kn32, in0=kn32, in1=knp[:, 0, :], op=ALU.mult),
                  reads=[t_kn32, t_knp], writes=[t_kn32])
            P.add("dve", lambda e: e.tensor_tensor(out=kdup, in0=kn32, in1=knp[:, 1, :], op=ALU.add),
                  reads=[t_kn32, t_knp], writes=[t_kdup])
            bk2 = nbank()
            pb2 = bank_ap(bk2).bitcast(BF16)
            P.add("pe", lambda e, pb2=pb2: e.transpose(out=pb2[:, 0:128], in_=kdup, identity=ident),
                  reads=[t_kdup, t_const], writes=[t_bank[bk2]])
            P.add("act", lambda e, pb2=pb2, jb=jb: e.activation(out=kiT[:, jb * 128:(jb + 1) * 128], in_=pb2[:, 0:128],
                                                                  func=AF.Copy), reads=[t_bank[bk2]], writes=[t_kiT[c]])

        fmi = 0
        for ti in range(5):
            wt, wtok, d = next_tile("fm")
            ncs = len(d[1])
            wt3 = wt[:, 0:16 * 128 * ncs].rearrange("p (k n) -> p k n", n=128 * ncs)
            for j in range(ncs):
                bk = nbank()
                proj_fm(wt3, wtok, j * 128, xT, t_xT, bk)
                if fmi == 0:
                    P.add("act", lambda e, bk=bk, t0=t0: e.activation(out=kT[:, t0:t0 + 512], in_=bank_ap(bk), func=AF.Copy),
                          reads=[t_bank[bk]], writes=[t_kT[c]])
                else:
                    qc = fmi - 1
                    P.add("act", lambda e, bk=bk, qc=qc: e.activation(out=qiT[:, qc, :], in_=bank_ap(bk), func=AF.Copy),
                          reads=[t_bank[bk]], writes=[t_qiT[qc]])
                fmi += 1

        acc = h1tm[:, 0, :]
        phx["on"] = True

        def idx_gen(c=c, t0=t0, NKB=NKB, LC=LC):
            for b in range(4):
                jb = 4 * c + b
                L = (jb + 1) * 128
                halves = [(0, min(L, 1024))] + ([(1024, L)] if L > 1024 else [])
                for h in range(16):
                    qc, half = h // 2, h % 2
                    for hv, (c0h, c1h) in enumerate(halves):
                        W = c1h - c0h
                        pbase = hv * 1024
                        btoks = [t_bank[hv * 2], t_bank[hv * 2 + 1]][:(W + 511) // 512]
                        relu = h1tm[:, 1, hv * 1024:hv * 1024 + W]
                        t_rl = t_relu[hv]

                        def fn(e, qc=qc, half=half, b=b, c0h=c0h, W=W, pbase=pbase):
                            ins = None
                            for p0 in range(0, W, 512):
                                w_ = min(512, W - p0)
                                ins = e.matmul(ps[:, pbase + p0:pbase + p0 + w_],
                                               lhsT=qiT[64 * half:64 * half + 64, qc, b * 128:(b + 1) * 128],
                                               rhs=kiT[64 * half:64 * half + 64, c0h + p0:c0h + p0 + w_],
                                               start=True, stop=True)
                            return ins

                        P.add("pe", fn, reads=[t_qiT[qc]] + t_kiT[:c + 1], writes=btoks)
                        P.add("act", lambda e, relu=relu, pbase=pbase, W=W: e.activation(
                            out=relu, in_=ps[:, pbase:pbase + W], func=AF.Relu), reads=btoks, writes=[t_rl])
                        if h == 0:
                            P.add("dve", lambda e, relu=relu, c0h=c0h, c1h=c1h, b=b, h=h: e.tensor_scalar(
                                out=acc[:, c0h:c1h], in0=relu, scalar1=wis[:, b, h:h + 1], scalar2=None, op0=ALU.mult),
                                  reads=[t_rl, t_wis], writes=[t_h1[0]])
                        else:
                            P.add("dve", lambda e, relu=relu, c0h=c0h, c1h=c1h, b=b, h=h: e.scalar_tensor_tensor(
                                out=acc[:, c0h:c1h], in0=relu, scalar=wis[:, b, h:h + 1], in1=acc[:, c0h:c1h],
                                op0=ALU.mult, op1=ALU.add), reads=[t_rl, t_h1[0], t_wis], writes=[t_h1[0]])
                        yield
                P.add("dve", lambda e, L=L: e.tensor_tensor(out=acc[:, L - 128:L], in0=acc[:, L - 128:L], in1=trimask, op=ALU.add),
                      reads=[t_h1[0], t_const], writes=[t_h1[0]])
                if dbg and c == nchunks - 1 and b == 3:
                    out_ops.append(P.add("sp", lambda e: e.dma_start(out=dbg_d["d_acc"], in_=acc), reads=[t_h1[0]], dma=True))
                if jb >= 2:
                    P.add("dve", lambda e, L=L: e.tensor_reduce(out=thr, in_=acc[:, 0:L - 128], axis=mybir.AxisListType.X,
                                                                op=ALU.min), reads=[t_h1[0]], writes=[t_thr])
                    P.add("dve", lambda e, L=L: e.tensor_reduce(out=bh0, in_=acc[:, 0:L], axis=mybir.AxisListType.X,
                                                                op=ALU.max), reads=[t_h1[0]], writes=[t_bh0])
                    yield
                    P.add("dve", lambda e: e.scalar_tensor_tensor(out=bh0, in0=bh0, scalar=1.0, in1=thr, op0=ALU.mult,
                                                                  op1=ALU.subtract), reads=[t_bh0, t_thr], writes=[t_bh0])
                    P.add("dve", lambda e: e.tensor_scalar(out=bh0, in0=bh0, scalar1=1.0009765625, scalar2=1e-30,
                                                           op0=ALU.mult, op1=ALU.add), reads=[t_bh0], writes=[t_bh0])
                    for it in range(NBIS):
                        sc_ = 2.0 ** -(it + 1)
                        P.add("dve", lambda e, sc_=sc_: e.scalar_tensor_tensor(out=bmid, in0=bh0, scalar=sc_, in1=thr,
                                                                               op0=ALU.mult, op1=ALU.add),
                              reads=[t_bh0, t_thr], writes=[t_bmid])
                        P.add("dve", lambda e, L=L: e.tensor_scalar(out=maskrow[:, 0:L], in0=acc[:, 0:L], scalar1=bmid,
                                                                    scalar2=None, op0=ALU.is_ge, op1=ALU.add, accum_out=bcnt),
                              reads=[t_h1[0], t_bmid], writes=[t_maskrow, t_bcnt])
                        yield
                        P.add("dve", lambda e: e.tensor_scalar(out=bch, in0=bcnt, scalar1=255.5, scalar2=bh0,
                                                               op0=ALU.is_ge, op1=ALU.mult),
                              reads=[t_bcnt, t_bh0], writes=[t_bch])
                        P.add("dve", lambda e, sc_=sc_: e.scalar_tensor_tensor(out=thr, in0=bch, scalar=sc_, in1=thr,
                                                                               op0=ALU.mult, op1=ALU.add),
                              reads=[t_bch, t_thr], writes=[t_thr])
                else:
                    P.add("dve", lambda e: e.memset(thr, -1.0e29), writes=[t_thr])
                P.add("dve", lambda e, L=L: e.tensor_scalar(out=maskrow[:, 0:L], in0=acc[:, 0:L], scalar1=thr, scalar2=MASKNEG,
                                                            op0=ALU.is_lt, op1=ALU.mult), reads=[t_h1[0], t_thr],
                      writes=[t_maskrow])
                if L < LC:
                    P.add("dve", lambda e, L=L, LC=LC: e.memset(maskrow[:, L:LC], MASKNEG), writes=[t_maskrow])
                yield
                for i0 in range(0, NKB, 8):
                    n8 = min(8, NKB - i0)
                    bk = nbank()
                    pb = bank_ap(bk).bitcast(BF16)

                    def fn(e, i0=i0, n8=n8, pb=pb):
                        ins = None
                        for j in range(n8):
                            ins = e.transpose(out=pb[:, j * 128:(j + 1) * 128],
                                              in_=maskrow[:, (i0 + j) * 128:(i0 + j + 1) * 128], identity=ident)
                        return ins

                    P.add("pe", fn, reads=[t_maskrow, t_const], writes=[t_bank[bk]])
                    P.add("act", lambda e, i0=i0, n8=n8, pb=pb, b=b: e.activation(
                        out=big[:, 32 + i0:32 + i0 + n8, b * 128:(b + 1) * 128],
                        in_=pb[:, 0:n8 * 128].rearrange("p (j t) -> p j t", t=128), func=AF.Copy),
                          reads=[t_bank[bk]], writes=t_big[32 + i0:32 + i0 + n8])
                    yield

        def lru_gen(n, c=c):
            for _once in range(1):
                wt, wtok, d = next_tile("lru")
                wt3 = wt[:, 0:4096].rearrange("p (k n) -> p k n", n=256)
                gb_ = gbuf[n % 2]
                gtok = t_gbuf[n % 2]
                P.add("pool", lambda e, gb_=gb_, wt=wt: e.tensor_copy(out=gb_, in_=wt[:, 4096:4352]), reads=[wtok], writes=[gtok])
                g4 = gb_.rearrange("p (a k) -> p a k", a=2)
                bx_, bg_, br, bi = [4 * (n % 2) + k_ for k_ in range(4)]
                proj_fm(wt3, wtok, 0, xT, t_xT, bx_)
                proj_fm(wt3, wtok, 128, xT, t_xT, bg_)
                if n % 2 == 0:
                    sl = [tmp[:, i, :] for i in range(6)]
                    tk = [t_tmp[i] for i in range(6)]
                else:
                    sl = [h1tm[:, 2 + i // 3, (i % 3) * 682:(i % 3) * 682 + 516] for i in range(6)]
                    tk = [t_B[i] for i in range(6)]
                xcb = xcbs[n % 2]
                t_xcb = t_xcbs[n % 2]
                xs, xc, ra, igu, hs, mu = sl
                t_xs, t_xc_, t_ra, t_igu, t_hs, t_mu = tk
                gg, t_gg = xs, t_xs
                P.add("dve", lambda e, n=n, xs=xs: e.tensor_copy(out=xs[:, 0:3], in_=lruc[:, n, :]), reads=[t_lruc],
                      writes=[t_xs])
                P.add("act", lambda e, xs=xs, bx_=bx_: e.activation(out=xs[:, 3:515], in_=bank_ap(bx_), func=AF.Copy),
                      reads=[t_bank[bx_]], writes=[t_xs])
                yield
                P.add("dve", lambda e, n=n, xs=xs: e.tensor_copy(out=lruc[:, n, :], in_=xs[:, 512:515]), reads=[t_xs],
                      writes=[t_lruc])
                P.add("dve", lambda e, n=n, xs=xs, xc=xc: e.tensor_scalar(
                    out=xc[:, 0:512], in0=xs[:, 3:515], scalar1=pvec[:, PV_LCW + n * 4 + 3:PV_LCW + n * 4 + 4],
                    scalar2=pvec[:, PV_LCB + n:PV_LCB + n + 1], op0=ALU.mult, op1=ALU.add),
                      reads=[t_xs, t_pvec], writes=[t_xc_])
                for j in range(3):
                    P.add("dve", lambda e, n=n, xs=xs, xc=xc, j=j: e.scalar_tensor_tensor(
                        out=xc[:, 0:512], in0=xs[:, j:j + 512], scalar=pvec[:, PV_LCW + n * 4 + j:PV_LCW + n * 4 + j + 1],
                        in1=xc[:, 0:512], op0=ALU.mult, op1=ALU.add), reads=[t_xs, t_xc_, t_pvec], writes=[t_xc_])
                yield
                P.add("act", lambda e, xc=xc, xcb=xcb: e.activation(out=xcb, in_=xc[:, 0:512], func=AF.Copy), reads=[t_xc_],
                      writes=[t_xcb])
                P.add("pe", lambda e, br=br, g4=g4, xcb=xcb: e.matmul(bank_ap(br), lhsT=g4[:, 0, :], rhs=xcb, start=True, stop=True),
                      reads=[gtok, t_xcb], writes=[t_bank[br]])
                P.add("pe", lambda e, bi=bi, g4=g4, xcb=xcb: e.matmul(bank_ap(bi), lhsT=g4[:, 1, :], rhs=xcb, start=True, stop=True),
                      reads=[gtok, t_xcb], writes=[t_bank[bi]])
                P.add("act", lambda e, n=n, br=br, ra=ra: e.activation(out=ra[:, 0:512], in_=bank_ap(br), func=AF.Sigmoid,
                                                                        bias=pvec[:, PV_BA + n:PV_BA + n + 1]),
                      reads=[t_bank[br], t_pvec], writes=[t_ra])
                P.add("act", lambda e, n=n, bi=bi, igu=igu: e.activation(out=igu[:, 0:512], in_=bank_ap(bi), func=AF.Sigmoid,
                                                                          bias=pvec[:, PV_BX + n:PV_BX + n + 1]),
                      reads=[t_bank[bi], t_pvec], writes=[t_igu])
                yield
                P.add("act", lambda e, n=n, ra=ra, mu=mu: e.activation(out=mu[:, 0:512], in_=ra[:, 0:512], func=AF.Exp,
                                                                        scale=cneg2[:, n:n + 1]),
                      reads=[t_ra, t_cneg], writes=[t_mu])
                P.add("act", lambda e, n=n, ra=ra: e.activation(out=ra[:, 0:512], in_=ra[:, 0:512], func=AF.Exp,
                                                                 scale=cneg[:, n:n + 1]),
                      reads=[t_ra, t_cneg], writes=[t_ra])
                yield
                P.add("dve", lambda e, mu=mu: e.tensor_scalar(out=mu[:, 0:512], in0=mu[:, 0:512], scalar1=-1.0, scalar2=1.0,
                                                              op0=ALU.mult, op1=ALU.add), reads=[t_mu], writes=[t_mu])
                P.add("act", lambda e, mu=mu: e.activation(out=mu[:, 0:512], in_=mu[:, 0:512], func=AF.Sqrt),
                      reads=[t_mu], writes=[t_mu])
                yield
                P.add("act", lambda e, bg_=bg_, gg=gg: e.activation(out=gg[:, 0:512], in_=bank_ap(bg_), func=AF.Gelu_apprx_tanh),
                      reads=[t_bank[bg_]], writes=[t_gg])
                P.add("dve", lambda e, igu=igu, xc=xc: e.tensor_tensor(out=igu[:, 0:512], in0=igu[:, 0:512], in1=xc[:, 0:512],
                                                                       op=ALU.mult),
                      reads=[t_igu, t_xc_], writes=[t_igu])
                yield
                if c == 0:
                    P.add("dve", lambda e, mu=mu: e.memset(mu[:, 0:1], 1.0), writes=[t_mu])
                P.add("dve", lambda e, mu=mu, igu=igu: e.tensor_tensor(out=igu[:, 0:512], in0=igu[:, 0:512], in1=mu[:, 0:512],
                                                                       op=ALU.mult),
                      reads=[t_igu, t_mu], writes=[t_igu])
                P.add("dve", lambda e, n=n, ra=ra, igu=igu, hs=hs: e.tensor_tensor_scan(
                    out=hs[:, 0:512], data0=ra[:, 0:512], data1=igu[:, 0:512], initial=hprev[:, n:n + 1],
                    op0=ALU.mult, op1=ALU.add), reads=[t_ra, t_igu, t_hprev], writes=[t_hs])
                yield
                P.add("dve", lambda e, n=n, hs=hs: e.tensor_copy(out=hprev[:, n:n + 1], in_=hs[:, 511:512]),
                      reads=[t_hs], writes=[t_hprev])
                P.add("dve", lambda e, n=n, gg=gg, hs=hs: e.tensor_tensor(out=big[:, n, :], in0=gg[:, 0:512], in1=hs[:, 0:512],
                                                                          op=ALU.mult),
                      reads=[t_gg, t_hs], writes=[t_big[n]])
                yield

        P.add("dve", lambda e: e.memset(bmid, 0.0), reads=[], writes=t_h1[1:4] + t_B + t_relu + [t_bmid])
        for _ in idx_gen():
            pass
        for pr_ in range(8):
            gens = [lru_gen(2 * pr_), lru_gen(2 * pr_ + 1)]
            while gens:
                for g_ in list(gens):
                    try:
                        next(g_)
                    except StopIteration:
                        gens.remove(g_)
        phx["on"] = False
        if dbg and c == nchunks - 1:
            out_ops.append(P.add("sp", lambda e: e.dma_start(out=dbg_d["d_ylru"], in_=big[:, 0:16, :]),
                                 reads=t_big[0:16], dma=True))
            out_ops.append(P.add("sp", lambda e: e.dma_start(out=dbg_d["d_tmp"], in_=tmp), reads=t_tmp, dma=True))
            out_ops.append(P.add("sp", lambda e: e.dma_start(out=dbg_d["d_cneg"], in_=cneg), reads=[t_cneg], dma=True))

        for hp in range(8):
            wt, wtok, d = next_tile("fm")
            wt3 = wt[:, 0:4096].rearrange("p (k n) -> p k n", n=256)
            for j in range(2):
                h = 2 * hp + j
                lg_rr = [h % 4]

                def nbank4():
                    lg_rr[0] = (lg_rr[0] + 1) % 4
                    return lg_rr[0]

                bq = nbank4()
                proj_fm(wt3, wtok, j * 128, xT, t_xT, bq)
                P.add("act", lambda e, bq=bq, j=j: e.activation(out=qTh[j], in_=bank_ap(bq), func=AF.Copy),
                      reads=[t_bank[bq]], writes=[t_qTh[j]])
                bo, bd = (4, 5) if j == 0 else (6, 7)
                bls = {}

                def emit_front(i, j=j, h=h):
                    bl = nbank4()
                    bls[i] = bl
                    near = [(b, 4 * c + b - i) for b in range(4) if (4 * c + b - i) in (0, 1)]

                    def fn(e, bl=bl, i=i, j=j, h=h, near=near):
                        e.matmul(bank_ap(bl), lhsT=kT[:, i * 128:(i + 1) * 128], rhs=qTh[j], start=True, stop=False)
                        ins = e.matmul(bank_ap(bl), lhsT=ident, rhs=big[:, 32 + i, :], start=False, stop=(len(near) == 0))
                        for k_, (b, dd) in enumerate(near):
                            ins = e.matmul(bank_ap(bl)[:, b * 128:(b + 1) * 128], lhsT=ident, rhs=DBf[:, h, dd, :],
                                           start=False, stop=(k_ == len(near) - 1))
                        return ins

                    P.add("pe", fn, reads=[t_kT[i // 4], t_qTh[j], t_big[32 + i], t_const, t_DB], writes=[t_bank[bl]])

                emit_front(0)
                if NKB > 1:
                    emit_front(1)
                for i in range(NKB):
                    if i + 2 < NKB:
                        emit_front(i + 2)
                    bl = bls[i]
                    es = i % 2
                    P.add("act", lambda e, bl=bl, es=es, h=h: e.activation(out=Eb[es], in_=bank_ap(bl), func=AF.Exp,
                                                                            scale=ATT_SCALE, bias=rb31[:, h:h + 1]),
                          reads=[t_bank[bl], t_const], writes=[t_Eb[es]])

                    def pv(e, i=i, es=es, bo=bo, bd=bd, NKB=NKB):
                        e.matmul(bank_ap(bo), lhsT=Vb[:, i, :], rhs=Eb[es], start=(i == 0), stop=(i == NKB - 1))
                        return e.matmul(bank_ap(bd), lhsT=ones, rhs=Eb[es], start=(i == 0), stop=(i == NKB - 1))

                    P.add("pe", pv, reads=[t_V[i // 4], t_Eb[es], t_const], writes=[t_bank[bo], t_bank[bd]])
                P.add("dve", lambda e, bd=bd: e.reciprocal(out=rc, in_=bank_ap(bd)), reads=[t_bank[bd]], writes=[t_rc])
                P.add("dve", lambda e, bo=bo, h=h: e.tensor_tensor(out=big[:, 16 + h, :], in0=bank_ap(bo), in1=rc, op=ALU.mult),
                      reads=[t_bank[bo], t_rc], writes=[t_big[16 + h]])
        if dbg and c == nchunks - 1:
            out_ops.append(P.add("sp", lambda e: e.dma_start(out=dbg_d["d_yatt"], in_=big[:, 16:32, :]),
                                 reads=t_big[16:32], dma=True))

        for n in range(16):
            so = (n % 2) * 4
            s1, s2, m1, m2 = [tmp[:, so + i, :] for i in range(4)]
            ts1, ts2, tm1, tm2 = [t_tmp[so + i] for i in range(4)]
            wt, wtok, d = next_tile("fm")
            wt3 = wt[:, 0:4096].rearrange("p (k n) -> p k n", n=256)
            bA, bG1 = nbank(), nbank()
            proj_fm(wt3, wtok, 0, big[:, 0:16, :], t_big[0:16], bA)
            proj_fm(wt3, wtok, 128, xT, t_xT, bG1)
            wt, wtok, d = next_tile("fm")
            wt3 = wt[:, 0:4096].rearrange("p (k n) -> p k n", n=256)
            bB, bG2 = nbank(), nbank()
            proj_fm(wt3, wtok, 0, big[:, 16:32, :], t_big[16:32], bB)
            proj_fm(wt3, wtok, 128, xT, t_xT, bG2)
            P.add("act", lambda e, bG1=bG1, s1=s1: e.activation(out=s1[:, 0:512], in_=bank_ap(bG1), func=AF.Sigmoid),
                  reads=[t_bank[bG1]], writes=[ts1])
            P.add("act", lambda e, bG2=bG2, s2=s2: e.activation(out=s2[:, 0:512], in_=bank_ap(bG2), func=AF.Sigmoid),
                  reads=[t_bank[bG2]], writes=[ts2])
            P.add("dve", lambda e, bA=bA, s1=s1, m1=m1: e.tensor_tensor(out=m1[:, 0:512], in0=s1[:, 0:512], in1=bank_ap(bA),
                                                                        op=ALU.mult),
                  reads=[ts1, t_bank[bA]], writes=[tm1])
            P.add("dve", lambda e, bB=bB, s2=s2, m2=m2: e.tensor_tensor(out=m2[:, 0:512], in0=s2[:, 0:512], in1=bank_ap(bB),
                                                                        op=ALU.mult),
                  reads=[ts2, t_bank[bB]], writes=[tm2])
            P.add("dve", lambda e, n=n, m1=m1, m2=m2: e.tensor_tensor(out=big[:, 32 + n, :], in0=m1[:, 0:512],
                                                                      in1=m2[:, 0:512], op=ALU.add),
                  reads=[tm1, tm2], writes=[t_big[32 + n]])
        if dbg and c == nchunks - 1:
            out_ops.append(P.add("sp", lambda e: e.dma_start(out=dbg_d["d_merged"], in_=big[:, 32:48, :]),
                                 reads=t_big[32:48], dma=True))

        for b in range(4):
            P.add("sp", lambda e, b=b, t0=t0: e.dma_start(out=h1tm[:, b, :], in_=x_d[t0 + b * 128:t0 + (b + 1) * 128, :]),
                  writes=[t_h1[b]] + t_B + t_relu, dma=True)
        P.add("sp", lambda e: e.dma_start(out=lnp[:, 0, :], in_=lnp_d[0]), writes=[t_lnp[0]], dma=True)
        P.add("sp", lambda e: e.dma_start(out=lnp[:, 1, :], in_=lnp_d[1]), writes=[t_lnp[1]], dma=True)
        for nc4 in range(4):
            banks = [nbank() for _ in range(4)]
            for kh in range(2):
                wt, wtok, d = next_tile("kn")
                wt3 = wt[:, 0:4096].rearrange("p (k n) -> p k n", n=512)
                for b in range(4):
                    def fn(e, b=b, kh=kh, wt3=wt3, bk=banks[b]):
                        ins = None
                        for kc in range(8):
                            ins = e.matmul(bank_ap(bk), lhsT=big[:, 32 + kh * 8 + kc, b * 128:(b + 1) * 128],
                                           rhs=wt3[:, kc, :], start=(kh == 0 and kc == 0), stop=(kh == 1 and kc == 7))
                        return ins

                    P.add("pe", fn, reads=[wtok] + t_big[32 + kh * 8:32 + kh * 8 + 8], writes=[t_bank[banks[b]]])
            for b in range(4):
                zc = h1tm[:, b, nc4 * 512:(nc4 + 1) * 512]
                P.add("dve", lambda e, zc=zc, bk=banks[b]: e.scalar_tensor_tensor(
                    out=zc, in0=zc, scalar=ALPHA, in1=bank_ap(bk), op0=ALU.mult, op1=ALU.add),
                      reads=[t_h1[b], t_bank[banks[b]]], writes=[t_h1[b]])
        for b in range(4):
            layer_norm_inplace(h1tm[:, b, :], t_h1[b], 0)
            transpose_rows(h1tm[:, b, :], t_h1[b], xT, t_xT[b], b)
        if dbg and c == nchunks - 1:
            out_ops.append(P.add("sp", lambda e: e.dma_start(out=dbg_d["d_h1"], in_=h1tm[:, 3, :]), reads=[t_h1[3]], dma=True))

        for n in range(48):
            wt, wtok, d = next_tile("fm")
            wt3 = wt[:, 0:4096].rearrange("p (k n) -> p k n", n=256)
            bg_, bv_ = nbank(), nbank()
            proj_fm(wt3, wtok, 0, xT, t_xT, bg_)
            proj_fm(wt3, wtok, 128, xT, t_xT, bv_)
            so = (n % 2) * 5
            xg, xv, cg, cv, gl = [tmp[:, so + i, :] for i in range(5)]
            tg, tv, tcg, tcv, tgl = [t_tmp[so + i] for i in range(5)]
            for (xx, tx, bk_, ch, co, tco) in ((xg, tg, bg_, n, cg, tcg), (xv, tv, bv_, 48 + n, cv, tcv)):
                P.add("dve", lambda e, xx=xx, ch=ch: e.tensor_copy(out=xx[:, 0:2], in_=fcar[:, ch, :]), reads=[t_fcar],
                      writes=[tx])
                P.add("act", lambda e, xx=xx, bk_=bk_: e.activation(out=xx[:, 2:514], in_=bank_ap(bk_), func=AF.Copy),
                      reads=[t_bank[bk_]], writes=[tx])
                P.add("dve", lambda e, xx=xx, ch=ch: e.tensor_copy(out=fcar[:, ch, :], in_=xx[:, 512:514]), reads=[tx],
                      writes=[t_fcar])

                P.add("dve", lambda e, xx=xx, ch=ch, co=co: e.tensor_scalar(
                    out=co[:, 0:512], in0=xx[:, 2:514], scalar1=pvec[:, PV_FCW + ch * 3 + 2:PV_FCW + ch * 3 + 3],
                    scalar2=pvec[:, PV_FCB + ch:PV_FCB + ch + 1], op0=ALU.mult, op1=ALU.add),
                      reads=[tx, t_pvec], writes=[tco])
                for j in range(2):
                    P.add("dve", lambda e, xx=xx, ch=ch, co=co, j=j: e.scalar_tensor_tensor(
                        out=co[:, 0:512], in0=xx[:, j:j + 512], scalar=pvec[:, PV_FCW + ch * 3 + j:PV_FCW + ch * 3 + j + 1],
                        in1=co[:, 0:512], op0=ALU.mult, op1=ALU.add), reads=[tx, tco, t_pvec], writes=[tco])
            P.add("act", lambda e, cg=cg, gl=gl: e.activation(out=gl[:, 0:512], in_=cg[:, 0:512], func=AF.Gelu_apprx_tanh),
                  reads=[tcg], writes=[tgl])
            P.add("dve", lambda e, n=n, gl=gl, cv=cv: e.tensor_tensor(out=big[:, n, :], in0=gl[:, 0:512], in1=cv[:, 0:512],
                                                                      op=ALU.mult),
                  reads=[tgl, tcv], writes=[t_big[n]])
        if dbg and c == nchunks - 1:
            out_ops.append(P.add("sp", lambda e: e.dma_start(out=dbg_d["d_act"], in_=big), reads=t_big, dma=True))

        P.add("sp", lambda e: e.dma_start(out=lnp[:, 0, :], in_=lnp_d[2]), writes=[t_lnp[0]], dma=True)
        P.add("sp", lambda e: e.dma_start(out=lnp[:, 1, :], in_=lnp_d[3]), writes=[t_lnp[1]], dma=True)
        for nc4 in range(4):
            banks = [nbank() for _ in range(4)]
            for kg in range(6):
                wt, wtok, d = next_tile("kn")
                wt3 = wt[:, 0:4096].rearrange("p (k n) -> p k n", n=512)
                for b in range(4):
                    def fn(e, b=b, kg=kg, wt3=wt3, bk=banks[b]):
                        ins = None
                        for kc in range(8):
                            ins = e.matmul(bank_ap(bk), lhsT=big[:, kg * 8 + kc, b * 128:(b + 1) * 128],
                                           rhs=wt3[:, kc, :], start=(kg == 0 and kc == 0), stop=(kg == 5 and kc == 7))
                        return ins

                    P.add("pe", fn, reads=[wtok] + t_big[kg * 8:kg * 8 + 8], writes=[t_bank[banks[b]]])
            for b in range(4):
                zc = h1tm[:, b, nc4 * 512:(nc4 + 1) * 512]
                P.add("dve", lambda e, zc=zc, bk=banks[b]: e.scalar_tensor_tensor(
                    out=zc, in0=zc, scalar=ALPHA, in1=bank_ap(bk), op0=ALU.mult, op1=ALU.add),
                      reads=[t_h1[b], t_bank[banks[b]]], writes=[t_h1[b]])
        for b in range(4):
            layer_norm_inplace(h1tm[:, b, :], t_h1[b], 1)
            out_ops.append(P.add("sp", lambda e, b=b, t0=t0: e.dma_start(out=out_d[t0 + b * 128:t0 + (b + 1) * 128, :],
                                                                   in_=h1tm[:, b, :]), reads=[t_h1[b]], dma=True))

    print("sbuf bytes remaining", nc.sbuf_bytes_remaining)
    P.emit(out_dma_ops=out_ops)
    return nc


def host_pack(inputs):
    f = lambda k: np.ascontiguousarray(np.asarray(inputs[k], dtype=np.float32))
    mats = {"w_in": f("w_in")[0], "w_proj_lru": f("w_proj_lru")[0], "w_proj_attn": f("w_proj_attn")[0],
            "w_out": f("w_out")[0], "ffn_w_up": f("ffn_w_up")[0], "ffn_w_down": f("ffn_w_down")[0],
            "lru_gate_a_w": f("lru_gate_a_w")[0], "lru_gate_x_w": f("lru_gate_x_w")[0]}
    wstream = pack_wstream(mats)
    pvec = np.zeros((128, PV_N), np.float32)
    chan = lambda v: v.reshape(-1, 128).T
    lcw = f("lru_conv_w")[0]
    pvec[:, PV_LCW:PV_LCW + 64] = np.stack([chan(lcw[j]) for j in range(4)], axis=2).reshape(128, 64)
    pvec[:, PV_LCB:PV_LCB + 16] = chan(f("lru_conv_b")[0])
    pvec[:, PV_BA:PV_BA + 16] = chan(f("lru_gate_a_b")[0])
    pvec[:, PV_BX:PV_BX + 16] = chan(f("lru_gate_x_b")[0])
    pvec[:, PV_LAM:PV_LAM + 16] = chan(f("lru_lambda")[0])
    fcw = f("ffn_conv_w")[0]
    pvec[:, PV_FCW:PV_FCW + 288] = np.stack([chan(fcw[j]) for j in range(3)], axis=2).reshape(128, 288)
    pvec[:, PV_FCB:PV_FCB + 96] = chan(f("ffn_conv_b")[0])
    lnp = np.stack([np.broadcast_to(f(k)[0][None, :], (128, D)) for k in ("ln1_g", "ln1_b", "ln2_g", "ln2_b")], axis=0)
    lnp = np.ascontiguousarray(lnp)
    kg, kb = f("idx_knorm_g")[0], f("idx_knorm_b")[0]
    knp = np.zeros((128, 2, 128), np.float32)
    knp[:, 0, :] = np.concatenate([kg, kg])[None, :]
    knp[:, 1, :] = np.concatenate([kb, kb])[None, :]
    rel_bias = f("rel_bias")
    ss = np.arange(128)[:, None]
    tt = np.arange(128)[None, :]
    btab = np.zeros((128, 16, 2, 128), np.float32)
    for dd in range(2):
        rel = dd * 128 + tt - ss
        bkt = np.where(rel >= 0, t5_bucket_np(np.maximum(rel, 0)), 31)
        btab[:, :, dd, :] = rel_bias[bkt].transpose(0, 2, 1)
    rb31 = np.ascontiguousarray(np.broadcast_to(rel_bias[31][None, :], (128, 16)))
    return {"wstream": wstream, "pvec": pvec, "lnp": lnp, "knp": knp, "btab": btab, "rb31": rb31}


_CACHE = {}


def run(inputs, dbg=False, nchunks=NCHUNK, ncores=8, trace=False):
    key = (dbg, nchunks)
    shared = host_pack(inputs)
    x = np.asarray(inputs["x"], dtype=np.float32)
    nc = build_program(dbg=dbg, nchunks=nchunks)
    in_maps = []
    for b in range(ncores):
        m = dict(shared)
        m["x"] = np.ascontiguousarray(x[b])
        in_maps.append(m)
    res = run_bass_kernel_spmd(nc, in_maps, core_ids=list(range(ncores)), trace=trace)
    return res


def kernel(**inputs):
    res = run(inputs)
    out = np.stack([r["out"] for r in res.results], axis=0).astype(np.float32)
    return out
```

```python
import math
import os
import contextlib
import numpy as np
import concourse.bass as bass
import concourse.mybir as mybir
from concourse.bass_utils import run_bass_kernel_spmd

F32 = mybir.dt.float32
BF16 = mybir.dt.bfloat16
AF = mybir.ActivationFunctionType
ALU = mybir.AluOpType

D = 2048
S = 2048
T = 512
NCHUNK = S // T
DFF = 6144
NEG = -1.0e30
ALPHA = 2.0 ** 0.25
LN_EPS = 1e-5
ATT_SCALE = 128.0 ** -0.5
O_LRUX, O_LRUG, O_Q, O_K, O_V, O_QI, O_KI, O_WI, O_GL, O_GA = 0, 2048, 4096, 6144, 6272, 6400, 7424, 7488, 7504, 9552

ENGS = ["pe", "act", "dve", "pool", "sp"]
NDMA_SEMS = 12
NSLOT = 4
STRICT_SYNC = True
NBIS = 22
MASKNEG = -30000.0
TILE_E = 4352


class Tok:
    __slots__ = ("w", "r")

    def __init__(self):
        self.w = None
        self.r = []


class Op:
    __slots__ = ("eng", "fn", "deps", "is_dma", "dslot", "dval", "sig", "cnt", "prev_dma")


class Prog:
    def __init__(self, nc):
        self.nc = nc
        self.ops = {e: [] for e in ENGS}
        self.ndma = {e: 0 for e in ENGS}
        self.dma_hist = {e: [] for e in ENGS}

    def add(self, eng, fn, reads=(), writes=(), dma=False):
        op = Op()
        op.eng, op.fn, op.is_dma = eng, fn, dma
        op.deps, op.sig, op.cnt, op.prev_dma, op.dslot, op.dval = [], False, None, None, None, None
        seen = set()
        rawset = set(id(t.w) for t in reads if t.w is not None)

        def consider(d):
            if d is None or id(d) in seen:
                return
            seen.add(id(d))
            if (not d.is_dma) and (not dma) and d.eng == eng:
                if eng == "pe":
                    return
                if id(d) not in rawset and not STRICT_SYNC:
                    return
            op.deps.append(d)

        for t in reads:
            consider(t.w)
        for t in writes:
            consider(t.w)
            for r in t.r:
                consider(r)
        for t in reads:
            t.r.append(op)
        for t in writes:
            t.w = op
            t.r = []
        if dma:
            m = self.ndma[eng]
            self.ndma[eng] = m + 1
            op.dslot = m % NDMA_SEMS
            op.dval = 16 * (m // NDMA_SEMS + 1)
            if m >= NDMA_SEMS:
                op.prev_dma = self.dma_hist[eng][m - NDMA_SEMS]
            self.dma_hist[eng].append(op)
        self.ops[eng].append(op)
        return op

    def emit(self, out_dma_ops=()):
        nc = self.nc
        for e in ENGS:
            for op in self.ops[e]:
                for d in op.deps:
                    if not d.is_dma:
                        d.sig = True
        for e in ENGS:
            c = 0
            for op in self.ops[e]:
                if (not op.is_dma) and op.sig:
                    c += 1
                    op.cnt = c
        with contextlib.ExitStack() as st:
            csem = {e: st.enter_context(nc.semaphore("c_" + e)) for e in ENGS}
            dsem = {e: [st.enter_context(nc.semaphore("d_%s_%d" % (e, i))) for i in range(NDMA_SEMS)]
                    for e in ENGS if self.ndma[e] > 0}
            block = st.enter_context(nc.Block())
            prog = self

            def run(ename, eng):
                waited = {}

                def wait(key, sem, val):
                    if waited.get(key, 0) >= val:
                        return
                    waited[key] = val
                    eng.wait_ge(sem, val)

                for op in prog.ops[ename]:
                    for d in op.deps:
                        if d.is_dma:
                            wait(("d", d.eng, d.dslot), dsem[d.eng][d.dslot], d.dval)
                        else:
                            wait(("c", d.eng), csem[d.eng], d.cnt)
                    if op.is_dma and op.prev_dma is not None:
                        p = op.prev_dma
                        wait(("d", p.eng, p.dslot), dsem[p.eng][p.dslot], p.dval)
                    ins = op.fn(eng)
                    if op.is_dma:
                        ins.then_inc(dsem[ename][op.dslot], 16)
                    elif op.sig:
                        ins.then_inc(csem[ename], 1)
                if ename == "sp":
                    for d in out_dma_ops:
                        wait(("d", d.eng, d.dslot), dsem[d.eng][d.dslot], d.dval)

            @block.tensor
            def _(eng):
                run("pe", eng)

            @block.scalar
            def _(eng):
                run("act", eng)

            @block.vector
            def _(eng):
                run("dve", eng)

            @block.gpsimd
            def _(eng):
                run("pool", eng)

            @block.sync
            def _(eng):
                run("sp", eng)


def tile_plan():
    plan = [("tm",)]
    fm = [("w_in", O_K)] + [("w_in", O_QI + 128 * i) for i in range(8)]
    plan.append(("fm", [fm[0], fm[1]]))
    plan.append(("fm", [fm[2], fm[3]]))
    plan.append(("fm", [fm[4], fm[5]]))
    plan.append(("fm", [fm[6], fm[7]]))
    plan.append(("fm", [fm[8]]))
    for n in range(16):
        plan.append(("lru", [("w_in", O_LRUX + 128 * n), ("w_in", O_LRUG + 128 * n)], n))
    for i in range(8):
        plan.append(("fm", [("w_in", O_Q + 256 * i), ("w_in", O_Q + 256 * i + 128)]))
    for n in range(16):
        plan.append(("fm", [("w_proj_lru", 128 * n), ("w_in", O_GL + 128 * n)]))
        plan.append(("fm", [("w_proj_attn", 128 * n), ("w_in", O_GA + 128 * n)]))
    for nc4 in range(4):
        for kh in range(2):
            plan.append(("kn", "w_out", nc4, kh))
    for n in range(48):
        plan.append(("fm", [("ffn_w_up", 128 * n), ("ffn_w_up", DFF + 128 * n)]))
    for nc4 in range(4):
        for kg in range(6):
            plan.append(("kn", "ffn_w_down", nc4, kg))
    return plan


def tile_elems(desc):
    if desc[0] == "tm":
        return 16 * 208
    if desc[0] == "fm":
        return 16 * 128 * len(desc[1])
    if desc[0] == "lru":
        return 4096 + 256
    return 4096


def pack_wstream(mats):
    plan = tile_plan()
    out = np.zeros((len(plan), 128, TILE_E), np.float32)
    w_in = mats["w_in"]
    for i, d in enumerate(plan):
        if d[0] == "tm":
            cols = np.concatenate([np.arange(O_V, O_V + 128), np.arange(O_KI, O_KI + 64), np.arange(O_WI, O_WI + 16)])
            blk = w_in[:, cols].reshape(16, 128, 208).transpose(1, 0, 2)
            out[i, :, :16 * 208] = blk.reshape(128, -1)
        elif d[0] == "fm":
            parts = [mats[m][:, c0:c0 + 128].reshape(16, 128, 128).transpose(1, 0, 2) for (m, c0) in d[1]]
            blk = np.concatenate(parts, axis=2)
            out[i, :, :blk.shape[1] * blk.shape[2]] = blk.reshape(128, -1)
        elif d[0] == "lru":
            parts = [mats[m][:, c0:c0 + 128].reshape(16, 128, 128).transpose(1, 0, 2) for (m, c0) in d[1]]
            blk = np.concatenate(parts, axis=2)
            out[i, :, :4096] = blk.reshape(128, -1)
            n = d[2]
            out[i, :, 4096:4096 + 128] = mats["lru_gate_a_w"][n]
            out[i, :, 4096 + 128:4096 + 256] = mats["lru_gate_x_w"][n]
        else:
            _, m, nc4, kg = d
            blk = mats[m][kg * 1024:(kg + 1) * 1024, nc4 * 512:(nc4 + 1) * 512].reshape(8, 128, 512).transpose(1, 0, 2)
            out[i, :, :4096] = blk.reshape(128, -1)
    return out


PV_LCW, PV_LCB, PV_BA, PV_BX, PV_LAM, PV_FCW, PV_FCB, PV_N = 0, 64, 80, 96, 112, 128, 416, 512


def t5_bucket_np(rel):
    rel = np.asarray(rel)
    nf = np.maximum(rel, 1).astype(np.float32)
    large = 16 + (np.log(nf / np.float32(16)) / np.float32(math.log(128 / 16)) * np.float32(16)).astype(np.int32)
    large = np.minimum(large, 31)
    return np.where(rel < 16, np.maximum(rel, 0), large)


def build_program(dbg=False, nchunks=NCHUNK):
    nc = bass.Bass("TRN2", target_bir_lowering=False)
    plan = tile_plan()
    NT = len(plan)
    x_d = nc.dram_tensor("x", [S, D], F32, kind="ExternalInput").ap()
    ws_d = nc.dram_tensor("wstream", [NT, 128, TILE_E], F32, kind="ExternalInput").ap()
    pvec_d = nc.dram_tensor("pvec", [128, PV_N], F32, kind="ExternalInput").ap()
    lnp_d = nc.dram_tensor("lnp", [4, 128, D], F32, kind="ExternalInput").ap()
    knp_d = nc.dram_tensor("knp", [128, 2, 128], F32, kind="ExternalInput").ap()
    btab_d = nc.dram_tensor("btab", [128, 16, 2, 128], F32, kind="ExternalInput").ap()
    rb31_d = nc.dram_tensor("rb31", [128, 16], F32, kind="ExternalInput").ap()
    out_d = nc.dram_tensor("out", [S, D], F32, kind="ExternalOutput").ap()
    dbg_d = {}
    if dbg:
        for nm, shp, dt_ in [("d_ylru", [128, 16, 512], BF16), ("d_yatt", [128, 16, 512], BF16),
                             ("d_acc", [128, 2048], F32), ("d_maskT", [128, 16, 512], BF16),
                             ("d_h1", [128, 2048], F32), ("d_merged", [128, 16, 512], BF16),
                             ("d_kT", [128, 2048], BF16), ("d_kiT", [128, 2048], BF16), ("d_V", [128, 16, 128], BF16),
                             ("d_qiT", [128, 8, 512], BF16), ("d_xT", [128, 16, 512], BF16), ("d_wis", [128, 4, 16], F32),
                             ("d_act", [128, 48, 512], BF16), ("d_tmp", [128, 10, 516], F32), ("d_cneg", [128, 16], F32)]:
            dbg_d[nm] = nc.dram_tensor(nm, shp, dt_, kind="ExternalOutput").ap()

    P = Prog(nc)
    A = nc.alloc_sbuf_tensor

    def sb(name, shape, dt_):
        return A(name, shape, dt_).ap()

    xT = sb("xT", [128, 16, 512], BF16)
    big = sb("big", [128, 48, 512], BF16)
    h1tm = sb("h1tm", [128, 4, 2048], F32)
    lnp = sb("lnp_s", [128, 2, 2048], F32)
    qiT = sb("qiT", [128, 8, 512], BF16)
    stg = sb("stg", [128, 2048], BF16)
    maskrow = stg
    NTMP = 10
    tmp = sb("tmp", [128, NTMP, 516], F32)
    xcbs = [sb("xcb%d" % i, [128, 512], BF16) for i in range(2)]
    qTh = [tmp[:, j, 0:256].bitcast(BF16) for j in range(2)]
    Eb = [tmp[:, 2 + j, 0:256].bitcast(BF16) for j in range(2)]
    Pm = [tmp[:, 4 + j, 0:256].bitcast(BF16) for j in range(2)]
    rc = tmp[:, 6, 0:512]
    kT = sb("kT", [128, 2048], BF16)
    Vb = sb("Vb", [128, 16, 128], BF16)
    kiT = sb("kiT", [128, 2048], BF16)
    ident = sb("ident", [128, 128], BF16)
    ones = sb("ones", [128, 128], BF16)
    trimask = sb("trimask", [128, 128], F32)
    DBf = sb("DBf", [128, 16, 2, 128], BF16)
    rb31 = sb("rb31_s", [128, 16], F32)
    pvec = sb("pvec_s", [128, PV_N], F32)
    cneg = sb("cneg", [128, 16], F32)
    cneg2 = sb("cneg2", [128, 16], F32)
    hprev = sb("hprev", [128, 16], F32)
    lruc = sb("lruc", [128, 16, 3], F32)
    fcar = sb("fcar", [128, 96, 2], F32)
    wis = sb("wis", [128, 4, 16], F32)
    knp = sb("knp_s", [128, 2, 128], F32)
    kdup = sb("kdup", [128, 128], BF16)
    kn32 = sb("kn32", [128, 128], F32)
    st6 = sb("st6", [128, 4, 6], F32)
    mv = sb("mv", [128, 2], F32)
    rstd = sb("rstd", [128, 1], F32)
    nb = sb("nb", [128, 1], F32)
    m8 = sb("m8", [128, 8], F32)
    thr = sb("thr", [128, 1], F32)
    epsb = sb("epsb", [128, 1], F32)
    bh0 = sb("bh0", [128, 1], F32)
    bmid = sb("bmid", [128, 1], F32)
    bcnt = sb("bcnt", [128, 1], F32)
    bch = sb("bch", [128, 1], F32)
    t_bh0, t_bmid, t_bcnt, t_bch = Tok(), Tok(), Tok(), Tok()
    ring = [sb("ring%d" % i, [128, TILE_E], BF16) for i in range(NSLOT)]
    ps = nc.alloc_psum_tensor("ps", [128, 4096], F32).ap()

    t_xT = [Tok() for _ in range(4)]
    t_big = [Tok() for _ in range(48)]
    t_h1 = [Tok() for _ in range(4)]
    t_lnp = [Tok(), Tok()]
    t_qiT = [Tok() for _ in range(8)]
    t_stg = Tok()
    t_xcbs = [Tok(), Tok()]
    t_B = [Tok() for _ in range(9)]
    t_relu = [Tok() for _ in range(4)]
    t_R = t_relu
    t_dg = [Tok(), Tok()]
    t_maskrow = t_stg
    t_tmp = [Tok() for _ in range(NTMP)]
    t_qTh = [t_tmp[0], t_tmp[1]]
    t_Eb = [t_tmp[2], t_tmp[3]]
    t_Pm = [t_tmp[4], t_tmp[5]]
    t_rc = t_tmp[6]
    t_kT = [Tok() for _ in range(4)]
    t_V = [Tok() for _ in range(4)]
    t_kiT = [Tok() for _ in range(4)]
    t_const, t_DB, t_pvec, t_cneg, t_hprev, t_lruc, t_fcar, t_wis, t_knp = [Tok() for _ in range(9)]
    t_kdup, t_kn32, t_st, t_mv, t_rstd, t_nb, t_m8, t_thr = [Tok() for _ in range(8)]
    t_ring = [Tok() for _ in range(NSLOT)]
    t_bank = [Tok() for _ in range(8)]

    def bank_ap(b, n=512):
        return ps[:, b * 512:b * 512 + n]

    bank_rr = [0]

    phx = {"on": False}

    def nbank():
        b = bank_rr[0]
        if phx["on"] == "idx":
            b = b % 4
            bank_rr[0] = (b + 1) % 4
            return b
        bank_rr[0] = (b + 1) % 8
        return b

    grp_rr = [0]

    def ngroup():
        g = grp_rr[0]
        grp_rr[0] = 1 - g
        bank_rr[0] = 0 if g == 1 else 4
        return g

    total_tiles = NT * nchunks
    wstate = {"issued": 0, "next": 0}

    def issue_upto(g):
        while wstate["issued"] <= g and wstate["issued"] < total_tiles:
            gi = wstate["issued"]
            i = gi % NT
            slot = gi % NSLOT
            ne = tile_elems(plan[i])
            P.add("pool", lambda e, i=i, slot=slot, ne=ne: e.dma_start(out=ring[slot][:, 0:ne], in_=ws_d[i, :, 0:ne]),
                  writes=[t_ring[slot]], dma=True)
            wstate["issued"] += 1

    def next_tile(expect):
        g = wstate["next"]
        wstate["next"] += 1
        i = g % NT
        assert plan[i][0] == expect, (plan[i], expect)
        issue_upto(g + NSLOT - 1)
        slot = g % NSLOT
        return ring[slot], t_ring[slot], plan[i]

    P.add("sp", lambda e: e.dma_start(out=pvec, in_=pvec_d), writes=[t_pvec], dma=True)
    P.add("sp", lambda e: e.dma_start(out=knp, in_=knp_d), writes=[t_knp], dma=True)
    bstage = h1tm[:, 0:2, :].rearrange("p a f -> p (a f)").rearrange("p (h d t) -> p h d t", h=16, d=2)
    P.add("sp", lambda e: e.dma_start(out=bstage, in_=btab_d), writes=[t_h1[0], t_h1[1]], dma=True)
    P.add("sp", lambda e: e.dma_start(out=rb31, in_=rb31_d), writes=[t_const], dma=True)
    issue_upto(NSLOT - 2)

    zsrc = tmp[:, 9, 0:128]
    gbuf = [sb("gbuf%d" % i, [128, 256], BF16) for i in range(2)]
    t_gbuf = [Tok(), Tok()]
    t_z = t_tmp[9]
    P.add("pool", lambda e: e.memset(zsrc, 0.0), writes=[t_z])
    P.add("pool", lambda e: e.affine_select(out=ident, in_=zsrc, pattern=[[-1, 128]], compare_op=ALU.not_equal, fill=1.0,
                                            base=0, channel_multiplier=1), reads=[t_z], writes=[t_const])
    P.add("pool", lambda e: e.affine_select(out=trimask, in_=zsrc, pattern=[[-1, 128]], compare_op=ALU.is_ge, fill=NEG,
                                            base=0, channel_multiplier=1), reads=[t_z], writes=[t_const])

    def setup_consts(e):
        e.memset(ones, 1.0)
        e.memset(hprev, 0.0)
        e.memset(epsb, LN_EPS)
        e.memset(lruc, 0.0)
        return e.memset(fcar, 0.0)

    P.add("pool", setup_consts, writes=[t_const, t_hprev, t_lruc, t_fcar])
    P.add("act", lambda e: e.activation(out=cneg, in_=pvec[:, PV_LAM:PV_LAM + 16], func=AF.Exp, scale=-1.0),
          reads=[t_pvec], writes=[t_cneg])
    P.add("dve", lambda e: e.tensor_scalar(out=cneg, in0=cneg, scalar1=1.0, scalar2=None, op0=ALU.add),
          reads=[t_cneg], writes=[t_cneg])
    P.add("act", lambda e: e.activation(out=cneg, in_=cneg, func=AF.Ln), reads=[t_cneg], writes=[t_cneg])
    P.add("dve", lambda e: e.tensor_scalar(out=cneg2, in0=cneg, scalar1=-16.0, scalar2=None, op0=ALU.mult),
          reads=[t_cneg], writes=[t_cneg])
    P.add("dve", lambda e: e.tensor_scalar(out=cneg, in0=cneg, scalar1=-8.0, scalar2=None, op0=ALU.mult),
          reads=[t_cneg], writes=[t_cneg])

    def setup_db(e):
        ins = None
        for h in range(16):
            ins = e.tensor_scalar(out=DBf[:, h, :, :], in0=bstage[:, h, :, :], scalar1=rb31[:, h:h + 1],
                                  scalar2=1.0 / ATT_SCALE, op0=ALU.subtract, op1=ALU.mult)
        return ins

    P.add("dve", setup_db, reads=[t_h1[0], t_h1[1], t_const], writes=[t_DB])

    out_ops = []

    def transpose_rows(src_f32, src_tok, dstT, dst_tok, b):
        P.add("act", lambda e: e.activation(out=stg, in_=src_f32, func=AF.Copy), reads=[src_tok], writes=[t_stg])
        for half in range(2):
            bk = nbank()
            pb = bank_ap(bk).bitcast(BF16)

            def fn(e, half=half, pb=pb):
                ins = None
                for j in range(8):
                    kc = half * 8 + j
                    ins = e.transpose(out=pb[:, j * 128:(j + 1) * 128], in_=stg[:, kc * 128:(kc + 1) * 128],
                                      identity=ident)
                return ins

            P.add("pe", fn, reads=[t_stg, t_const], writes=[t_bank[bk]])
            P.add("dve", lambda e, half=half, pb=pb: e.tensor_copy(
                out=dstT[:, half * 8:(half + 1) * 8, b * 128:(b + 1) * 128],
                in_=pb.rearrange("p (j t) -> p j t", t=128)), reads=[t_bank[bk]], writes=[dst_tok])

    def proj_fm(wt3, wtok, col0, rhsT, rhs_toks, bk):
        def fn(e):
            ins = None
            for kc in range(16):
                ins = e.matmul(bank_ap(bk), lhsT=wt3[:, kc, col0:col0 + 128], rhs=rhsT[:, kc, :],
                               start=(kc == 0), stop=(kc == 15))
            return ins

        P.add("pe", fn, reads=[wtok] + list(rhs_toks), writes=[t_bank[bk]])

    def layer_norm_inplace(z, ztok, gi):
        def stats(e):
            ins = None
            for k in range(4):
                ins = e.bn_stats(out=st6[:, k, :], in_=z[:, k * 512:(k + 1) * 512])
            return ins

        P.add("dve", stats, reads=[ztok], writes=[t_st])
        P.add("dve", lambda e: e.bn_aggr(out=mv, in_=st6.rearrange("p a b -> p (a b)")), reads=[t_st], writes=[t_mv])
        P.add("act", lambda e: e.activation(out=rstd, in_=mv[:, 1:2], func=AF.Sqrt, bias=epsb, scale=1.0),
              reads=[t_mv, t_const], writes=[t_rstd])
        P.add("dve", lambda e: e.reciprocal(out=rstd, in_=rstd), reads=[t_rstd], writes=[t_rstd])
        P.add("dve", lambda e: e.scalar_tensor_tensor(out=nb, in0=mv[:, 0:1], scalar=-1.0, in1=rstd,
                                                      op0=ALU.mult, op1=ALU.mult), reads=[t_mv, t_rstd], writes=[t_nb])
        P.add("dve", lambda e: e.tensor_scalar(out=z, in0=z, scalar1=rstd, scalar2=nb, op0=ALU.mult, op1=ALU.add),
              reads=[ztok, t_rstd, t_nb], writes=[ztok])
        P.add("dve", lambda e: e.tensor_tensor(out=z, in0=z, in1=lnp[:, 0, :], op=ALU.mult),
              reads=[ztok, t_lnp[0]], writes=[ztok])
        P.add("dve", lambda e: e.tensor_tensor(out=z, in0=z, in1=lnp[:, 1, :], op=ALU.add),
              reads=[ztok, t_lnp[1]], writes=[ztok])

    for c in range(nchunks):
        t0 = c * T
        NKB = 4 * (c + 1)
        LC = NKB * 128
        for b in range(4):
            P.add("sp", lambda e, b=b, t0=t0: e.dma_start(out=h1tm[:, 3, :], in_=x_d[t0 + b * 128:t0 + (b + 1) * 128, :]),
                  writes=[t_h1[3]], dma=True)
            transpose_rows(h1tm[:, 3, :], t_h1[3], xT, t_xT[b], b)

        wt, wtok, _ = next_tile("tm")
        wt3 = wt[:, 0:16 * 208].rearrange("p (k n) -> p k n", n=208)
        for b in range(4):
            jb = 4 * c + b
            bk = nbank()

            def fn(e, b=b, bk=bk, wt3=wt3):
                ins = None
                for kc in range(16):
                    ins = e.matmul(bank_ap(bk, 208), lhsT=xT[:, kc, b * 128:(b + 1) * 128], rhs=wt3[:, kc, :],
                                   start=(kc == 0), stop=(kc == 15))
                return ins

            P.add("pe", fn, reads=[wtok, t_xT[b]], writes=[t_bank[bk]])
            pb = bank_ap(bk, 208)
            P.add("act", lambda e, pb=pb, jb=jb: e.activation(out=Vb[:, jb, :], in_=pb[:, 0:128], func=AF.Copy),
                  reads=[t_bank[bk]], writes=[t_V[c]])
            P.add("act", lambda e, pb=pb, b=b: e.activation(out=wis[:, b, :], in_=pb[:, 192:208], func=AF.Copy,
                                                            scale=0.25 * 0.125),
                  reads=[t_bank[bk]], writes=[t_wis])
            P.add("dve", lambda e, pb=pb: e.bn_stats(out=st6[:, 0, :], in_=pb[:, 128:192]), reads=[t_bank[bk]], writes=[t_st])
            P.add("dve", lambda e: e.bn_aggr(out=mv, in_=st6[:, 0, :]), reads=[t_st], writes=[t_mv])
            P.add("act", lambda e: e.activation(out=rstd, in_=mv[:, 1:2], func=AF.Sqrt, bias=epsb, scale=1.0),
                  reads=[t_mv, t_const], writes=[t_rstd])
            P.add("dve", lambda e: e.reciprocal(out=rstd, in_=rstd), reads=[t_rstd], writes=[t_rstd])
            P.add("dve", lambda e: e.scalar_tensor_tensor(out=nb, in0=mv[:, 0:1], scalar=-1.0, in1=rstd,
                                                          op0=ALU.mult, op1=ALU.mult), reads=[t_mv, t_rstd], writes=[t_nb])

            def kn_fn(e, pb=pb):
                e.tensor_scalar(out=kn32[:, 0:64], in0=pb[:, 128:192], scalar1=rstd, scalar2=nb, op0=ALU.mult, op1=ALU.add)
                return e.tensor_scalar(out=kn32[:, 64:128], in0=pb[:, 128:192], scalar1=rstd, scalar2=nb,
                                       op0=ALU.mult, op1=ALU.add)

            P.add("dve", kn_fn, reads=[t_bank[bk], t_rstd, t_nb], writes=[t_kn32] + t_dg)
            P.add("dve", lambda e: e.tensor_tensor(out=kn32, in0=kn32, in1=knp[:, 0, :], op=ALU.mult),
                  reads=[t_kn32, t_knp], writes=[t_kn32])
            P.add("dve", lambda e: e.tensor_tensor(out=kdup, in0=kn32, in1=knp[:, 1, :], op=ALU.add),
                  reads=[t_kn32, t_knp], writes=[t_kdup])
            bk2 = nbank()
            pb2 = bank_ap(bk2).bitcast(BF16)
            P.add("pe", lambda e, pb2=pb2: e.transpose(out=pb2[:, 0:128], in_=kdup, identity=ident),
                  reads=[t_kdup, t_const], writes=[t_bank[bk2]])
            P.add("act", lambda e, pb2=pb2, jb=jb: e.activation(out=kiT[:, jb * 128:(jb + 1) * 128], in_=pb2[:, 0:128],
                                                                  func=AF.Copy), reads=[t_bank[bk2]], writes=[t_kiT[c]])

        fmi = 0
        for ti in range(5):
            wt, wtok, d = next_tile("fm")
            ncs = len(d[1])
            wt3 = wt[:, 0:16 * 128 * ncs].rearrange("p (k n) -> p k n", n=128 * ncs)
            for j in range(ncs):
                bk = nbank()
                proj_fm(wt3, wtok, j * 128, xT, t_xT, bk)
                if fmi == 0:
                    P.add("act", lambda e, bk=bk, t0=t0: e.activation(out=kT[:, t0:t0 + 512], in_=bank_ap(bk), func=AF.Copy),
                          reads=[t_bank[bk]], writes=[t_kT[c]])
                else:
                    qc = fmi - 1
                    P.add("act", lambda e, bk=bk, qc=qc: e.activation(out=qiT[:, qc, :], in_=bank_ap(bk), func=AF.Copy),
                          reads=[t_bank[bk]], writes=[t_qiT[qc]])
                fmi += 1

        acc = h1tm[:, 0, :]
        phx["on"] = "idx"
        Rb = [h1tm[:, 1, :].bitcast(BF16)[:, k_ * 1024:(k_ + 1) * 1024] for k_ in range(4)]
        dg = [kn32[:, k_ * 64:(k_ + 1) * 64].bitcast(BF16) for k_ in range(2)]

        def idx_gen(c=c, t0=t0, NKB=NKB, LC=LC):
            for b in range(4):
                jb = 4 * c + b
                L = (jb + 1) * 128
                halves = [(0, min(L, 1024))] + ([(1024, L)] if L > 1024 else [])
                for h in range(16):
                    qc, half = h // 2, h % 2
                    dk = h % 2
                    P.add("pool", lambda e, dk=dk, b=b, h=h: e.affine_select(
                        out=dg[dk], in_=wis[:, b, h:h + 1].to_broadcast([128, 128]), pattern=[[-1, 128]],
                        compare_op=ALU.is_equal, fill=0.0, base=0, channel_multiplier=1),
                          reads=[t_wis], writes=[t_dg[dk], t_kn32])
                    for hv, (c0h, c1h) in enumerate(halves):
                        W = c1h - c0h
                        pbase = hv * 1024
                        btoks = [t_bank[hv * 2], t_bank[hv * 2 + 1]][:(W + 511) // 512]
                        rk = hv * 2 + (h % 2)
                        relu = Rb[rk][:, 0:W]
                        t_rl = t_R[rk]

                        def fn(e, qc=qc, half=half, b=b, c0h=c0h, W=W, pbase=pbase):
                            ins = None
                            for p0 in range(0, W, 512):
                                w_ = min(512, W - p0)
                                ins = e.matmul(ps[:, pbase + p0:pbase + p0 + w_],
                                               lhsT=qiT[64 * half:64 * half + 64, qc, b * 128:(b + 1) * 128],
                                               rhs=kiT[64 * half:64 * half + 64, c0h + p0:c0h + p0 + w_],
                                               start=True, stop=True)
                            return ins

                        P.add("pe", fn, reads=[t_qiT[qc]] + t_kiT[:c + 1], writes=btoks)
                        P.add("act", lambda e, relu=relu, pbase=pbase, W=W: e.activation(
                            out=relu, in_=ps[:, pbase:pbase + W], func=AF.Relu), reads=btoks, writes=[t_rl])
                        atoks = [t_bank[4 + (c0h + p0) // 512] for p0 in range(0, W, 512)]

                        def hs_fn(e, relu=relu, dk=dk, c0h=c0h, W=W, h=h):
                            ins = None
                            for p0 in range(0, W, 512):
                                w_ = min(512, W - p0)
                                ins = e.matmul(ps[:, 2048 + c0h + p0:2048 + c0h + p0 + w_], lhsT=dg[dk],
                                               rhs=relu[:, p0:p0 + w_], start=(h == 0), stop=(h == 15))
                            return ins

                        P.add("pe", hs_fn, reads=[t_rl, t_dg[dk]], writes=atoks)
                        yield
                P.add("act", lambda e, L=L: e.activation(out=acc[:, 0:L], in_=ps[:, 2048:2048 + L], func=AF.Copy),
                      reads=[t_bank[4 + k_] for k_ in range((L + 511) // 512)], writes=[t_h1[0]])
                P.add("dve", lambda e, L=L: e.tensor_tensor(out=acc[:, L - 128:L], in0=acc[:, L - 128:L], in1=trimask, op=ALU.add),
                      reads=[t_h1[0], t_const], writes=[t_h1[0]])
                if dbg and c == nchunks - 1 and b == 3:
                    out_ops.append(P.add("sp", lambda e: e.dma_start(out=dbg_d["d_acc"], in_=acc), reads=[t_h1[0]], dma=True))
                if jb >= 2:
                    P.add("dve", lambda e, L=L: e.tensor_reduce(out=thr, in_=acc[:, 0:L - 128], axis=mybir.AxisListType.X,
                                                                op=ALU.min), reads=[t_h1[0]], writes=[t_thr])
                    P.add("dve", lambda e, L=L: e.tensor_reduce(out=bh0, in_=acc[:, 0:L], axis=mybir.AxisListType.X,
                                                                op=ALU.max), reads=[t_h1[0]], writes=[t_bh0])
                    yield
                    P.add("dve", lambda e: e.scalar_tensor_tensor(out=bh0, in0=bh0, scalar=1.0, in1=thr, op0=ALU.mult,
                                                                  op1=ALU.subtract), reads=[t_bh0, t_thr], writes=[t_bh0])
                    P.add("dve", lambda e: e.tensor_scalar(out=bh0, in0=bh0, scalar1=1.0009765625, scalar2=1e-30,
                                                           op0=ALU.mult, op1=ALU.add), reads=[t_bh0], writes=[t_bh0])
                    for it in range(NBIS):
                        sc_ = 2.0 ** -(it + 1)
                        P.add("dve", lambda e, sc_=sc_: e.scalar_tensor_tensor(out=bmid, in0=bh0, scalar=sc_, in1=thr,
                                                                               op0=ALU.mult, op1=ALU.add),
                              reads=[t_bh0, t_thr], writes=[t_bmid])
                        P.add("dve", lambda e, L=L: e.tensor_scalar(out=maskrow[:, 0:L], in0=acc[:, 0:L], scalar1=bmid,
                                                                    scalar2=None, op0=ALU.is_ge, op1=ALU.add, accum_out=bcnt),
                              reads=[t_h1[0], t_bmid], writes=[t_maskrow, t_bcnt])
                        yield
                        P.add("dve", lambda e: e.tensor_scalar(out=bch, in0=bcnt, scalar1=255.5, scalar2=bh0,
                                                               op0=ALU.is_ge, op1=ALU.mult),
                              reads=[t_bcnt, t_bh0], writes=[t_bch])
                        P.add("dve", lambda e, sc_=sc_: e.scalar_tensor_tensor(out=thr, in0=bch, scalar=sc_, in1=thr,
                                                                               op0=ALU.mult, op1=ALU.add),
                              reads=[t_bch, t_thr], writes=[t_thr])
                else:
                    P.add("dve", lambda e: e.memset(thr, -1.0e29), writes=[t_thr])
                P.add("dve", lambda e, L=L: e.tensor_scalar(out=maskrow[:, 0:L], in0=acc[:, 0:L], scalar1=thr, scalar2=MASKNEG,
                                                            op0=ALU.is_lt, op1=ALU.mult), reads=[t_h1[0], t_thr],
                      writes=[t_maskrow])
                if L < LC:
                    P.add("dve", lambda e, L=L, LC=LC: e.memset(maskrow[:, L:LC], MASKNEG), writes=[t_maskrow])
                yield
                for i0 in range(0, NKB, 8):
                    n8 = min(8, NKB - i0)
                    bk = nbank()
                    pb = bank_ap(bk).bitcast(BF16)

                    def fn(e, i0=i0, n8=n8, pb=pb):
                        ins = None
                        for j in range(n8):
                            ins = e.transpose(out=pb[:, j * 128:(j + 1) * 128],
                                              in_=maskrow[:, (i0 + j) * 128:(i0 + j + 1) * 128], identity=ident)
                        return ins

                    P.add("pe", fn, reads=[t_maskrow, t_const], writes=[t_bank[bk]])
                    P.add("act", lambda e, i0=i0, n8=n8, pb=pb, b=b: e.activation(
                        out=big[:, 32 + i0:32 + i0 + n8, b * 128:(b + 1) * 128],
                        in_=pb[:, 0:n8 * 128].rearrange("p (j t) -> p j t", t=128), func=AF.Copy),
                          reads=[t_bank[bk]], writes=t_big[32 + i0:32 + i0 + n8])
                    yield

        def lru_gen(n, c=c):
            for _once in range(1):
                wt, wtok, d = next_tile("lru")
                wt3 = wt[:, 0:4096].rearrange("p (k n) -> p k n", n=256)
                gb_ = gbuf[n % 2]
                gtok = t_gbuf[n % 2]
                P.add("pool", lambda e, gb_=gb_, wt=wt: e.tensor_copy(out=gb_, in_=wt[:, 4096:4352]), reads=[wtok], writes=[gtok])
                g4 = gb_.rearrange("p (a k) -> p a k", a=2)
                bx_, bg_, br, bi = [4 * (n % 2) + k_ for k_ in range(4)]
                proj_fm(wt3, wtok, 0, xT, t_xT, bx_)
                proj_fm(wt3, wtok, 128, xT, t_xT, bg_)
                if n % 2 == 0:
                    sl = [tmp[:, i, :] for i in range(6)]
                    tk = [t_tmp[i] for i in range(6)]
                else:
                    sl = [h1tm[:, 2 + i // 3, (i % 3) * 682:(i % 3) * 682 + 516] for i in range(6)]
                    tk = [t_B[i] for i in range(6)]
                xcb = xcbs[n % 2]
                t_xcb = t_xcbs[n % 2]
                xs, xc, ra, igu, hs, mu = sl
                t_xs, t_xc_, t_ra, t_igu, t_hs, t_mu = tk
                gg, t_gg = xs, t_xs
                P.add("dve", lambda e, n=n, xs=xs: e.tensor_copy(out=xs[:, 0:3], in_=lruc[:, n, :]), reads=[t_lruc],
                      writes=[t_xs])
                P.add("act", lambda e, xs=xs, bx_=bx_: e.activation(out=xs[:, 3:515], in_=bank_ap(bx_), func=AF.Copy),
                      reads=[t_bank[bx_]], writes=[t_xs])
                yield
                P.add("dve", lambda e, n=n, xs=xs: e.tensor_copy(out=lruc[:, n, :], in_=xs[:, 512:515]), reads=[t_xs],
                      writes=[t_lruc])
                P.add("dve", lambda e, n=n, xs=xs, xc=xc: e.tensor_scalar(
                    out=xc[:, 0:512], in0=xs[:, 3:515], scalar1=pvec[:, PV_LCW + n * 4 + 3:PV_LCW + n * 4 + 4],
                    scalar2=pvec[:, PV_LCB + n:PV_LCB + n + 1], op0=ALU.mult, op1=ALU.add),
                      reads=[t_xs, t_pvec], writes=[t_xc_])
                for j in range(3):
                    P.add("dve", lambda e, n=n, xs=xs, xc=xc, j=j: e.scalar_tensor_tensor(
                        out=xc[:, 0:512], in0=xs[:, j:j + 512], scalar=pvec[:, PV_LCW + n * 4 + j:PV_LCW + n * 4 + j + 1],
                        in1=xc[:, 0:512], op0=ALU.mult, op1=ALU.add), reads=[t_xs, t_xc_, t_pvec], writes=[t_xc_])
                yield
                P.add("act", lambda e, xc=xc, xcb=xcb: e.activation(out=xcb, in_=xc[:, 0:512], func=AF.Copy), reads=[t_xc_],
                      writes=[t_xcb])
                P.add("pe", lambda e, br=br, g4=g4, xcb=xcb: e.matmul(bank_ap(br), lhsT=g4[:, 0, :], rhs=xcb, start=True, stop=True),
                      reads=[gtok, t_xcb], writes=[t_bank[br]])
                P.add("pe", lambda e, bi=bi, g4=g4, xcb=xcb: e.matmul(bank_ap(bi), lhsT=g4[:, 1, :], rhs=xcb, start=True, stop=True),
                      reads=[gtok, t_xcb], writes=[t_bank[bi]])
                P.add("act", lambda e, n=n, br=br, ra=ra: e.activation(out=ra[:, 0:512], in_=bank_ap(br), func=AF.Sigmoid,
                                                                        bias=pvec[:, PV_BA + n:PV_BA + n + 1]),
                      reads=[t_bank[br], t_pvec], writes=[t_ra])
                P.add("act", lambda e, n=n, bi=bi, igu=igu: e.activation(out=igu[:, 0:512], in_=bank_ap(bi), func=AF.Sigmoid,
                                                                          bias=pvec[:, PV_BX + n:PV_BX + n + 1]),
                      reads=[t_bank[bi], t_pvec], writes=[t_igu])
                yield
                P.add("act", lambda e, n=n, ra=ra, mu=mu: e.activation(out=mu[:, 0:512], in_=ra[:, 0:512], func=AF.Exp,
                                                                        scale=cneg2[:, n:n + 1]),
                      reads=[t_ra, t_cneg], writes=[t_mu])
                P.add("act", lambda e, n=n, ra=ra: e.activation(out=ra[:, 0:512], in_=ra[:, 0:512], func=AF.Exp,
                                                                 scale=cneg[:, n:n + 1]),
                      reads=[t_ra, t_cneg], writes=[t_ra])
                yield
                P.add("dve", lambda e, mu=mu: e.tensor_scalar(out=mu[:, 0:512], in0=mu[:, 0:512], scalar1=-1.0, scalar2=1.0,
                                                              op0=ALU.mult, op1=ALU.add), reads=[t_mu], writes=[t_mu])
                P.add("act", lambda e, mu=mu: e.activation(out=mu[:, 0:512], in_=mu[:, 0:512], func=AF.Sqrt),
                      reads=[t_mu], writes=[t_mu])
                yield
                P.add("act", lambda e, bg_=bg_, gg=gg: e.activation(out=gg[:, 0:512], in_=bank_ap(bg_), func=AF.Gelu_apprx_tanh),
                      reads=[t_bank[bg_]], writes=[t_gg])
                P.add("dve", lambda e, igu=igu, xc=xc: e.tensor_tensor(out=igu[:, 0:512], in0=igu[:, 0:512], in1=xc[:, 0:512],
                                                                       op=ALU.mult),
                      reads=[t_igu, t_xc_], writes=[t_igu])
                yield
                if c == 0:
                    P.add("dve", lambda e, mu=mu: e.memset(mu[:, 0:1], 1.0), writes=[t_mu])
                P.add("dve", lambda e, mu=mu, igu=igu: e.tensor_tensor(out=igu[:, 0:512], in0=igu[:, 0:512], in1=mu[:, 0:512],
                                                                       op=ALU.mult),
                      reads=[t_igu, t_mu], writes=[t_igu])
                P.add("dve", lambda e, n=n, ra=ra, igu=igu, hs=hs: e.tensor_tensor_scan(
                    out=hs[:, 0:512], data0=ra[:, 0:512], data1=igu[:, 0:512], initial=hprev[:, n:n + 1],
                    op0=ALU.mult, op1=ALU.add), reads=[t_ra, t_igu, t_hprev], writes=[t_hs])
                yield
                P.add("dve", lambda e, n=n, hs=hs: e.tensor_copy(out=hprev[:, n:n + 1], in_=hs[:, 511:512]),
                      reads=[t_hs], writes=[t_hprev])
                P.add("dve", lambda e, n=n, gg=gg, hs=hs: e.tensor_tensor(out=big[:, n, :], in0=gg[:, 0:512], in1=hs[:, 0:512],
                                                                          op=ALU.mult),
                      reads=[t_gg, t_hs], writes=[t_big[n]])
                yield

        P.add("dve", lambda e: e.memset(bmid, 0.0), reads=[], writes=t_h1[1:4] + t_B + t_relu + [t_bmid])
        for _ in idx_gen():
            pass
        for pr_ in range(8):
            gens = [lru_gen(2 * pr_), lru_gen(2 * pr_ + 1)]
            while gens:
                for g_ in list(gens):
                    try:
                        next(g_)
                    except StopIteration:
                        gens.remove(g_)
        phx["on"] = False
        if dbg and c == nchunks - 1:
            out_ops.append(P.add("sp", lambda e: e.dma_start(out=dbg_d["d_ylru"], in_=big[:, 0:16, :]),
                                 reads=t_big[0:16], dma=True))
            out_ops.append(P.add("sp", lambda e: e.dma_start(out=dbg_d["d_tmp"], in_=tmp), reads=t_tmp, dma=True))
            out_ops.append(P.add("sp", lambda e: e.dma_start(out=dbg_d["d_cneg"], in_=cneg), reads=[t_cneg], dma=True))

        for hp in range(8):
            wt, wtok, d = next_tile("fm")
            wt3 = wt[:, 0:4096].rearrange("p (k n) -> p k n", n=256)
            for j in range(2):
                h = 2 * hp + j
                lg_rr = [h % 4]

                def nbank4():
                    lg_rr[0] = (lg_rr[0] + 1) % 4
                    return lg_rr[0]

                bq = nbank4()
                proj_fm(wt3, wtok, j * 128, xT, t_xT, bq)
                P.add("act", lambda e, bq=bq, j=j: e.activation(out=qTh[j], in_=bank_ap(bq), func=AF.Copy),
                      reads=[t_bank[bq]], writes=[t_qTh[j]])
                bo, bd = (4, 5) if j == 0 else (6, 7)
                bls = {}

                def emit_front(i, j=j, h=h):
                    bl = nbank4()
                    bls[i] = bl
                    near = [(b, 4 * c + b - i) for b in range(4) if (4 * c + b - i) in (0, 1)]

                    def fn(e, bl=bl, i=i, j=j, h=h, near=near):
                        e.matmul(bank_ap(bl), lhsT=kT[:, i * 128:(i + 1) * 128], rhs=qTh[j], start=True, stop=False)
                        ins = e.matmul(bank_ap(bl), lhsT=ident, rhs=big[:, 32 + i, :], start=False, stop=(len(near) == 0))
                        for k_, (b, dd) in enumerate(near):
                            ins = e.matmul(bank_ap(bl)[:, b * 128:(b + 1) * 128], lhsT=ident, rhs=DBf[:, h, dd, :],
                                           start=False, stop=(k_ == len(near) - 1))
                        return ins

                    P.add("pe", fn, reads=[t_kT[i // 4], t_qTh[j], t_big[32 + i], t_const, t_DB], writes=[t_bank[bl]])

                emit_front(0)
                if NKB > 1:
                    emit_front(1)
                for i in range(NKB):
                    if i + 2 < NKB:
                        emit_front(i + 2)
                    bl = bls[i]
                    es = i % 2
                    P.add("act", lambda e, bl=bl, es=es, h=h: e.activation(out=Eb[es], in_=bank_ap(bl), func=AF.Exp,
                                                                            scale=ATT_SCALE, bias=rb31[:, h:h + 1]),
                          reads=[t_bank[bl], t_const], writes=[t_Eb[es]])

                    def pv(e, i=i, es=es, bo=bo, bd=bd, NKB=NKB):
                        e.matmul(bank_ap(bo), lhsT=Vb[:, i, :], rhs=Eb[es], start=(i == 0), stop=(i == NKB - 1))
                        return e.matmul(bank_ap(bd), lhsT=ones, rhs=Eb[es], start=(i == 0), stop=(i == NKB - 1))

                    P.add("pe", pv, reads=[t_V[i // 4], t_Eb[es], t_const], writes=[t_bank[bo], t_bank[bd]])
                P.add("dve", lambda e, bd=bd: e.reciprocal(out=rc, in_=bank_ap(bd)), reads=[t_bank[bd]], writes=[t_rc])
                P.add("dve", lambda e, bo=bo, h=h: e.tensor_tensor(out=big[:, 16 + h, :], in0=bank_ap(bo), in1=rc, op=ALU.mult),
                      reads=[t_bank[bo], t_rc], writes=[t_big[16 + h]])
        if dbg and c == nchunks - 1:
            out_ops.append(P.add("sp", lambda e: e.dma_start(out=dbg_d["d_yatt"], in_=big[:, 16:32, :]),
                                 reads=t_big[16:32], dma=True))

        for n in range(16):
            so = (n % 2) * 4
            s1, s2, m1, m2 = [tmp[:, so + i, :] for i in range(4)]
            ts1, ts2, tm1, tm2 = [t_tmp[so + i] for i in range(4)]
            wt, wtok, d = next_tile("fm")
            wt3 = wt[:, 0:4096].rearrange("p (k n) -> p k n", n=256)
            bA, bG1 = nbank(), nbank()
            proj_fm(wt3, wtok, 0, big[:, 0:16, :], t_big[0:16], bA)
            proj_fm(wt3, wtok, 128, xT, t_xT, bG1)
            wt, wtok, d = next_tile("fm")
            wt3 = wt[:, 0:4096].rearrange("p (k n) -> p k n", n=256)
            bB, bG2 = nbank(), nbank()
            proj_fm(wt3, wtok, 0, big[:, 16:32, :], t_big[16:32], bB)
            proj_fm(wt3, wtok, 128, xT, t_xT, bG2)
            P.add("act", lambda e, bG1=bG1, s1=s1: e.activation(out=s1[:, 0:512], in_=bank_ap(bG1), func=AF.Sigmoid),
                  reads=[t_bank[bG1]], writes=[ts1])
            P.add("act", lambda e, bG2=bG2, s2=s2: e.activation(out=s2[:, 0:512], in_=bank_ap(bG2), func=AF.Sigmoid),
                  reads=[t_bank[bG2]], writes=[ts2])
            P.add("dve", lambda e, bA=bA, s1=s1, m1=m1: e.tensor_tensor(out=m1[:, 0:512], in0=s1[:, 0:512], in1=bank_ap(bA),
                                                                        op=ALU.mult),
                  reads=[ts1, t_bank[bA]], writes=[tm1])
            P.add("dve", lambda e, bB=bB, s2=s2, m2=m2: e.tensor_tensor(out=m2[:, 0:512], in0=s2[:, 0:512], in1=bank_ap(bB),
                                                                        op=ALU.mult),
                  reads=[ts2, t_bank[bB]], writes=[tm2])
            P.add("dve", lambda e, n=n, m1=m1, m2=m2: e.tensor_tensor(out=big[:, 32 + n, :], in0=m1[:, 0:512],
                                                                      in1=m2[:, 0:512], op=ALU.add),
                  reads=[tm1, tm2], writes=[t_big[32 + n]])
        if dbg and c == nchunks - 1:
            out_ops.append(P.add("sp", lambda e: e.dma_start(out=dbg_d["d_merged"], in_=big[:, 32:48, :]),
                                 reads=t_big[32:48], dma=True))

        for b in range(4):
            P.add("sp", lambda e, b=b, t0=t0: e.dma_start(out=h1tm[:, b, :], in_=x_d[t0 + b * 128:t0 + (b + 1) * 128, :]),
                  writes=[t_h1[b]] + t_B + t_relu, dma=True)
        P.add("sp", lambda e: e.dma_start(out=lnp[:, 0, :], in_=lnp_d[0]), writes=[t_lnp[0]], dma=True)
        P.add("sp", lambda e: e.dma_start(out=lnp[:, 1, :], in_=lnp_d[1]), writes=[t_lnp[1]], dma=True)
        for nc4 in range(4):
            banks = [nbank() for _ in range(4)]
            for kh in range(2):
                wt, wtok, d = next_tile("kn")
                wt3 = wt[:, 0:4096].rearrange("p (k n) -> p k n", n=512)
                for b in range(4):
                    def fn(e, b=b, kh=kh, wt3=wt3, bk=banks[b]):
                        ins = None
                        for kc in range(8):
                            ins = e.matmul(bank_ap(bk), lhsT=big[:, 32 + kh * 8 + kc, b * 128:(b + 1) * 128],
                                           rhs=wt3[:, kc, :], start=(kh == 0 and kc == 0), stop=(kh == 1 and kc == 7))
                        return ins

                    P.add("pe", fn, reads=[wtok] + t_big[32 + kh * 8:32 + kh * 8 + 8], writes=[t_bank[banks[b]]])
            for b in range(4):
                zc = h1tm[:, b, nc4 * 512:(nc4 + 1) * 512]
                P.add("dve", lambda e, zc=zc, bk=banks[b]: e.scalar_tensor_tensor(
                    out=zc, in0=zc, scalar=ALPHA, in1=bank_ap(bk), op0=ALU.mult, op1=ALU.add),
                      reads=[t_h1[b], t_bank[banks[b]]], writes=[t_h1[b]])
        for b in range(4):
            layer_norm_inplace(h1tm[:, b, :], t_h1[b], 0)
            transpose_rows(h1tm[:, b, :], t_h1[b], xT, t_xT[b], b)
        if dbg and c == nchunks - 1:
            out_ops.append(P.add("sp", lambda e: e.dma_start(out=dbg_d["d_h1"], in_=h1tm[:, 3, :]), reads=[t_h1[3]], dma=True))

        for n in range(48):
            wt, wtok, d = next_tile("fm")
            wt3 = wt[:, 0:4096].rearrange("p (k n) -> p k n", n=256)
            bg_, bv_ = nbank(), nbank()
            proj_fm(wt3, wtok, 0, xT, t_xT, bg_)
            proj_fm(wt3, wtok, 128, xT, t_xT, bv_)
            so = (n % 2) * 5
            xg, xv, cg, cv, gl = [tmp[:, so + i, :] for i in range(5)]
            tg, tv, tcg, tcv, tgl = [t_tmp[so + i] for i in range(5)]
            for (xx, tx, bk_, ch, co, tco) in ((xg, tg, bg_, n, cg, tcg), (xv, tv, bv_, 48 + n, cv, tcv)):
                P.add("dve", lambda e, xx=xx, ch=ch: e.tensor_copy(out=xx[:, 0:2], in_=fcar[:, ch, :]), reads=[t_fcar],
                      writes=[tx])
                P.add("act", lambda e, xx=xx, bk_=bk_: e.activation(out=xx[:, 2:514], in_=bank_ap(bk_), func=AF.Copy),
                      reads=[t_bank[bk_]], writes=[tx])
                P.add("dve", lambda e, xx=xx, ch=ch: e.tensor_copy(out=fcar[:, ch, :], in_=xx[:, 512:514]), reads=[tx],
                      writes=[t_fcar])

                P.add("dve", lambda e, xx=xx, ch=ch, co=co: e.tensor_scalar(
                    out=co[:, 0:512], in0=xx[:, 2:514], scalar1=pvec[:, PV_FCW + ch * 3 + 2:PV_FCW + ch * 3 + 3],
                    scalar2=pvec[:, PV_FCB + ch:PV_FCB + ch + 1], op0=ALU.mult, op1=ALU.add),
                      reads=[tx, t_pvec], writes=[tco])
                for j in range(2):
                    P.add("dve", lambda e, xx=xx, ch=ch, co=co, j=j: e.scalar_tensor_tensor(
                        out=co[:, 0:512], in0=xx[:, j:j + 512], scalar=pvec[:, PV_FCW + ch * 3 + j:PV_FCW + ch * 3 + j + 1],
                        in1=co[:, 0:512], op0=ALU.mult, op1=ALU.add), reads=[tx, tco, t_pvec], writes=[tco])
            P.add("act", lambda e, cg=cg, gl=gl: e.activation(out=gl[:, 0:512], in_=cg[:, 0:512], func=AF.Gelu_apprx_tanh),
                  reads=[tcg], writes=[tgl])
            P.add("dve", lambda e, n=n, gl=gl, cv=cv: e.tensor_tensor(out=big[:, n, :], in0=gl[:, 0:512], in1=cv[:, 0:512],
                                                                      op=ALU.mult),
                  reads=[tgl, tcv], writes=[t_big[n]])
        if dbg and c == nchunks - 1:
            out_ops.append(P.add("sp", lambda e: e.dma_start(out=dbg_d["d_act"], in_=big), reads=t_big, dma=True))

        P.add("sp", lambda e: e.dma_start(out=lnp[:, 0, :], in_=lnp_d[2]), writes=[t_lnp[0]], dma=True)
        P.add("sp", lambda e: e.dma_start(out=lnp[:, 1, :], in_=lnp_d[3]), writes=[t_lnp[1]], dma=True)
        for nc4 in range(4):
            banks = [nbank() for _ in range(4)]
            for kg in range(6):
                wt, wtok, d = next_tile("kn")
                wt3 = wt[:, 0:4096].rearrange("p (k n) -> p k n", n=512)
                for b in range(4):
                    def fn(e, b=b, kg=kg, wt3=wt3, bk=banks[b]):
                        ins = None
                        for kc in range(8):
                            ins = e.matmul(bank_ap(bk), lhsT=big[:, kg * 8 + kc, b * 128:(b + 1) * 128],
                                           rhs=wt3[:, kc, :], start=(kg == 0 and kc == 0), stop=(kg == 5 and kc == 7))
                        return ins

                    P.add("pe", fn, reads=[wtok] + t_big[kg * 8:kg * 8 + 8], writes=[t_bank[banks[b]]])
            for b in range(4):
                zc = h1tm[:, b, nc4 * 512:(nc4 + 1) * 512]
                P.add("dve", lambda e, zc=zc, bk=banks[b]: e.scalar_tensor_tensor(
                    out=zc, in0=zc, scalar=ALPHA, in1=bank_ap(bk), op0=ALU.mult, op1=ALU.add),
                      reads=[t_h1[b], t_bank[banks[b]]], writes=[t_h1[b]])
        for b in range(4):
            layer_norm_inplace(h1tm[:, b, :], t_h1[b], 1)
            out_ops.append(P.add("sp", lambda e, b=b, t0=t0: e.dma_start(out=out_d[t0 + b * 128:t0 + (b + 1) * 128, :],
                                                                   in_=h1tm[:, b, :]), reads=[t_h1[b]], dma=True))

    print("sbuf bytes remaining", nc.sbuf_bytes_remaining)
    P.emit(out_dma_ops=out_ops)
    return nc


def host_pack(inputs):
    f = lambda k: np.ascontiguousarray(np.asarray(inputs[k], dtype=np.float32))
    mats = {"w_in": f("w_in")[0], "w_proj_lru": f("w_proj_lru")[0], "w_proj_attn": f("w_proj_attn")[0],
            "w_out": f("w_out")[0], "ffn_w_up": f("ffn_w_up")[0], "ffn_w_down": f("ffn_w_down")[0],
            "lru_gate_a_w": f("lru_gate_a_w")[0], "lru_gate_x_w": f("lru_gate_x_w")[0]}
    wstream = pack_wstream(mats)
    pvec = np.zeros((128, PV_N), np.float32)
    chan = lambda v: v.reshape(-1, 128).T
    lcw = f("lru_conv_w")[0]
    pvec[:, PV_LCW:PV_LCW + 64] = np.stack([chan(lcw[j]) for j in range(4)], axis=2).reshape(128, 64)
    pvec[:, PV_LCB:PV_LCB + 16] = chan(f("lru_conv_b")[0])
    pvec[:, PV_BA:PV_BA + 16] = chan(f("lru_gate_a_b")[0])
    pvec[:, PV_BX:PV_BX + 16] = chan(f("lru_gate_x_b")[0])
    pvec[:, PV_LAM:PV_LAM + 16] = chan(f("lru_lambda")[0])
    fcw = f("ffn_conv_w")[0]
    pvec[:, PV_FCW:PV_FCW + 288] = np.stack([chan(fcw[j]) for j in range(3)], axis=2).reshape(128, 288)
    pvec[:, PV_FCB:PV_FCB + 96] = chan(f("ffn_conv_b")[0])
    lnp = np.stack([np.broadcast_to(f(k)[0][None, :], (128, D)) for k in ("ln1_g", "ln1_b", "ln2_g", "ln2_b")], axis=0)
    lnp = np.ascontiguousarray(lnp)
    kg, kb = f("idx_knorm_g")[0], f("idx_knorm_b")[0]
    knp = np.zeros((128, 2, 128), np.float32)
    knp[:, 0, :] = np.concatenate([kg, kg])[None, :]
    knp[:, 1, :] = np.concatenate([kb, kb])[None, :]
    rel_bias = f("rel_bias")
    ss = np.arange(128)[:, None]
    tt = np.arange(128)[None, :]
    btab = np.zeros((128, 16, 2, 128), np.float32)
    for dd in range(2):
        rel = dd * 128 + tt - ss
        bkt = np.where(rel >= 0, t5_bucket_np(np.maximum(rel, 0)), 31)
        btab[:, :, dd, :] = rel_bias[bkt].transpose(0, 2, 1)
    rb31 = np.ascontiguousarray(np.broadcast_to(rel_bias[31][None, :], (128, 16)))
    return {"wstream": wstream, "pvec": pvec, "lnp": lnp, "knp": knp, "btab": btab, "rb31": rb31}


_CACHE = {}


def run(inputs, dbg=False, nchunks=NCHUNK, ncores=8, trace=False):
    key = (dbg, nchunks)
    shared = host_pack(inputs)
    x = np.asarray(inputs["x"], dtype=np.float32)
    nc = build_program(dbg=dbg, nchunks=nchunks)
    in_maps = []
    for b in range(ncores):
        m = dict(shared)
        m["x"] = np.ascontiguousarray(x[b])
        in_maps.append(m)
    res = run_bass_kernel_spmd(nc, in_maps, core_ids=list(range(ncores)), trace=trace)
    return res


def kernel(**inputs):
    res = run(inputs)
    out = np.stack([r["out"] for r in res.results], axis=0).astype(np.float32)
    return out
```

```python
import math
import os
import contextlib
import numpy as np
import concourse.bass as bass
import concourse.mybir as mybir
from concourse.bass_utils import run_bass_kernel_spmd

F32 = mybir.dt.float32
BF16 = mybir.dt.bfloat16
AF = mybir.ActivationFunctionType
ALU = mybir.AluOpType

D = 2048
S = 2048
T = 512
NCHUNK = S // T
DFF = 6144
NEG = -1.0e30
ALPHA = 2.0 ** 0.25
LN_EPS = 1e-5
ATT_SCALE = 128.0 ** -0.5
O_LRUX, O_LRUG, O_Q, O_K, O_V, O_QI, O_KI, O_WI, O_GL, O_GA = 0, 2048, 4096, 6144, 6272, 6400, 7424, 7488, 7504, 9552

ENGS = ["pe", "act", "dve", "pool", "sp"]
NDMA_SEMS = 12
NSLOT = 4
STRICT_SYNC = True
NBIS = 22
MASKNEG = -30000.0
TILE_E = 4352


class Tok:
    __slots__ = ("w", "r")

    def __init__(self):
        self.w = None
        self.r = []


class Op:
    __slots__ = ("eng", "fn", "deps", "is_dma", "dslot", "dval", "sig", "cnt", "prev_dma")


class Prog:
    def __init__(self, nc):
        self.nc = nc
        self.ops = {e: [] for e in ENGS}
        self.ndma = {e: 0 for e in ENGS}
        self.dma_hist = {e: [] for e in ENGS}

    def add(self, eng, fn, reads=(), writes=(), dma=False):
        op = Op()
        op.eng, op.fn, op.is_dma = eng, fn, dma
        op.deps, op.sig, op.cnt, op.prev_dma, op.dslot, op.dval = [], False, None, None, None, None
        seen = set()
        rawset = set(id(t.w) for t in reads if t.w is not None)

        def consider(d):
            if d is None or id(d) in seen:
                return
            seen.add(id(d))
            if (not d.is_dma) and (not dma) and d.eng == eng:
                if eng == "pe":
                    return
                if id(d) not in rawset and not STRICT_SYNC:
                    return
            op.deps.append(d)

        for t in reads:
            consider(t.w)
        for t in writes:
            consider(t.w)
            for r in t.r:
                consider(r)
        for t in reads:
            t.r.append(op)
        for t in writes:
            t.w = op
            t.r = []
        if dma:
            m = self.ndma[eng]
            self.ndma[eng] = m + 1
            op.dslot = m % NDMA_SEMS
            op.dval = 16 * (m // NDMA_SEMS + 1)
            if m >= NDMA_SEMS:
                op.prev_dma = self.dma_hist[eng][m - NDMA_SEMS]
            self.dma_hist[eng].append(op)
        self.ops[eng].append(op)
        return op

    def emit(self, out_dma_ops=()):
        nc = self.nc
        for e in ENGS:
            for op in self.ops[e]:
                for d in op.deps:
                    if not d.is_dma:
                        d.sig = True
        for e in ENGS:
            c = 0
            for op in self.ops[e]:
                if (not op.is_dma) and op.sig:
                    c += 1
                    op.cnt = c
        with contextlib.ExitStack() as st:
            csem = {e: st.enter_context(nc.semaphore("c_" + e)) for e in ENGS}
            dsem = {e: [st.enter_context(nc.semaphore("d_%s_%d" % (e, i))) for i in range(NDMA_SEMS)]
                    for e in ENGS if self.ndma[e] > 0}
            block = st.enter_context(nc.Block())
            prog = self

            def run(ename, eng):
                waited = {}

                def wait(key, sem, val):
                    if waited.get(key, 0) >= val:
                        return
                    waited[key] = val
                    eng.wait_ge(sem, val)

                for op in prog.ops[ename]:
                    for d in op.deps:
                        if d.is_dma:
                            wait(("d", d.eng, d.dslot), dsem[d.eng][d.dslot], d.dval)
                        else:
                            wait(("c", d.eng), csem[d.eng], d.cnt)
                    if op.is_dma and op.prev_dma is not None:
                        p = op.prev_dma
                        wait(("d", p.eng, p.dslot), dsem[p.eng][p.dslot], p.dval)
                    ins = op.fn(eng)
                    if op.is_dma:
                        ins.then_inc(dsem[ename][op.dslot], 16)
                    elif op.sig:
                        ins.then_inc(csem[ename], 1)
                if ename == "sp":
                    for d in out_dma_ops:
                        wait(("d", d.eng, d.dslot), dsem[d.eng][d.dslot], d.dval)

            @block.tensor
            def _(eng):
                run("pe", eng)

            @block.scalar
            def _(eng):
                run("act", eng)

            @block.vector
            def _(eng):
                run("dve", eng)

            @block.gpsimd
            def _(eng):
                run("pool", eng)

            @block.sync
            def _(eng):
                run("sp", eng)


def tile_plan():
    plan = [("tm",)]
    fm = [("w_in", O_K)] + [("w_in", O_QI + 128 * i) for i in range(8)]
    plan.append(("fm", [fm[0], fm[1]]))
    plan.append(("fm", [fm[2], fm[3]]))
    plan.append(("fm", [fm[4], fm[5]]))
    plan.append(("fm", [fm[6], fm[7]]))
    plan.append(("fm", [fm[8]]))
    for n in range(16):
        plan.append(("lru", [("w_in", O_LRUX + 128 * n), ("w_in", O_LRUG + 128 * n)], n))
    for i in range(8):
        plan.append(("fm", [("w_in", O_Q + 256 * i), ("w_in", O_Q + 256 * i + 128)]))
    for n in range(16):
        plan.append(("fm", [("w_proj_lru", 128 * n), ("w_in", O_GL + 128 * n)]))
        plan.append(("fm", [("w_proj_attn", 128 * n), ("w_in", O_GA + 128 * n)]))
    for nc4 in range(4):
        for kh in range(2):
            plan.append(("kn", "w_out", nc4, kh))
    for n in range(48):
        plan.append(("fm", [("ffn_w_up", 128 * n), ("ffn_w_up", DFF + 128 * n)]))
    for nc4 in range(4):
        for kg in range(6):
            plan.append(("kn", "ffn_w_down", nc4, kg))
    return plan


def tile_elems(desc):
    if desc[0] == "tm":
        return 16 * 208
    if desc[0] == "fm":
        return 16 * 128 * len(desc[1])
    if desc[0] == "lru":
        return 4096 + 256
    return 4096


def pack_wstream(mats):
    plan = tile_plan()
    out = np.zeros((len(plan), 128, TILE_E), np.float32)
    w_in = mats["w_in"]
    for i, d in enumerate(plan):
        if d[0] == "tm":
            cols = np.concatenate([np.arange(O_V, O_V + 128), np.arange(O_KI, O_KI + 64), np.arange(O_WI, O_WI + 16)])
            blk = w_in[:, cols].reshape(16, 128, 208).transpose(1, 0, 2)
            out[i, :, :16 * 208] = blk.reshape(128, -1)
        elif d[0] == "fm":
            parts = [mats[m][:, c0:c0 + 128].reshape(16, 128, 128).transpose(1, 0, 2) for (m, c0) in d[1]]
            blk = np.concatenate(parts, axis=2)
            out[i, :, :blk.shape[1] * blk.shape[2]] = blk.reshape(128, -1)
        elif d[0] == "lru":
            parts = [mats[m][:, c0:c0 + 128].reshape(16, 128, 128).transpose(1, 0, 2) for (m, c0) in d[1]]
            blk = np.concatenate(parts, axis=2)
            out[i, :, :4096] = blk.reshape(128, -1)
            n = d[2]
            out[i, :, 4096:4096 + 128] = mats["lru_gate_a_w"][n]
            out[i, :, 4096 + 128:4096 + 256] = mats["lru_gate_x_w"][n]
        else:
            _, m, nc4, kg = d
            blk = mats[m][kg * 1024:(kg + 1) * 1024, nc4 * 512:(nc4 + 1) * 512].reshape(8, 128, 512).transpose(1, 0, 2)
            out[i, :, :4096] = blk.reshape(128, -1)
    return out


PV_LCW, PV_LCB, PV_BA, PV_BX, PV_LAM, PV_FCW, PV_FCB, PV_N = 0, 64, 80, 96, 112, 128, 416, 512


def t5_bucket_np(rel):
    rel = np.asarray(rel)
    nf = np.maximum(rel, 1).astype(np.float32)
    large = 16 + (np.log(nf / np.float32(16)) / np.float32(math.log(128 / 16)) * np.float32(16)).astype(np.int32)
    large = np.minimum(large, 31)
    return np.where(rel < 16, np.maximum(rel, 0), large)


def build_program(dbg=False, nchunks=NCHUNK):
    nc = bass.Bass("TRN2", target_bir_lowering=False)
    plan = tile_plan()
    NT = len(plan)
    x_d = nc.dram_tensor("x", [S, D], F32, kind="ExternalInput").ap()
    ws_d = nc.dram_tensor("wstream", [NT, 128, TILE_E], F32, kind="ExternalInput").ap()
    pvec_d = nc.dram_tensor("pvec", [128, PV_N], F32, kind="ExternalInput").ap()
    lnp_d = nc.dram_tensor("lnp", [4, 128, D], F32, kind="ExternalInput").ap()
    knp_d = nc.dram_tensor("knp", [128, 2, 128], F32, kind="ExternalInput").ap()
    btab_d = nc.dram_tensor("btab", [128, 16, 2, 128], F32, kind="ExternalInput").ap()
    rb31_d = nc.dram_tensor("rb31", [128, 16], F32, kind="ExternalInput").ap()
    out_d = nc.dram_tensor("out", [S, D], F32, kind="ExternalOutput").ap()
    dbg_d = {}
    if dbg:
        for nm, shp, dt_ in [("d_ylru", [128, 16, 512], BF16), ("d_yatt", [128, 16, 512], BF16),
                             ("d_acc", [128, 2048], F32), ("d_maskT", [128, 16, 512], BF16),
                             ("d_h1", [128, 2048], F32), ("d_merged", [128, 16, 512], BF16),
                             ("d_kT", [128, 2048], BF16), ("d_kiT", [128, 2048], BF16), ("d_V", [128, 16, 128], BF16),
                             ("d_qiT", [128, 8, 512], BF16), ("d_xT", [128, 16, 512], BF16), ("d_wis", [128, 4, 16], F32),
                             ("d_act", [128, 48, 512], BF16), ("d_tmp", [128, 10, 516], F32), ("d_cneg", [128, 16], F32)]:
            dbg_d[nm] = nc.dram_tensor(nm, shp, dt_, kind="ExternalOutput").ap()

    P = Prog(nc)
    A = nc.alloc_sbuf_tensor

    def sb(name, shape, dt_):
        return A(name, shape, dt_).ap()

    xT = sb("xT", [128, 16, 512], BF16)
    big = sb("big", [128, 48, 512], BF16)
    h1tm = sb("h1tm", [128, 4, 2048], F32)
    lnp = sb("lnp_s", [128, 2, 2048], F32)
    qiT = sb("qiT", [128, 8, 512], BF16)
    stg = sb("stg", [128, 2048], BF16)
    maskrow = stg
    NTMP = 10
    tmp = sb("tmp", [128, NTMP, 516], F32)
    xcbs = [sb("xcb%d" % i, [128, 512], BF16) for i in range(2)]
    qTh = [tmp[:, j, 0:256].bitcast(BF16) for j in range(2)]
    Eb = [tmp[:, 2 + j, 0:256].bitcast(BF16) for j in range(2)]
    Pm = [tmp[:, 4 + j, 0:256].bitcast(BF16) for j in range(2)]
    rc = tmp[:, 6, 0:512]
    kT = sb("kT", [128, 2048], BF16)
    Vb = sb("Vb", [128, 16, 128], BF16)
    kiT = sb("kiT", [128, 2048], BF16)
    ident = sb("ident", [128, 128], BF16)
    ones = sb("ones", [128, 128], BF16)
    trimask = sb("trimask", [128, 128], F32)
    DBf = sb("DBf", [128, 16, 2, 128], BF16)
    rb31 = sb("rb31_s", [128, 16], F32)
    pvec = sb("pvec_s", [128, PV_N], F32)
    cneg = sb("cneg", [128, 16], F32)
    cneg2 = sb("cneg2", [128, 16], F32)
    hprev = sb("hprev", [128, 16], F32)
    lruc = sb("lruc", [128, 16, 3], F32)
    fcar = sb("fcar", [128, 96, 2], F32)
    wis = sb("wis", [128, 4, 16], F32)
    knp = sb("knp_s", [128, 2, 128], F32)
    kdup = sb("kdup", [128, 128], BF16)
    kn32 = sb("kn32", [128, 128], F32)
    st6 = sb("st6", [128, 4, 6], F32)
    mv = sb("mv", [128, 2], F32)
    rstd = sb("rstd", [128, 1], F32)
    nb = sb("nb", [128, 1], F32)
    m8 = sb("m8", [128, 8], F32)
    thr = sb("thr", [128, 1], F32)
    epsb = sb("epsb", [128, 1], F32)
    bh0 = sb("bh0", [128, 1], F32)
    bmid = sb("bmid", [128, 1], F32)
    bcnt = sb("bcnt", [128, 1], F32)
    bch = sb("bch", [128, 1], F32)
    t_bh0, t_bmid, t_bcnt, t_bch = Tok(), Tok(), Tok(), Tok()
    ring = [sb("ring%d" % i, [128, TILE_E], BF16) for i in range(NSLOT)]
    ps = nc.alloc_psum_tensor("ps", [128, 4096], F32).ap()

    t_xT = [Tok() for _ in range(4)]
    t_big = [Tok() for _ in range(48)]
    t_h1 = [Tok() for _ in range(4)]
    t_lnp = [Tok(), Tok()]
    t_qiT = [Tok() for _ in range(8)]
    t_stg = Tok()
    t_xcbs = [Tok(), Tok()]
    t_B = [Tok() for _ in range(9)]
    t_relu = [Tok() for _ in range(4)]
    t_R = t_relu
    t_dg = [Tok(), Tok()]
    t_maskrow = t_stg
    t_tmp = [Tok() for _ in range(NTMP)]
    t_qTh = [t_tmp[0], t_tmp[1]]
    t_Eb = [t_tmp[2], t_tmp[3]]
    t_Pm = [t_tmp[4], t_tmp[5]]
    t_rc = t_tmp[6]
    t_kT = [Tok() for _ in range(4)]
    t_V = [Tok() for _ in range(4)]
    t_kiT = [Tok() for _ in range(4)]
    t_const, t_DB, t_pvec, t_cneg, t_hprev, t_lruc, t_fcar, t_wis, t_knp = [Tok() for _ in range(9)]
    t_kdup, t_kn32, t_st, t_mv, t_rstd, t_nb, t_m8, t_thr = [Tok() for _ in range(8)]
    t_ring = [Tok() for _ in range(NSLOT)]
    t_bank = [Tok() for _ in range(8)]

    def bank_ap(b, n=512):
        return ps[:, b * 512:b * 512 + n]

    bank_rr = [0]

    phx = {"on": False}

    def nbank():
        b = bank_rr[0]
        if phx["on"] == "idx":
            b = b % 4
            bank_rr[0] = (b + 1) % 4
            return b
        bank_rr[0] = (b + 1) % 8
        return b

    grp_rr = [0]

    def ngroup():
        g = grp_rr[0]
        grp_rr[0] = 1 - g
        bank_rr[0] = 0 if g == 1 else 4
        return g

    total_tiles = NT * nchunks
    wstate = {"issued": 0, "next": 0}

    def issue_upto(g):
        while wstate["issued"] <= g and wstate["issued"] < total_tiles:
            gi = wstate["issued"]
            i = gi % NT
            slot = gi % NSLOT
            ne = tile_elems(plan[i])
            P.add("pool", lambda e, i=i, slot=slot, ne=ne: e.dma_start(out=ring[slot][:, 0:ne], in_=ws_d[i, :, 0:ne]),
                  writes=[t_ring[slot]], dma=True)
            wstate["issued"] += 1

    def next_tile(expect):
        g = wstate["next"]
        wstate["next"] += 1
        i = g % NT
        assert plan[i][0] == expect, (plan[i], expect)
        issue_upto(g + NSLOT - 1)
        slot = g % NSLOT
        return ring[slot], t_ring[slot], plan[i]

    P.add("sp", lambda e: e.dma_start(out=pvec, in_=pvec_d), writes=[t_pvec], dma=True)
    P.add("sp", lambda e: e.dma_start(out=knp, in_=knp_d), writes=[t_knp], dma=True)
    bstage = h1tm[:, 0:2, :].rearrange("p a f -> p (a f)").rearrange("p (h d t) -> p h d t", h=16, d=2)
    P.add("sp", lambda e: e.dma_start(out=bstage, in_=btab_d), writes=[t_h1[0], t_h1[1]], dma=True)
    P.add("sp", lambda e: e.dma_start(out=rb31, in_=rb31_d), writes=[t_const], dma=True)
    issue_upto(NSLOT - 2)

    zsrc = tmp[:, 9, 0:128]
    gbuf = [sb("gbuf%d" % i, [128, 256], BF16) for i in range(2)]
    t_gbuf = [Tok(), Tok()]
    t_z = t_tmp[9]
    P.add("pool", lambda e: e.memset(zsrc, 0.0), writes=[t_z])
    P.add("pool", lambda e: e.affine_select(out=ident, in_=zsrc, pattern=[[-1, 128]], compare_op=ALU.not_equal, fill=1.0,
                                            base=0, channel_multiplier=1), reads=[t_z], writes=[t_const])
    P.add("pool", lambda e: e.affine_select(out=trimask, in_=zsrc, pattern=[[-1, 128]], compare_op=ALU.is_ge, fill=NEG,
                                            base=0, channel_multiplier=1), reads=[t_z], writes=[t_const])

    def setup_consts(e):
        e.memset(ones, 1.0)
        e.memset(hprev, 0.0)
        e.memset(epsb, LN_EPS)
        e.memset(lruc, 0.0)
        return e.memset(fcar, 0.0)

    P.add("pool", setup_consts, writes=[t_const, t_hprev, t_lruc, t_fcar])
    P.add("act", lambda e: e.activation(out=cneg, in_=pvec[:, PV_LAM:PV_LAM + 16], func=AF.Exp, scale=-1.0),
          reads=[t_pvec], writes=[t_cneg])
    P.add("dve", lambda e: e.tensor_scalar(out=cneg, in0=cneg, scalar1=1.0, scalar2=None, op0=ALU.add),
          reads=[t_cneg], writes=[t_cneg])
    P.add("act", lambda e: e.activation(out=cneg, in_=cneg, func=AF.Ln), reads=[t_cneg], writes=[t_cneg])
    P.add("dve", lambda e: e.tensor_scalar(out=cneg2, in0=cneg, scalar1=-16.0, scalar2=None, op0=ALU.mult),
          reads=[t_cneg], writes=[t_cneg])
    P.add("dve", lambda e: e.tensor_scalar(out=cneg, in0=cneg, scalar1=-8.0, scalar2=None, op0=ALU.mult),
          reads=[t_cneg], writes=[t_cneg])

    def setup_db(e):
        ins = None
        for h in range(16):
            ins = e.tensor_scalar(out=DBf[:, h, :, :], in0=bstage[:, h, :, :], scalar1=rb31[:, h:h + 1],
                                  scalar2=1.0 / ATT_SCALE, op0=ALU.subtract, op1=ALU.mult)
        return ins

    P.add("dve", setup_db, reads=[t_h1[0], t_h1[1], t_const], writes=[t_DB])

    out_ops = []

    def transpose_rows(src_f32, src_tok, dstT, dst_tok, b):
        P.add("act", lambda e: e.activation(out=stg, in_=src_f32, func=AF.Copy), reads=[src_tok], writes=[t_stg])
        for half in range(2):
            bk = nbank()
            pb = bank_ap(bk).bitcast(BF16)

            def fn(e, half=half, pb=pb):
                ins = None
                for j in range(8):
                    kc = half * 8 + j
                    ins = e.transpose(out=pb[:, j * 128:(j + 1) * 128], in_=stg[:, kc * 128:(kc + 1) * 128],
                                      identity=ident)
                return ins

            P.add("pe", fn, reads=[t_stg, t_const], writes=[t_bank[bk]])
            P.add("dve", lambda e, half=half, pb=pb: e.tensor_copy(
                out=dstT[:, half * 8:(half + 1) * 8, b * 128:(b + 1) * 128],
                in_=pb.rearrange("p (j t) -> p j t", t=128)), reads=[t_bank[bk]], writes=[dst_tok])

    def proj_fm(wt3, wtok, col0, rhsT, rhs_toks, bk):
        def fn(e):
            ins = None
            for kc in range(16):
                ins = e.matmul(bank_ap(bk), lhsT=wt3[:, kc, col0:col0 + 128], rhs=rhsT[:, kc, :],
                               start=(kc == 0), stop=(kc == 15))
            return ins

        P.add("pe", fn, reads=[wtok] + list(rhs_toks), writes=[t_bank[bk]])

    def layer_norm_inplace(z, ztok, gi):
        def stats(e):
            ins = None
            for k in range(4):
                ins = e.bn_stats(out=st6[:, k, :], in_=z[:, k * 512:(k + 1) * 512])
            return ins

        P.add("dve", stats, reads=[ztok], writes=[t_st])
        P.add("dve", lambda e: e.bn_aggr(out=mv, in_=st6.rearrange("p a b -> p (a b)")), reads=[t_st], writes=[t_mv])
        P.add("act", lambda e: e.activation(out=rstd, in_=mv[:, 1:2], func=AF.Sqrt, bias=epsb, scale=1.0),
              reads=[t_mv, t_const], writes=[t_rstd])
        P.add("dve", lambda e: e.reciprocal(out=rstd, in_=rstd), reads=[t_rstd], writes=[t_rstd])
        P.add("dve", lambda e: e.scalar_tensor_tensor(out=nb, in0=mv[:, 0:1], scalar=-1.0, in1=rstd,
                                                      op0=ALU.mult, op1=ALU.mult), reads=[t_mv, t_rstd], writes=[t_nb])
        P.add("dve", lambda e: e.tensor_scalar(out=z, in0=z, scalar1=rstd, scalar2=nb, op0=ALU.mult, op1=ALU.add),
              reads=[ztok, t_rstd, t_nb], writes=[ztok])
        P.add("dve", lambda e: e.tensor_tensor(out=z, in0=z, in1=lnp[:, 0, :], op=ALU.mult),
              reads=[ztok, t_lnp[0]], writes=[ztok])
        P.add("dve", lambda e: e.tensor_tensor(out=z, in0=z, in1=lnp[:, 1, :], op=ALU.add),
              reads=[ztok, t_lnp[1]], writes=[ztok])

    for c in range(nchunks):
        t0 = c * T
        NKB = 4 * (c + 1)
        LC = NKB * 128
        for b in range(4):
            P.add("sp", lambda e, b=b, t0=t0: e.dma_start(out=h1tm[:, 3, :], in_=x_d[t0 + b * 128:t0 + (b + 1) * 128, :]),
                  writes=[t_h1[3]], dma=True)
            transpose_rows(h1tm[:, 3, :], t_h1[3], xT, t_xT[b], b)

        wt, wtok, _ = next_tile("tm")
        wt3 = wt[:, 0:16 * 208].rearrange("p (k n) -> p k n", n=208)
        for b in range(4):
            jb = 4 * c + b
            bk = nbank()

            def fn(e, b=b, bk=bk, wt3=wt3):
                ins = None
                for kc in range(16):
                    ins = e.matmul(bank_ap(bk, 208), lhsT=xT[:, kc, b * 128:(b + 1) * 128], rhs=wt3[:, kc, :],
                                   start=(kc == 0), stop=(kc == 15))
                return ins

            P.add("pe", fn, reads=[wtok, t_xT[b]], writes=[t_bank[bk]])
            pb = bank_ap(bk, 208)
            P.add("act", lambda e, pb=pb, jb=jb: e.activation(out=Vb[:, jb, :], in_=pb[:, 0:128], func=AF.Copy),
                  reads=[t_bank[bk]], writes=[t_V[c]])
            P.add("act", lambda e, pb=pb, b=b: e.activation(out=wis[:, b, :], in_=pb[:, 192:208], func=AF.Copy,
                                                            scale=0.25 * 0.125),
                  reads=[t_bank[bk]], writes=[t_wis])
            P.add("dve", lambda e, pb=pb: e.bn_stats(out=st6[:, 0, :], in_=pb[:, 128:192]), reads=[t_bank[bk]], writes=[t_st])
            P.add("dve", lambda e: e.bn_aggr(out=mv, in_=st6[:, 0, :]), reads=[t_st], writes=[t_mv])
            P.add("act", lambda e: e.activation(out=rstd, in_=mv[:, 1:2], func=AF.Sqrt, bias=epsb, scale=1.0),
                  reads=[t_mv, t_const], writes=[t_rstd])
            P.add("dve", lambda e: e.reciprocal(out=rstd, in_=rstd), reads=[t_rstd], writes=[t_rstd])
            P.add("dve", lambda e: e.scalar_tensor_tensor(out=nb, in0=mv[:, 0:1], scalar=-1.0, in1=rstd,
                                                          op0=ALU.mult, op1=ALU.mult), reads=[t_mv, t_rstd], writes=[t_nb])

            def kn_fn(e, pb=pb):
                e.tensor_scalar(out=kn32[:, 0:64], in0=pb[:, 128:192], scalar1=rstd, scalar2=nb, op0=ALU.mult, op1=ALU.add)
                return e.tensor_scalar(out=kn32[:, 64:128], in0=pb[:, 128:192], scalar1=rstd, scalar2=nb,
                                       op0=ALU.mult, op1=ALU.add)

            P.add("dve", kn_fn, reads=[t_bank[bk], t_rstd, t_nb], writes=[t_kn32] + t_dg)
            P.add("dve", lambda e: e.tensor_tensor(out=kn32, in0=kn32, in1=knp[:, 0, :], op=ALU.mult),
                  reads=[t_kn32, t_knp], writes=[t_kn32])
            P.add("dve", lambda e: e.tensor_tensor(out=kdup, in0=kn32, in1=knp[:, 1, :], op=ALU.add),
                  reads=[t_kn32, t_knp], writes=[t_kdup])
            bk2 = nbank()
            pb2 = bank_ap(bk2).bitcast(BF16)
            P.add("pe", lambda e, pb2=pb2: e.transpose(out=pb2[:, 0:128], in_=kdup, identity=ident),
                  reads=[t_kdup, t_const], writes=[t_bank[bk2]])
            P.add("act", lambda e, pb2=pb2, jb=jb: e.activation(out=kiT[:, jb * 128:(jb + 1) * 128], in_=pb2[:, 0:128],
                                                                  func=AF.Copy), reads=[t_bank[bk2]], writes=[t_kiT[c]])

        fmi = 0
        for ti in range(5):
            wt, wtok, d = next_tile("fm")
            ncs = len(d[1])
            wt3 = wt[:, 0:16 * 128 * ncs].rearrange("p (k n) -> p k n", n=128 * ncs)
            for j in range(ncs):
                bk = nbank()
                proj_fm(wt3, wtok, j * 128, xT, t_xT, bk)
                if fmi == 0:
                    P.add("act", lambda e, bk=bk, t0=t0: e.activation(out=kT[:, t0:t0 + 512], in_=bank_ap(bk), func=AF.Copy),
                          reads=[t_bank[bk]], writes=[t_kT[c]])
                else:
                    qc = fmi - 1
                    P.add("act", lambda e, bk=bk, qc=qc: e.activation(out=qiT[:, qc, :], in_=bank_ap(bk), func=AF.Copy),
                          reads=[t_bank[bk]], writes=[t_qiT[qc]])
                fmi += 1

        acc = h1tm[:, 0, :]
        phx["on"] = "idx"
        Rb = [h1tm[:, 1, :].bitcast(BF16)[:, k_ * 1024:(k_ + 1) * 1024] for k_ in range(4)]
        dg = [kn32[:, k_ * 64:(k_ + 1) * 64].bitcast(BF16) for k_ in range(2)]

        def idx_heads(b, c=c):
            jb = 4 * c + b
            L = (jb + 1) * 128
            halves = [(0, min(L, 1024))] + ([(1024, L)] if L > 1024 else [])

            def emit_dots(h):
                qc, half = h // 2, h % 2
                for hv, (c0h, c1h) in enumerate(halves):
                    W = c1h - c0h
                    pbase = hv * 1024
                    btoks = [t_bank[hv * 2], t_bank[hv * 2 + 1]][:(W + 511) // 512]

                    def fn(e, qc=qc, half=half, b=b, c0h=c0h, W=W, pbase=pbase):
                        ins = None
                        for p0 in range(0, W, 512):
                            w_ = min(512, W - p0)
                            ins = e.matmul(ps[:, pbase + p0:pbase + p0 + w_],
                                           lhsT=qiT[64 * half:64 * half + 64, qc, b * 128:(b + 1) * 128],
                                           rhs=kiT[64 * half:64 * half + 64, c0h + p0:c0h + p0 + w_],
                                           start=True, stop=True)
                        return ins

                    P.add("pe", fn, reads=[t_qiT[qc]] + t_kiT[:c + 1], writes=btoks)

            def emit_relu(h, hv):
                c0h, c1h = halves[hv]
                W = c1h - c0h
                pbase = hv * 1024
                btoks = [t_bank[hv * 2], t_bank[hv * 2 + 1]][:(W + 511) // 512]
                rk = hv * 2 + (h % 2)
                relu = Rb[rk][:, 0:W]
                P.add("act", lambda e, relu=relu, pbase=pbase, W=W: e.activation(
                    out=relu, in_=ps[:, pbase:pbase + W], func=AF.Relu), reads=btoks, writes=[t_R[rk]])

            def emit_hs(h, hv):
                c0h, c1h = halves[hv]
                W = c1h - c0h
                dk = h % 2
                rk = hv * 2 + (h % 2)
                relu = Rb[rk][:, 0:W]
                atoks = [t_bank[4 + (c0h + p0) // 512] for p0 in range(0, W, 512)]

                def hs_fn(e, relu=relu, dk=dk, c0h=c0h, W=W, h=h):
                    ins = None
                    for p0 in range(0, W, 512):
                        w_ = min(512, W - p0)
                        ins = e.matmul(ps[:, 2048 + c0h + p0:2048 + c0h + p0 + w_], lhsT=dg[dk],
                                       rhs=relu[:, p0:p0 + w_], start=(h == 0), stop=(h == 15))
                    return ins

                P.add("pe", hs_fn, reads=[t_R[rk], t_dg[dk]], writes=atoks)

            emit_dots(0)
            for h in range(16):
                dk = h % 2
                P.add("pool", lambda e, dk=dk, b=b, h=h: e.affine_select(
                    out=dg[dk], in_=wis[:, b, h:h + 1].to_broadcast([128, 128]), pattern=[[-1, 128]],
                    compare_op=ALU.is_equal, fill=0.0, base=0, channel_multiplier=1),
                      reads=[t_wis], writes=[t_dg[dk], t_kn32])
                for hv in range(len(halves)):
                    emit_relu(h, hv)
                if h + 1 < 16:
                    emit_dots(h + 1)
                for hv in range(len(halves)):
                    emit_hs(h, hv)

        def idx_evac(b, c=c):
            L = (4 * c + b + 1) * 128
            P.add("act", lambda e, L=L: e.activation(out=acc[:, 0:L], in_=ps[:, 2048:2048 + L], func=AF.Copy),
                  reads=[t_bank[4 + k_] for k_ in range((L + 511) // 512)], writes=[t_h1[0]])

        def idx_bisect(b, c=c, LC=LC):
            jb = 4 * c + b
            L = (jb + 1) * 128
            P.add("dve", lambda e, L=L: e.tensor_tensor(out=acc[:, L - 128:L], in0=acc[:, L - 128:L], in1=trimask, op=ALU.add),
                  reads=[t_h1[0], t_const], writes=[t_h1[0]])
            if dbg and c == nchunks - 1 and b == 3:
                out_ops.append(P.add("sp", lambda e: e.dma_start(out=dbg_d["d_acc"], in_=acc), reads=[t_h1[0]], dma=True))
            if jb >= 2:
                P.add("dve", lambda e, L=L: e.tensor_reduce(out=thr, in_=acc[:, 0:L - 128], axis=mybir.AxisListType.X,
                                                            op=ALU.min), reads=[t_h1[0]], writes=[t_thr])
                P.add("dve", lambda e, L=L: e.tensor_reduce(out=bh0, in_=acc[:, 0:L], axis=mybir.AxisListType.X,
                                                            op=ALU.max), reads=[t_h1[0]], writes=[t_bh0])
                P.add("dve", lambda e: e.scalar_tensor_tensor(out=bh0, in0=bh0, scalar=1.0, in1=thr, op0=ALU.mult,
                                                              op1=ALU.subtract), reads=[t_bh0, t_thr], writes=[t_bh0])
                P.add("dve", lambda e: e.tensor_scalar(out=bh0, in0=bh0, scalar1=1.0009765625, scalar2=1e-30,
                                                       op0=ALU.mult, op1=ALU.add), reads=[t_bh0], writes=[t_bh0])
                for it in range(NBIS):
                    sc_ = 2.0 ** -(it + 1)
                    P.add("dve", lambda e, sc_=sc_: e.scalar_tensor_tensor(out=bmid, in0=bh0, scalar=sc_, in1=thr,
                                                                           op0=ALU.mult, op1=ALU.add),
                          reads=[t_bh0, t_thr], writes=[t_bmid])
                    P.add("dve", lambda e, L=L: e.tensor_scalar(out=maskrow[:, 0:L], in0=acc[:, 0:L], scalar1=bmid,
                                                                scalar2=None, op0=ALU.is_ge, op1=ALU.add, accum_out=bcnt),
                          reads=[t_h1[0], t_bmid], writes=[t_maskrow, t_bcnt])
                    P.add("dve", lambda e: e.tensor_scalar(out=bch, in0=bcnt, scalar1=255.5, scalar2=bh0,
                                                           op0=ALU.is_ge, op1=ALU.mult),
                          reads=[t_bcnt, t_bh0], writes=[t_bch])
                    P.add("dve", lambda e, sc_=sc_: e.scalar_tensor_tensor(out=thr, in0=bch, scalar=sc_, in1=thr,
                                                                           op0=ALU.mult, op1=ALU.add),
                          reads=[t_bch, t_thr], writes=[t_thr])
            else:
                P.add("dve", lambda e: e.memset(thr, -1.0e29), writes=[t_thr])
            P.add("dve", lambda e, L=L: e.tensor_scalar(out=maskrow[:, 0:L], in0=acc[:, 0:L], scalar1=thr, scalar2=MASKNEG,
                                                        op0=ALU.is_lt, op1=ALU.mult), reads=[t_h1[0], t_thr],
                  writes=[t_maskrow])
            if L < LC:
                P.add("dve", lambda e, L=L, LC=LC: e.memset(maskrow[:, L:LC], MASKNEG), writes=[t_maskrow])

        def idx_transposes(b, c=c, NKB=NKB):
            for i0 in range(0, NKB, 8):
                n8 = min(8, NKB - i0)
                bk = nbank()
                pb = bank_ap(bk).bitcast(BF16)

                def fn(e, i0=i0, n8=n8, pb=pb):
                    ins = None
                    for j in range(n8):
                        ins = e.transpose(out=pb[:, j * 128:(j + 1) * 128],
                                          in_=maskrow[:, (i0 + j) * 128:(i0 + j + 1) * 128], identity=ident)
                    return ins

                P.add("pe", fn, reads=[t_maskrow, t_const], writes=[t_bank[bk]])
                P.add("act", lambda e, i0=i0, n8=n8, pb=pb, b=b: e.activation(
                    out=big[:, 32 + i0:32 + i0 + n8, b * 128:(b + 1) * 128],
                    in_=pb[:, 0:n8 * 128].rearrange("p (j t) -> p j t", t=128), func=AF.Copy),
                      reads=[t_bank[bk]], writes=t_big[32 + i0:32 + i0 + n8])


        def lru_gen(n, c=c):
            for _once in range(1):
                wt, wtok, d = next_tile("lru")
                wt3 = wt[:, 0:4096].rearrange("p (k n) -> p k n", n=256)
                gb_ = gbuf[n % 2]
                gtok = t_gbuf[n % 2]
                P.add("pool", lambda e, gb_=gb_, wt=wt: e.tensor_copy(out=gb_, in_=wt[:, 4096:4352]), reads=[wtok], writes=[gtok])
                g4 = gb_.rearrange("p (a k) -> p a k", a=2)
                bx_, bg_, br, bi = [4 * (n % 2) + k_ for k_ in range(4)]
                proj_fm(wt3, wtok, 0, xT, t_xT, bx_)
                proj_fm(wt3, wtok, 128, xT, t_xT, bg_)
                if n % 2 == 0:
                    sl = [tmp[:, i, :] for i in range(6)]
                    tk = [t_tmp[i] for i in range(6)]
                else:
                    sl = [h1tm[:, 2 + i // 3, (i % 3) * 682:(i % 3) * 682 + 516] for i in range(6)]
                    tk = [t_B[i] for i in range(6)]
                xcb = xcbs[n % 2]
                t_xcb = t_xcbs[n % 2]
                xs, xc, ra, igu, hs, mu = sl
                t_xs, t_xc_, t_ra, t_igu, t_hs, t_mu = tk
                gg, t_gg = xs, t_xs
                P.add("dve", lambda e, n=n, xs=xs: e.tensor_copy(out=xs[:, 0:3], in_=lruc[:, n, :]), reads=[t_lruc],
                      writes=[t_xs])
                P.add("act", lambda e, xs=xs, bx_=bx_: e.activation(out=xs[:, 3:515], in_=bank_ap(bx_), func=AF.Copy),
                      reads=[t_bank[bx_]], writes=[t_xs])
                yield
                P.add("dve", lambda e, n=n, xs=xs: e.tensor_copy(out=lruc[:, n, :], in_=xs[:, 512:515]), reads=[t_xs],
                      writes=[t_lruc])
                P.add("dve", lambda e, n=n, xs=xs, xc=xc: e.tensor_scalar(
                    out=xc[:, 0:512], in0=xs[:, 3:515], scalar1=pvec[:, PV_LCW + n * 4 + 3:PV_LCW + n * 4 + 4],
                    scalar2=pvec[:, PV_LCB + n:PV_LCB + n + 1], op0=ALU.mult, op1=ALU.add),
                      reads=[t_xs, t_pvec], writes=[t_xc_])
                for j in range(3):
                    P.add("dve", lambda e, n=n, xs=xs, xc=xc, j=j: e.scalar_tensor_tensor(
                        out=xc[:, 0:512], in0=xs[:, j:j + 512], scalar=pvec[:, PV_LCW + n * 4 + j:PV_LCW + n * 4 + j + 1],
                        in1=xc[:, 0:512], op0=ALU.mult, op1=ALU.add), reads=[t_xs, t_xc_, t_pvec], writes=[t_xc_])
                yield
                P.add("act", lambda e, xc=xc, xcb=xcb: e.activation(out=xcb, in_=xc[:, 0:512], func=AF.Copy), reads=[t_xc_],
                      writes=[t_xcb])
                P.add("pe", lambda e, br=br, g4=g4, xcb=xcb: e.matmul(bank_ap(br), lhsT=g4[:, 0, :], rhs=xcb, start=True, stop=True),
                      reads=[gtok, t_xcb], writes=[t_bank[br]])
                P.add("pe", lambda e, bi=bi, g4=g4, xcb=xcb: e.matmul(bank_ap(bi), lhsT=g4[:, 1, :], rhs=xcb, start=True, stop=True),
                      reads=[gtok, t_xcb], writes=[t_bank[bi]])
                P.add("act", lambda e, n=n, br=br, ra=ra: e.activation(out=ra[:, 0:512], in_=bank_ap(br), func=AF.Sigmoid,
                                                                        bias=pvec[:, PV_BA + n:PV_BA + n + 1]),
                      reads=[t_bank[br], t_pvec], writes=[t_ra])
                P.add("act", lambda e, n=n, bi=bi, igu=igu: e.activation(out=igu[:, 0:512], in_=bank_ap(bi), func=AF.Sigmoid,
                                                                          bias=pvec[:, PV_BX + n:PV_BX + n + 1]),
                      reads=[t_bank[bi], t_pvec], writes=[t_igu])
                yield
                P.add("act", lambda e, n=n, ra=ra, mu=mu: e.activation(out=mu[:, 0:512], in_=ra[:, 0:512], func=AF.Exp,
                                                                        scale=cneg2[:, n:n + 1]),
                      reads=[t_ra, t_cneg], writes=[t_mu])
                P.add("act", lambda e, n=n, ra=ra: e.activation(out=ra[:, 0:512], in_=ra[:, 0:512], func=AF.Exp,
                                                                 scale=cneg[:, n:n + 1]),
                      reads=[t_ra, t_cneg], writes=[t_ra])
                yield
                P.add("dve", lambda e, mu=mu: e.tensor_scalar(out=mu[:, 0:512], in0=mu[:, 0:512], scalar1=-1.0, scalar2=1.0,
                                                              op0=ALU.mult, op1=ALU.add), reads=[t_mu], writes=[t_mu])
                P.add("act", lambda e, mu=mu: e.activation(out=mu[:, 0:512], in_=mu[:, 0:512], func=AF.Sqrt),
                      reads=[t_mu], writes=[t_mu])
                yield
                P.add("act", lambda e, bg_=bg_, gg=gg: e.activation(out=gg[:, 0:512], in_=bank_ap(bg_), func=AF.Gelu_apprx_tanh),
                      reads=[t_bank[bg_]], writes=[t_gg])
                P.add("dve", lambda e, igu=igu, xc=xc: e.tensor_tensor(out=igu[:, 0:512], in0=igu[:, 0:512], in1=xc[:, 0:512],
                                                                       op=ALU.mult),
                      reads=[t_igu, t_xc_], writes=[t_igu])
                yield
                if c == 0:
                    P.add("dve", lambda e, mu=mu: e.memset(mu[:, 0:1], 1.0), writes=[t_mu])
                P.add("dve", lambda e, mu=mu, igu=igu: e.tensor_tensor(out=igu[:, 0:512], in0=igu[:, 0:512], in1=mu[:, 0:512],
                                                                       op=ALU.mult),
                      reads=[t_igu, t_mu], writes=[t_igu])
                P.add("dve", lambda e, n=n, ra=ra, igu=igu, hs=hs: e.tensor_tensor_scan(
                    out=hs[:, 0:512], data0=ra[:, 0:512], data1=igu[:, 0:512], initial=hprev[:, n:n + 1],
                    op0=ALU.mult, op1=ALU.add), reads=[t_ra, t_igu, t_hprev], writes=[t_hs])
                yield
                P.add("dve", lambda e, n=n, hs=hs: e.tensor_copy(out=hprev[:, n:n + 1], in_=hs[:, 511:512]),
                      reads=[t_hs], writes=[t_hprev])
                P.add("dve", lambda e, n=n, gg=gg, hs=hs: e.tensor_tensor(out=big[:, n, :], in0=gg[:, 0:512], in1=hs[:, 0:512],
                                                                          op=ALU.mult),
                      reads=[t_gg, t_hs], writes=[t_big[n]])
                yield

        P.add("dve", lambda e: e.memset(bmid, 0.0), reads=[], writes=t_h1[1:4] + t_B + t_relu + [t_bmid])
        idx_heads(0)
        idx_evac(0)
        for b in range(4):
            if b + 1 < 4:
                idx_heads(b + 1)
            idx_bisect(b)
            if b + 1 < 4:
                idx_evac(b + 1)
            idx_transposes(b)
        for pr_ in range(8):
            gens = [lru_gen(2 * pr_), lru_gen(2 * pr_ + 1)]
            while gens:
                for g_ in list(gens):
                    try:
                        next(g_)
                    except StopIteration:
                        gens.remove(g_)
        phx["on"] = False
        if dbg and c == nchunks - 1:
            out_ops.append(P.add("sp", lambda e: e.dma_start(out=dbg_d["d_ylru"], in_=big[:, 0:16, :]),
                                 reads=t_big[0:16], dma=True))
            out_ops.append(P.add("sp", lambda e: e.dma_start(out=dbg_d["d_tmp"], in_=tmp), reads=t_tmp, dma=True))
            out_ops.append(P.add("sp", lambda e: e.dma_start(out=dbg_d["d_cneg"], in_=cneg), reads=[t_cneg], dma=True))

        for hp in range(8):
            wt, wtok, d = next_tile("fm")
            wt3 = wt[:, 0:4096].rearrange("p (k n) -> p k n", n=256)
            for j in range(2):
                h = 2 * hp + j
                lg_rr = [h % 4]

                def nbank4():
                    lg_rr[0] = (lg_rr[0] + 1) % 4
                    return lg_rr[0]

                bq = nbank4()
                proj_fm(wt3, wtok, j * 128, xT, t_xT, bq)
                P.add("act", lambda e, bq=bq, j=j: e.activation(out=qTh[j], in_=bank_ap(bq), func=AF.Copy),
                      reads=[t_bank[bq]], writes=[t_qTh[j]])
                bo, bd = (4, 5) if j == 0 else (6, 7)
                bls = {}

                def emit_front(i, j=j, h=h):
                    bl = nbank4()
                    bls[i] = bl
                    near = [(b, 4 * c + b - i) for b in range(4) if (4 * c + b - i) in (0, 1)]

                    def fn(e, bl=bl, i=i, j=j, h=h, near=near):
                        e.matmul(bank_ap(bl), lhsT=kT[:, i * 128:(i + 1) * 128], rhs=qTh[j], start=True, stop=False)
                        ins = e.matmul(bank_ap(bl), lhsT=ident, rhs=big[:, 32 + i, :], start=False, stop=(len(near) == 0))
                        for k_, (b, dd) in enumerate(near):
                            ins = e.matmul(bank_ap(bl)[:, b * 128:(b + 1) * 128], lhsT=ident, rhs=DBf[:, h, dd, :],
                                           start=False, stop=(k_ == len(near) - 1))
                        return ins

                    P.add("pe", fn, reads=[t_kT[i // 4], t_qTh[j], t_big[32 + i], t_const, t_DB], writes=[t_bank[bl]])

                emit_front(0)
                if NKB > 1:
                    emit_front(1)
                for i in range(NKB):
                    if i + 2 < NKB:
                        emit_front(i + 2)
                    bl = bls[i]
                    es = i % 2
                    P.add("act", lambda e, bl=bl, es=es, h=h: e.activation(out=Eb[es], in_=bank_ap(bl), func=AF.Exp,
                                                                            scale=ATT_SCALE, bias=rb31[:, h:h + 1]),
                          reads=[t_bank[bl], t_const], writes=[t_Eb[es]])

                    def pv(e, i=i, es=es, bo=bo, bd=bd, NKB=NKB):
                        e.matmul(bank_ap(bo), lhsT=Vb[:, i, :], rhs=Eb[es], start=(i == 0), stop=(i == NKB - 1))
                        return e.matmul(bank_ap(bd), lhsT=ones, rhs=Eb[es], start=(i == 0), stop=(i == NKB - 1))

                    P.add("pe", pv, reads=[t_V[i // 4], t_Eb[es], t_const], writes=[t_bank[bo], t_bank[bd]])
                P.add("dve", lambda e, bd=bd: e.reciprocal(out=rc, in_=bank_ap(bd)), reads=[t_bank[bd]], writes=[t_rc])
                P.add("dve", lambda e, bo=bo, h=h: e.tensor_tensor(out=big[:, 16 + h, :], in0=bank_ap(bo), in1=rc, op=ALU.mult),
                      reads=[t_bank[bo], t_rc], writes=[t_big[16 + h]])
        if dbg and c == nchunks - 1:
            out_ops.append(P.add("sp", lambda e: e.dma_start(out=dbg_d["d_yatt"], in_=big[:, 16:32, :]),
                                 reads=t_big[16:32], dma=True))

        for n in range(16):
            so = (n % 2) * 4
            s1, s2, m1, m2 = [tmp[:, so + i, :] for i in range(4)]
            ts1, ts2, tm1, tm2 = [t_tmp[so + i] for i in range(4)]
            wt, wtok, d = next_tile("fm")
            wt3 = wt[:, 0:4096].rearrange("p (k n) -> p k n", n=256)
            bA, bG1 = nbank(), nbank()
            proj_fm(wt3, wtok, 0, big[:, 0:16, :], t_big[0:16], bA)
            proj_fm(wt3, wtok, 128, xT, t_xT, bG1)
            wt, wtok, d = next_tile("fm")
            wt3 = wt[:, 0:4096].rearrange("p (k n) -> p k n", n=256)
            bB, bG2 = nbank(), nbank()
            proj_fm(wt3, wtok, 0, big[:, 16:32, :], t_big[16:32], bB)
            proj_fm(wt3, wtok, 128, xT, t_xT, bG2)
            P.add("act", lambda e, bG1=bG1, s1=s1: e.activation(out=s1[:, 0:512], in_=bank_ap(bG1), func=AF.Sigmoid),
                  reads=[t_bank[bG1]], writes=[ts1])
            P.add("act", lambda e, bG2=bG2, s2=s2: e.activation(out=s2[:, 0:512], in_=bank_ap(bG2), func=AF.Sigmoid),
                  reads=[t_bank[bG2]], writes=[ts2])
            P.add("dve", lambda e, bA=bA, s1=s1, m1=m1: e.tensor_tensor(out=m1[:, 0:512], in0=s1[:, 0:512], in1=bank_ap(bA),
                                                                        op=ALU.mult),
                  reads=[ts1, t_bank[bA]], writes=[tm1])
            P.add("dve", lambda e, bB=bB, s2=s2, m2=m2: e.tensor_tensor(out=m2[:, 0:512], in0=s2[:, 0:512], in1=bank_ap(bB),
                                                                        op=ALU.mult),
                  reads=[ts2, t_bank[bB]], writes=[tm2])
            P.add("dve", lambda e, n=n, m1=m1, m2=m2: e.tensor_tensor(out=big[:, 32 + n, :], in0=m1[:, 0:512],
                                                                      in1=m2[:, 0:512], op=ALU.add),
                  reads=[tm1, tm2], writes=[t_big[32 + n]])
        if dbg and c == nchunks - 1:
            out_ops.append(P.add("sp", lambda e: e.dma_start(out=dbg_d["d_merged"], in_=big[:, 32:48, :]),
                                 reads=t_big[32:48], dma=True))

        for b in range(4):
            P.add("sp", lambda e, b=b, t0=t0: e.dma_start(out=h1tm[:, b, :], in_=x_d[t0 + b * 128:t0 + (b + 1) * 128, :]),
                  writes=[t_h1[b]] + t_B + t_relu, dma=True)
        P.add("sp", lambda e: e.dma_start(out=lnp[:, 0, :], in_=lnp_d[0]), writes=[t_lnp[0]], dma=True)
        P.add("sp", lambda e: e.dma_start(out=lnp[:, 1, :], in_=lnp_d[1]), writes=[t_lnp[1]], dma=True)
        for nc4 in range(4):
            banks = [nbank() for _ in range(4)]
            for kh in range(2):
                wt, wtok, d = next_tile("kn")
                wt3 = wt[:, 0:4096].rearrange("p (k n) -> p k n", n=512)
                for b in range(4):
                    def fn(e, b=b, kh=kh, wt3=wt3, bk=banks[b]):
                        ins = None
                        for kc in range(8):
                            ins = e.matmul(bank_ap(bk), lhsT=big[:, 32 + kh * 8 + kc, b * 128:(b + 1) * 128],
                                           rhs=wt3[:, kc, :], start=(kh == 0 and kc == 0), stop=(kh == 1 and kc == 7))
                        return ins

                    P.add("pe", fn, reads=[wtok] + t_big[32 + kh * 8:32 + kh * 8 + 8], writes=[t_bank[banks[b]]])
            for b in range(4):
                zc = h1tm[:, b, nc4 * 512:(nc4 + 1) * 512]
                P.add("dve", lambda e, zc=zc, bk=banks[b]: e.scalar_tensor_tensor(
                    out=zc, in0=zc, scalar=ALPHA, in1=bank_ap(bk), op0=ALU.mult, op1=ALU.add),
                      reads=[t_h1[b], t_bank[banks[b]]], writes=[t_h1[b]])
        for b in range(4):
            layer_norm_inplace(h1tm[:, b, :], t_h1[b], 0)
            transpose_rows(h1tm[:, b, :], t_h1[b], xT, t_xT[b], b)
        if dbg and c == nchunks - 1:
            out_ops.append(P.add("sp", lambda e: e.dma_start(out=dbg_d["d_h1"], in_=h1tm[:, 3, :]), reads=[t_h1[3]], dma=True))

        for n in range(48):
            wt, wtok, d = next_tile("fm")
            wt3 = wt[:, 0:4096].rearrange("p (k n) -> p k n", n=256)
            bg_, bv_ = nbank(), nbank()
            proj_fm(wt3, wtok, 0, xT, t_xT, bg_)
            proj_fm(wt3, wtok, 128, xT, t_xT, bv_)
            so = (n % 2) * 5
            xg, xv, cg, cv, gl = [tmp[:, so + i, :] for i in range(5)]
            tg, tv, tcg, tcv, tgl = [t_tmp[so + i] for i in range(5)]
            for (xx, tx, bk_, ch, co, tco) in ((xg, tg, bg_, n, cg, tcg), (xv, tv, bv_, 48 + n, cv, tcv)):
                P.add("dve", lambda e, xx=xx, ch=ch: e.tensor_copy(out=xx[:, 0:2], in_=fcar[:, ch, :]), reads=[t_fcar],
                      writes=[tx])
                P.add("act", lambda e, xx=xx, bk_=bk_: e.activation(out=xx[:, 2:514], in_=bank_ap(bk_), func=AF.Copy),
                      reads=[t_bank[bk_]], writes=[tx])
                P.add("dve", lambda e, xx=xx, ch=ch: e.tensor_copy(out=fcar[:, ch, :], in_=xx[:, 512:514]), reads=[tx],
                      writes=[t_fcar])

                P.add("dve", lambda e, xx=xx, ch=ch, co=co: e.tensor_scalar(
                    out=co[:, 0:512], in0=xx[:, 2:514], scalar1=pvec[:, PV_FCW + ch * 3 + 2:PV_FCW + ch * 3 + 3],
                    scalar2=pvec[:, PV_FCB + ch:PV_FCB + ch + 1], op0=ALU.mult, op1=ALU.add),
                      reads=[tx, t_pvec], writes=[tco])
                for j in range(2):
                    P.add("dve", lambda e, xx=xx, ch=ch, co=co, j=j: e.scalar_tensor_tensor(
                        out=co[:, 0:512], in0=xx[:, j:j + 512], scalar=pvec[:, PV_FCW + ch * 3 + j:PV_FCW + ch * 3 + j + 1],
                        in1=co[:, 0:512], op0=ALU.mult, op1=ALU.add), reads=[tx, tco, t_pvec], writes=[tco])
            P.add("act", lambda e, cg=cg, gl=gl: e.activation(out=gl[:, 0:512], in_=cg[:, 0:512], func=AF.Gelu_apprx_tanh),
                  reads=[tcg], writes=[tgl])
            P.add("dve", lambda e, n=n, gl=gl, cv=cv: e.tensor_tensor(out=big[:, n, :], in0=gl[:, 0:512], in1=cv[:, 0:512],
                                                                      op=ALU.mult),
                  reads=[tgl, tcv], writes=[t_big[n]])
        if dbg and c == nchunks - 1:
            out_ops.append(P.add("sp", lambda e: e.dma_start(out=dbg_d["d_act"], in_=big), reads=t_big, dma=True))

        P.add("sp", lambda e: e.dma_start(out=lnp[:, 0, :], in_=lnp_d[2]), writes=[t_lnp[0]], dma=True)
        P.add("sp", lambda e: e.dma_start(out=lnp[:, 1, :], in_=lnp_d[3]), writes=[t_lnp[1]], dma=True)
        for nc4 in range(4):
            banks = [nbank() for _ in range(4)]
            for kg in range(6):
                wt, wtok, d = next_tile("kn")
                wt3 = wt[:, 0:4096].rearrange("p (k n) -> p k n", n=512)
                for b in range(4):
                    def fn(e, b=b, kg=kg, wt3=wt3, bk=banks[b]):
                        ins = None
                        for kc in range(8):
                            ins = e.matmul(bank_ap(bk), lhsT=big[:, kg * 8 + kc, b * 128:(b + 1) * 128],
                                           rhs=wt3[:, kc, :], start=(kg == 0 and kc == 0), stop=(kg == 5 and kc == 7))
                        return ins

                    P.add("pe", fn, reads=[wtok] + t_big[kg * 8:kg * 8 + 8], writes=[t_bank[banks[b]]])
            for b in range(4):
                zc = h1tm[:, b, nc4 * 512:(nc4 + 1) * 512]
                P.add("dve", lambda e, zc=zc, bk=banks[b]: e.scalar_tensor_tensor(
                    out=zc, in0=zc, scalar=ALPHA, in1=bank_ap(bk), op0=ALU.mult, op1=ALU.add),
                      reads=[t_h1[b], t_bank[banks[b]]], writes=[t_h1[b]])
        for b in range(4):
            layer_norm_inplace(h1tm[:, b, :], t_h1[b], 1)
            out_ops.append(P.add("sp", lambda e, b=b, t0=t0: e.dma_start(out=out_d[t0 + b * 128:t0 + (b + 1) * 128, :],
                                                                   in_=h1tm[:, b, :]), reads=[t_h1[b]], dma=True))

    print("sbuf bytes remaining", nc.sbuf_bytes_remaining)
    P.emit(out_dma_ops=out_ops)
    return nc


def host_pack(inputs):
    f = lambda k: np.ascontiguousarray(np.asarray(inputs[k], dtype=np.float32))
    mats = {"w_in": f("w_in")[0], "w_proj_lru": f("w_proj_lru")[0], "w_proj_attn": f("w_proj_attn")[0],
            "w_out": f("w_out")[0], "ffn_w_up": f("ffn_w_up")[0], "ffn_w_down": f("ffn_w_down")[0],
            "lru_gate_a_w": f("lru_gate_a_w")[0], "lru_gate_x_w": f("lru_gate_x_w")[0]}
    wstream = pack_wstream(mats)
    pvec = np.zeros((128, PV_N), np.float32)
    chan = lambda v: v.reshape(-1, 128).T
    lcw = f("lru_conv_w")[0]
    pvec[:, PV_LCW:PV_LCW + 64] = np.stack([chan(lcw[j]) for j in range(4)], axis=2).reshape(128, 64)
    pvec[:, PV_LCB:PV_LCB + 16] = chan(f("lru_conv_b")[0])
    pvec[:, PV_BA:PV_BA + 16] = chan(f("lru_gate_a_b")[0])
    pvec[:, PV_BX:PV_BX + 16] = chan(f("lru_gate_x_b")[0])
    pvec[:, PV_LAM:PV_LAM + 16] = chan(f("lru_lambda")[0])
    fcw = f("ffn_conv_w")[0]
    pvec[:, PV_FCW:PV_FCW + 288] = np.stack([chan(fcw[j]) for j in range(3)], axis=2).reshape(128, 288)
    pvec[:, PV_FCB:PV_FCB + 96] = chan(f("ffn_conv_b")[0])
    lnp = np.stack([np.broadcast_to(f(k)[0][None, :], (128, D)) for k in ("ln1_g", "ln1_b", "ln2_g", "ln2_b")], axis=0)
    lnp = np.ascontiguousarray(lnp)
    kg, kb = f("idx_knorm_g")[0], f("idx_knorm_b")[0]
    knp = np.zeros((128, 2, 128), np.float32)
    knp[:, 0, :] = np.concatenate([kg, kg])[None, :]
    knp[:, 1, :] = np.concatenate([kb, kb])[None, :]
    rel_bias = f("rel_bias")
    ss = np.arange(128)[:, None]
    tt = np.arange(128)[None, :]
    btab = np.zeros((128, 16, 2, 128), np.float32)
    for dd in range(2):
        rel = dd * 128 + tt - ss
        bkt = np.where(rel >= 0, t5_bucket_np(np.maximum(rel, 0)), 31)
        btab[:, :, dd, :] = rel_bias[bkt].transpose(0, 2, 1)
    rb31 = np.ascontiguousarray(np.broadcast_to(rel_bias[31][None, :], (128, 16)))
    return {"wstream": wstream, "pvec": pvec, "lnp": lnp, "knp": knp, "btab": btab, "rb31": rb31}


_CACHE = {}


def run(inputs, dbg=False, nchunks=NCHUNK, ncores=8, trace=False):
    key = (dbg, nchunks)
    shared = host_pack(inputs)
    x = np.asarray(inputs["x"], dtype=np.float32)
    nc = build_program(dbg=dbg, nchunks=nchunks)
    in_maps = []
    for b in range(ncores):
        m = dict(shared)
        m["x"] = np.ascontiguousarray(x[b])
        in_maps.append(m)
    res = run_bass_kernel_spmd(nc, in_maps, core_ids=list(range(ncores)), trace=trace)
    return res


def kernel(**inputs):
    res = run(inputs)
    out = np.stack([r["out"] for r in res.results], axis=0).astype(np.float32)
    return out
```

```python
import math
import os
import contextlib
import numpy as np
import concourse.bass as bass
import concourse.mybir as mybir
from concourse.bass_utils import run_bass_kernel_spmd

F32 = mybir.dt.float32
BF16 = mybir.dt.bfloat16
AF = mybir.ActivationFunctionType
ALU = mybir.AluOpType

D = 2048
S = 2048
T = 512
NCHUNK = S // T
DFF = 6144
NEG = -1.0e30
ALPHA = 2.0 ** 0.25
LN_EPS = 1e-5
ATT_SCALE = 128.0 ** -0.5
O_LRUX, O_LRUG, O_Q, O_K, O_V, O_QI, O_KI, O_WI, O_GL, O_GA = 0, 2048, 4096, 6144, 6272, 6400, 7424, 7488, 7504, 9552

ENGS = ["pe", "act", "dve", "pool", "sp"]
NDMA_SEMS = 12
NSLOT = 4
STRICT_SYNC = True
NBIS = 20
MASKNEG = -30000.0
TILE_E = 4352


class Tok:
    __slots__ = ("w", "r")

    def __init__(self):
        self.w = None
        self.r = []


class Op:
    __slots__ = ("eng", "fn", "deps", "is_dma", "dslot", "dval", "sig", "cnt", "prev_dma")


class Prog:
    def __init__(self, nc):
        self.nc = nc
        self.ops = {e: [] for e in ENGS}
        self.ndma = {e: 0 for e in ENGS}
        self.dma_hist = {e: [] for e in ENGS}

    def add(self, eng, fn, reads=(), writes=(), dma=False):
        op = Op()
        op.eng, op.fn, op.is_dma = eng, fn, dma
        op.deps, op.sig, op.cnt, op.prev_dma, op.dslot, op.dval = [], False, None, None, None, None
        seen = set()
        rawset = set(id(t.w) for t in reads if t.w is not None)

        def consider(d):
            if d is None or id(d) in seen:
                return
            seen.add(id(d))
            if (not d.is_dma) and (not dma) and d.eng == eng:
                if eng == "pe":
                    return
                if id(d) not in rawset and not STRICT_SYNC:
                    return
            op.deps.append(d)

        for t in reads:
            consider(t.w)
        for t in writes:
            consider(t.w)
            for r in t.r:
                consider(r)
        for t in reads:
            t.r.append(op)
        for t in writes:
            t.w = op
            t.r = []
        if dma:
            m = self.ndma[eng]
            self.ndma[eng] = m + 1
            op.dslot = m % NDMA_SEMS
            op.dval = 16 * (m // NDMA_SEMS + 1)
            if m >= NDMA_SEMS:
                op.prev_dma = self.dma_hist[eng][m - NDMA_SEMS]
            self.dma_hist[eng].append(op)
        self.ops[eng].append(op)
        return op

    def emit(self, out_dma_ops=()):
        nc = self.nc
        for e in ENGS:
            for op in self.ops[e]:
                for d in op.deps:
                    if not d.is_dma:
                        d.sig = True
        for e in ENGS:
            c = 0
            for op in self.ops[e]:
                if (not op.is_dma) and op.sig:
                    c += 1
                    op.cnt = c
        with contextlib.ExitStack() as st:
            csem = {e: st.enter_context(nc.semaphore("c_" + e)) for e in ENGS}
            dsem = {e: [st.enter_context(nc.semaphore("d_%s_%d" % (e, i))) for i in range(NDMA_SEMS)]
                    for e in ENGS if self.ndma[e] > 0}
            block = st.enter_context(nc.Block())
            prog = self

            def run(ename, eng):
                waited = {}

                def wait(key, sem, val):
                    if waited.get(key, 0) >= val:
                        return
                    waited[key] = val
                    eng.wait_ge(sem, val)

                for op in prog.ops[ename]:
                    for d in op.deps:
                        if d.is_dma:
                            wait(("d", d.eng, d.dslot), dsem[d.eng][d.dslot], d.dval)
                        else:
                            wait(("c", d.eng), csem[d.eng], d.cnt)
                    if op.is_dma and op.prev_dma is not None:
                        p = op.prev_dma
                        wait(("d", p.eng, p.dslot), dsem[p.eng][p.dslot], p.dval)
                    ins = op.fn(eng)
                    if op.is_dma:
                        ins.then_inc(dsem[ename][op.dslot], 16)
                    elif op.sig:
                        ins.then_inc(csem[ename], 1)
                if ename == "sp":
                    for d in out_dma_ops:
                        wait(("d", d.eng, d.dslot), dsem[d.eng][d.dslot], d.dval)

            @block.tensor
            def _(eng):
                run("pe", eng)

            @block.scalar
            def _(eng):
                run("act", eng)

            @block.vector
            def _(eng):
                run("dve", eng)

            @block.gpsimd
            def _(eng):
                run("pool", eng)

            @block.sync
            def _(eng):
                run("sp", eng)


def tile_plan():
    plan = [("tm",)]
    fm = [("w_in", O_K)] + [("w_in", O_QI + 128 * i) for i in range(8)]
    plan.append(("fm", [fm[0], fm[1]]))
    plan.append(("fm", [fm[2], fm[3]]))
    plan.append(("fm", [fm[4], fm[5]]))
    plan.append(("fm", [fm[6], fm[7]]))
    plan.append(("fm", [fm[8]]))
    for n in range(16):
        plan.append(("lru", [("w_in", O_LRUX + 128 * n), ("w_in", O_LRUG + 128 * n)], n))
    for i in range(8):
        plan.append(("fm", [("w_in", O_Q + 256 * i), ("w_in", O_Q + 256 * i + 128)]))
    for n in range(16):
        plan.append(("fm", [("w_proj_lru", 128 * n), ("w_in", O_GL + 128 * n)]))
        plan.append(("fm", [("w_proj_attn", 128 * n), ("w_in", O_GA + 128 * n)]))
    for nc4 in range(4):
        for kh in range(2):
            plan.append(("kn", "w_out", nc4, kh))
    for n in range(48):
        plan.append(("fm", [("ffn_w_up", 128 * n), ("ffn_w_up", DFF + 128 * n)]))
    for nc4 in range(4):
        for kg in range(6):
            plan.append(("kn", "ffn_w_down", nc4, kg))
    return plan


def tile_elems(desc):
    if desc[0] == "tm":
        return 16 * 208
    if desc[0] == "fm":
        return 16 * 128 * len(desc[1])
    if desc[0] == "lru":
        return 4096 + 256
    return 4096


def pack_wstream(mats):
    plan = tile_plan()
    out = np.zeros((len(plan), 128, TILE_E), np.float32)
    w_in = mats["w_in"]
    for i, d in enumerate(plan):
        if d[0] == "tm":
            cols = np.concatenate([np.arange(O_V, O_V + 128), np.arange(O_KI, O_KI + 64), np.arange(O_WI, O_WI + 16)])
            blk = w_in[:, cols].reshape(16, 128, 208).transpose(1, 0, 2)
            out[i, :, :16 * 208] = blk.reshape(128, -1)
        elif d[0] == "fm":
            parts = [mats[m][:, c0:c0 + 128].reshape(16, 128, 128).transpose(1, 0, 2) for (m, c0) in d[1]]
            blk = np.concatenate(parts, axis=2)
            out[i, :, :blk.shape[1] * blk.shape[2]] = blk.reshape(128, -1)
        elif d[0] == "lru":
            parts = [mats[m][:, c0:c0 + 128].reshape(16, 128, 128).transpose(1, 0, 2) for (m, c0) in d[1]]
            blk = np.concatenate(parts, axis=2)
            out[i, :, :4096] = blk.reshape(128, -1)
            n = d[2]
            out[i, :, 4096:4096 + 128] = mats["lru_gate_a_w"][n]
            out[i, :, 4096 + 128:4096 + 256] = mats["lru_gate_x_w"][n]
        else:
            _, m, nc4, kg = d
            blk = mats[m][kg * 1024:(kg + 1) * 1024, nc4 * 512:(nc4 + 1) * 512].reshape(8, 128, 512).transpose(1, 0, 2)
            out[i, :, :4096] = blk.reshape(128, -1)
    return out


PV_LCW, PV_LCB, PV_BA, PV_BX, PV_LAM, PV_FCW, PV_FCB, PV_N = 0, 64, 80, 96, 112, 128, 416, 512


def t5_bucket_np(rel):
    rel = np.asarray(rel)
    nf = np.maximum(rel, 1).astype(np.float32)
    large = 16 + (np.log(nf / np.float32(16)) / np.float32(math.log(128 / 16)) * np.float32(16)).astype(np.int32)
    large = np.minimum(large, 31)
    return np.where(rel < 16, np.maximum(rel, 0), large)


def build_program(dbg=False, nchunks=NCHUNK):
    nc = bass.Bass("TRN2", target_bir_lowering=False)
    plan = tile_plan()
    NT = len(plan)
    x_d = nc.dram_tensor("x", [S, D], F32, kind="ExternalInput").ap()
    ws_d = nc.dram_tensor("wstream", [NT, 128, TILE_E], F32, kind="ExternalInput").ap()
    pvec_d = nc.dram_tensor("pvec", [128, PV_N], F32, kind="ExternalInput").ap()
    lnp_d = nc.dram_tensor("lnp", [4, 128, D], F32, kind="ExternalInput").ap()
    knp_d = nc.dram_tensor("knp", [128, 2, 128], F32, kind="ExternalInput").ap()
    btab_d = nc.dram_tensor("btab", [128, 16, 2, 128], F32, kind="ExternalInput").ap()
    rb31_d = nc.dram_tensor("rb31", [128, 16], F32, kind="ExternalInput").ap()
    out_d = nc.dram_tensor("out", [S, D], F32, kind="ExternalOutput").ap()
    dbg_d = {}
    if dbg:
        for nm, shp, dt_ in [("d_ylru", [128, 16, 512], BF16), ("d_yatt", [128, 16, 512], BF16),
                             ("d_acc", [128, 2048], F32), ("d_maskT", [128, 16, 512], BF16),
                             ("d_h1", [128, 2048], F32), ("d_merged", [128, 16, 512], BF16),
                             ("d_kT", [128, 2048], BF16), ("d_kiT", [128, 2048], BF16), ("d_V", [128, 16, 128], BF16),
                             ("d_qiT", [128, 8, 512], BF16), ("d_xT", [128, 16, 512], BF16), ("d_wis", [128, 4, 16], F32),
                             ("d_act", [128, 48, 512], BF16), ("d_tmp", [128, 10, 516], F32), ("d_cneg", [128, 16], F32)]:
            dbg_d[nm] = nc.dram_tensor(nm, shp, dt_, kind="ExternalOutput").ap()

    P = Prog(nc)
    A = nc.alloc_sbuf_tensor

    def sb(name, shape, dt_):
        return A(name, shape, dt_).ap()

    xT = sb("xT", [128, 16, 512], BF16)
    big = sb("big", [128, 48, 512], BF16)
    h1tm = sb("h1tm", [128, 4, 2048], F32)
    lnp = sb("lnp_s", [128, 2, 2048], F32)
    qiT = sb("qiT", [128, 8, 512], BF16)
    stg = sb("stg", [128, 2048], BF16)
    maskrow = stg
    NTMP = 10
    tmp = sb("tmp", [128, NTMP, 516], F32)
    xcbs = [sb("xcb%d" % i, [128, 512], BF16) for i in range(2)]
    qTh = [tmp[:, j, 0:256].bitcast(BF16) for j in range(2)]
    Eb = [tmp[:, 2 + j, 0:256].bitcast(BF16) for j in range(2)]
    Pm = [tmp[:, 4 + j, 0:256].bitcast(BF16) for j in range(2)]
    rc = tmp[:, 6, 0:512]
    kT = sb("kT", [128, 2048], BF16)
    Vb = sb("Vb", [128, 16, 128], BF16)
    kiT = sb("kiT", [128, 2048], BF16)
    ident = sb("ident", [128, 128], BF16)
    ones = sb("ones", [128, 128], BF16)
    trimask = sb("trimask", [128, 128], F32)
    DBf = sb("DBf", [128, 16, 2, 128], BF16)
    rb31 = sb("rb31_s", [128, 16], F32)
    pvec = sb("pvec_s", [128, PV_N], F32)
    cneg = sb("cneg", [128, 16], F32)
    cneg2 = sb("cneg2", [128, 16], F32)
    hprev = sb("hprev", [128, 16], F32)
    lruc = sb("lruc", [128, 16, 3], F32)
    fcar = sb("fcar", [128, 96, 2], F32)
    wis = sb("wis", [128, 4, 16], F32)
    knp = sb("knp_s", [128, 2, 128], F32)
    kdup = sb("kdup", [128, 128], BF16)
    kn32 = sb("kn32", [128, 128], F32)
    st6 = sb("st6", [128, 4, 6], F32)
    mv = sb("mv", [128, 2], F32)
    rstd = sb("rstd", [128, 1], F32)
    nb = sb("nb", [128, 1], F32)
    m8 = sb("m8", [128, 8], F32)
    thr = sb("thr", [128, 1], F32)
    epsb = sb("epsb", [128, 1], F32)
    bh0 = sb("bh0", [128, 1], F32)
    bmid = sb("bmid", [128, 1], F32)
    bcnt = sb("bcnt", [128, 1], F32)
    bch = sb("bch", [128, 1], F32)
    t_bh0, t_bmid, t_bcnt, t_bch = Tok(), Tok(), Tok(), Tok()
    ring = [sb("ring%d" % i, [128, TILE_E], BF16) for i in range(NSLOT)]
    ps = nc.alloc_psum_tensor("ps", [128, 4096], F32).ap()

    t_xT = [Tok() for _ in range(4)]
    t_big = [Tok() for _ in range(48)]
    t_h1 = [Tok() for _ in range(4)]
    t_lnp = [Tok(), Tok()]
    t_qiT = [Tok() for _ in range(8)]
    t_stg = Tok()
    t_xcbs = [Tok(), Tok()]
    t_B = [Tok() for _ in range(9)]
    t_relu = [Tok() for _ in range(4)]
    t_R = t_relu
    t_dg = [Tok(), Tok()]
    t_maskrow = t_stg
    t_tmp = [Tok() for _ in range(NTMP)]
    t_qTh = [t_tmp[0], t_tmp[1]]
    t_Eb = [t_tmp[2], t_tmp[3]]
    t_Pm = [t_tmp[4], t_tmp[5]]
    t_rc = t_tmp[6]
    t_kT = [Tok() for _ in range(4)]
    t_V = [Tok() for _ in range(4)]
    t_kiT = [Tok() for _ in range(4)]
    t_const, t_DB, t_pvec, t_cneg, t_hprev, t_lruc, t_fcar, t_wis, t_knp = [Tok() for _ in range(9)]
    t_kdup, t_kn32, t_st, t_mv, t_rstd, t_nb, t_m8, t_thr = [Tok() for _ in range(8)]
    t_ring = [Tok() for _ in range(NSLOT)]
    t_bank = [Tok() for _ in range(8)]

    def bank_ap(b, n=512):
        return ps[:, b * 512:b * 512 + n]

    bank_rr = [0]

    phx = {"on": False}

    def nbank():
        b = bank_rr[0]
        if phx["on"] == "idx":
            b = b % 4
            bank_rr[0] = (b + 1) % 4
            return b
        bank_rr[0] = (b + 1) % 8
        return b

    grp_rr = [0]

    def ngroup():
        g = grp_rr[0]
        grp_rr[0] = 1 - g
        bank_rr[0] = 0 if g == 1 else 4
        return g

    total_tiles = NT * nchunks
    wstate = {"issued": 0, "next": 0}

    def issue_upto(g):
        while wstate["issued"] <= g and wstate["issued"] < total_tiles:
            gi = wstate["issued"]
            i = gi % NT
            slot = gi % NSLOT
            ne = tile_elems(plan[i])
            P.add("pool", lambda e, i=i, slot=slot, ne=ne: e.dma_start(out=ring[slot][:, 0:ne], in_=ws_d[i, :, 0:ne]),
                  writes=[t_ring[slot]], dma=True)
            wstate["issued"] += 1

    def next_tile(expect):
        g = wstate["next"]
        wstate["next"] += 1
        i = g % NT
        assert plan[i][0] == expect, (plan[i], expect)
        issue_upto(g + NSLOT - 1)
        slot = g % NSLOT
        return ring[slot], t_ring[slot], plan[i]

    P.add("sp", lambda e: e.dma_start(out=pvec, in_=pvec_d), writes=[t_pvec], dma=True)
    P.add("sp", lambda e: e.dma_start(out=knp, in_=knp_d), writes=[t_knp], dma=True)
    bstage = h1tm[:, 0:2, :].rearrange("p a f -> p (a f)").rearrange("p (h d t) -> p h d t", h=16, d=2)
    P.add("sp", lambda e: e.dma_start(out=bstage, in_=btab_d), writes=[t_h1[0], t_h1[1]], dma=True)
    P.add("sp", lambda e: e.dma_start(out=rb31, in_=rb31_d), writes=[t_const], dma=True)
    issue_upto(NSLOT - 2)

    zsrc = tmp[:, 9, 0:128]
    gbuf = [sb("gbuf%d" % i, [128, 256], BF16) for i in range(2)]
    t_gbuf = [Tok(), Tok()]
    t_z = t_tmp[9]
    P.add("pool", lambda e: e.memset(zsrc, 0.0), writes=[t_z])
    P.add("pool", lambda e: e.affine_select(out=ident, in_=zsrc, pattern=[[-1, 128]], compare_op=ALU.not_equal, fill=1.0,
                                            base=0, channel_multiplier=1), reads=[t_z], writes=[t_const])
    P.add("pool", lambda e: e.affine_select(out=trimask, in_=zsrc, pattern=[[-1, 128]], compare_op=ALU.is_ge, fill=NEG,
                                            base=0, channel_multiplier=1), reads=[t_z], writes=[t_const])

    def setup_consts(e):
        e.memset(ones, 1.0)
        e.memset(hprev, 0.0)
        e.memset(epsb, LN_EPS)
        e.memset(lruc, 0.0)
        return e.memset(fcar, 0.0)

    P.add("pool", setup_consts, writes=[t_const, t_hprev, t_lruc, t_fcar])
    P.add("act", lambda e: e.activation(out=cneg, in_=pvec[:, PV_LAM:PV_LAM + 16], func=AF.Exp, scale=-1.0),
          reads=[t_pvec], writes=[t_cneg])
    P.add("dve", lambda e: e.tensor_scalar(out=cneg, in0=cneg, scalar1=1.0, scalar2=None, op0=ALU.add),
          reads=[t_cneg], writes=[t_cneg])
    P.add("act", lambda e: e.activation(out=cneg, in_=cneg, func=AF.Ln), reads=[t_cneg], writes=[t_cneg])
    P.add("dve", lambda e: e.tensor_scalar(out=cneg2, in0=cneg, scalar1=-16.0, scalar2=None, op0=ALU.mult),
          reads=[t_cneg], writes=[t_cneg])
    P.add("dve", lambda e: e.tensor_scalar(out=cneg, in0=cneg, scalar1=-8.0, scalar2=None, op0=ALU.mult),
          reads=[t_cneg], writes=[t_cneg])

    def setup_db(e):
        ins = None
        for h in range(16):
            ins = e.tensor_scalar(out=DBf[:, h, :, :], in0=bstage[:, h, :, :], scalar1=rb31[:, h:h + 1],
                                  scalar2=1.0 / ATT_SCALE, op0=ALU.subtract, op1=ALU.mult)
        return ins

    P.add("dve", setup_db, reads=[t_h1[0], t_h1[1], t_const], writes=[t_DB])

    out_ops = []

    def transpose_rows(src_f32, src_tok, dstT, dst_tok, b):
        P.add("act", lambda e: e.activation(out=stg, in_=src_f32, func=AF.Copy), reads=[src_tok], writes=[t_stg])
        for half in range(2):
            bk = nbank()
            pb = bank_ap(bk).bitcast(BF16)

            def fn(e, half=half, pb=pb):
                ins = None
                for j in range(8):
                    kc = half * 8 + j
                    ins = e.transpose(out=pb[:, j * 128:(j + 1) * 128], in_=stg[:, kc * 128:(kc + 1) * 128],
                                      identity=ident)
                return ins

            P.add("pe", fn, reads=[t_stg, t_const], writes=[t_bank[bk]])
            P.add("dve", lambda e, half=half, pb=pb: e.tensor_copy(
                out=dstT[:, half * 8:(half + 1) * 8, b * 128:(b + 1) * 128],
                in_=pb.rearrange("p (j t) -> p j t", t=128)), reads=[t_bank[bk]], writes=[dst_tok])

    def proj_fm(wt3, wtok, col0, rhsT, rhs_toks, bk):
        def fn(e):
            ins = None
            for kc in range(16):
                ins = e.matmul(bank_ap(bk), lhsT=wt3[:, kc, col0:col0 + 128], rhs=rhsT[:, kc, :],
                               start=(kc == 0), stop=(kc == 15))
            return ins

        P.add("pe", fn, reads=[wtok] + list(rhs_toks), writes=[t_bank[bk]])

    def layer_norm_inplace(z, ztok, gi):
        def stats(e):
            ins = None
            for k in range(4):
                ins = e.bn_stats(out=st6[:, k, :], in_=z[:, k * 512:(k + 1) * 512])
            return ins

        P.add("dve", stats, reads=[ztok], writes=[t_st])
        P.add("dve", lambda e: e.bn_aggr(out=mv, in_=st6.rearrange("p a b -> p (a b)")), reads=[t_st], writes=[t_mv])
        P.add("act", lambda e: e.activation(out=rstd, in_=mv[:, 1:2], func=AF.Sqrt, bias=epsb, scale=1.0),
              reads=[t_mv, t_const], writes=[t_rstd])
        P.add("dve", lambda e: e.reciprocal(out=rstd, in_=rstd), reads=[t_rstd], writes=[t_rstd])
        P.add("dve", lambda e: e.scalar_tensor_tensor(out=nb, in0=mv[:, 0:1], scalar=-1.0, in1=rstd,
                                                      op0=ALU.mult, op1=ALU.mult), reads=[t_mv, t_rstd], writes=[t_nb])
        P.add("dve", lambda e: e.tensor_scalar(out=z, in0=z, scalar1=rstd, scalar2=nb, op0=ALU.mult, op1=ALU.add),
              reads=[ztok, t_rstd, t_nb], writes=[ztok])
        P.add("dve", lambda e: e.tensor_tensor(out=z, in0=z, in1=lnp[:, 0, :], op=ALU.mult),
              reads=[ztok, t_lnp[0]], writes=[ztok])
        P.add("dve", lambda e: e.tensor_tensor(out=z, in0=z, in1=lnp[:, 1, :], op=ALU.add),
              reads=[ztok, t_lnp[1]], writes=[ztok])

    xstage = tmp[:, 0:4, :].rearrange("p a f -> p (a f)")[:, 0:2048]

    def s1_block(cn, b):
        tt0 = cn * T
        P.add("sp", lambda e, b=b, tt0=tt0: e.dma_start(out=xstage, in_=x_d[tt0 + b * 128:tt0 + (b + 1) * 128, :]),
              writes=t_tmp[0:4], dma=True)
        P.add("act", lambda e: e.activation(out=stg, in_=xstage, func=AF.Copy), reads=t_tmp[0:4], writes=[t_stg])
        for half in range(2):
            bk = nbank()
            pb = bank_ap(bk).bitcast(BF16)

            def fn(e, half=half, pb=pb):
                ins = None
                for j in range(8):
                    kc = half * 8 + j
                    ins = e.transpose(out=pb[:, j * 128:(j + 1) * 128], in_=stg[:, kc * 128:(kc + 1) * 128],
                                      identity=ident)
                return ins

            P.add("pe", fn, reads=[t_stg, t_const], writes=[t_bank[bk]])
            P.add("dve", lambda e, half=half, pb=pb, b=b: e.tensor_copy(
                out=xT[:, half * 8:(half + 1) * 8, b * 128:(b + 1) * 128],
                in_=pb.rearrange("p (j t) -> p j t", t=128)), reads=[t_bank[bk]], writes=[t_xT[b]])

    for c in range(nchunks):
        t0 = c * T
        NKB = 4 * (c + 1)
        LC = NKB * 128
        if c == 0:
            for b in range(4):
                s1_block(0, b)

        wt, wtok, _ = next_tile("tm")
        wt3 = wt[:, 0:16 * 208].rearrange("p (k n) -> p k n", n=208)
        for b in range(4):
            jb = 4 * c + b
            bk = nbank()

            def fn(e, b=b, bk=bk, wt3=wt3):
                ins = None
                for kc in range(16):
                    ins = e.matmul(bank_ap(bk, 208), lhsT=xT[:, kc, b * 128:(b + 1) * 128], rhs=wt3[:, kc, :],
                                   start=(kc == 0), stop=(kc == 15))
                return ins

            P.add("pe", fn, reads=[wtok, t_xT[b]], writes=[t_bank[bk]])
            pb = bank_ap(bk, 208)
            P.add("act", lambda e, pb=pb, jb=jb: e.activation(out=Vb[:, jb, :], in_=pb[:, 0:128], func=AF.Copy),
                  reads=[t_bank[bk]], writes=[t_V[c]])
            P.add("act", lambda e, pb=pb, b=b: e.activation(out=wis[:, b, :], in_=pb[:, 192:208], func=AF.Copy,
                                                            scale=0.25 * 0.125),
                  reads=[t_bank[bk]], writes=[t_wis])
            P.add("dve", lambda e, pb=pb: e.bn_stats(out=st6[:, 0, :], in_=pb[:, 128:192]), reads=[t_bank[bk]], writes=[t_st])
            P.add("dve", lambda e: e.bn_aggr(out=mv, in_=st6[:, 0, :]), reads=[t_st], writes=[t_mv])
            P.add("act", lambda e: e.activation(out=rstd, in_=mv[:, 1:2], func=AF.Sqrt, bias=epsb, scale=1.0),
                  reads=[t_mv, t_const], writes=[t_rstd])
            P.add("dve", lambda e: e.reciprocal(out=rstd, in_=rstd), reads=[t_rstd], writes=[t_rstd])
            P.add("dve", lambda e: e.scalar_tensor_tensor(out=nb, in0=mv[:, 0:1], scalar=-1.0, in1=rstd,
                                                          op0=ALU.mult, op1=ALU.mult), reads=[t_mv, t_rstd], writes=[t_nb])

            def kn_fn(e, pb=pb):
                e.tensor_scalar(out=kn32[:, 0:64], in0=pb[:, 128:192], scalar1=rstd, scalar2=nb, op0=ALU.mult, op1=ALU.add)
                return e.tensor_scalar(out=kn32[:, 64:128], in0=pb[:, 128:192], scalar1=rstd, scalar2=nb,
                                       op0=ALU.mult, op1=ALU.add)

            P.add("dve", kn_fn, reads=[t_bank[bk], t_rstd, t_nb], writes=[t_kn32] + t_dg)
            P.add("dve", lambda e: e.tensor_tensor(out=kn32, in0=kn32, in1=knp[:, 0, :], op=ALU.mult),
                  reads=[t_kn32, t_knp], writes=[t_kn32])
            P.add("dve", lambda e: e.tensor_tensor(out=kdup, in0=kn32, in1=knp[:, 1, :], op=ALU.add),
                  reads=[t_kn32, t_knp], writes=[t_kdup])
            bk2 = nbank()
            pb2 = bank_ap(bk2).bitcast(BF16)
            P.add("pe", lambda e, pb2=pb2: e.transpose(out=pb2[:, 0:128], in_=kdup, identity=ident),
                  reads=[t_kdup, t_const], writes=[t_bank[bk2]])
            P.add("act", lambda e, pb2=pb2, jb=jb: e.activation(out=kiT[:, jb * 128:(jb + 1) * 128], in_=pb2[:, 0:128],
                                                                  func=AF.Copy), reads=[t_bank[bk2]], writes=[t_kiT[c]])

        fmi = 0
        for ti in range(5):
            wt, wtok, d = next_tile("fm")
            ncs = len(d[1])
            wt3 = wt[:, 0:16 * 128 * ncs].rearrange("p (k n) -> p k n", n=128 * ncs)
            for j in range(ncs):
                bk = nbank()
                proj_fm(wt3, wtok, j * 128, xT, t_xT, bk)
                if fmi == 0:
                    P.add("act", lambda e, bk=bk, t0=t0: e.activation(out=kT[:, t0:t0 + 512], in_=bank_ap(bk), func=AF.Copy),
                          reads=[t_bank[bk]], writes=[t_kT[c]])
                else:
                    qc = fmi - 1
                    P.add("act", lambda e, bk=bk, qc=qc: e.activation(out=qiT[:, qc, :], in_=bank_ap(bk), func=AF.Copy),
                          reads=[t_bank[bk]], writes=[t_qiT[qc]])
                fmi += 1

        acc = h1tm[:, 0, :]
        phx["on"] = "idx"
        Rb = [h1tm[:, 1, :].bitcast(BF16)[:, k_ * 1024:(k_ + 1) * 1024] for k_ in range(4)]
        dg = [kn32[:, k_ * 64:(k_ + 1) * 64].bitcast(BF16) for k_ in range(2)]

        def idx_heads(b, c=c):
            jb = 4 * c + b
            L = (jb + 1) * 128
            halves = [(0, min(L, 1024))] + ([(1024, L)] if L > 1024 else [])

            def emit_dots(h):
                qc, half = h // 2, h % 2
                for hv, (c0h, c1h) in enumerate(halves):
                    W = c1h - c0h
                    pbase = hv * 1024
                    btoks = [t_bank[hv * 2], t_bank[hv * 2 + 1]][:(W + 511) // 512]

                    def fn(e, qc=qc, half=half, b=b, c0h=c0h, W=W, pbase=pbase):
                        ins = None
                        for p0 in range(0, W, 512):
                            w_ = min(512, W - p0)
                            ins = e.matmul(ps[:, pbase + p0:pbase + p0 + w_],
                                           lhsT=qiT[64 * half:64 * half + 64, qc, b * 128:(b + 1) * 128],
                                           rhs=kiT[64 * half:64 * half + 64, c0h + p0:c0h + p0 + w_],
                                           start=True, stop=True)
                        return ins

                    P.add("pe", fn, reads=[t_qiT[qc]] + t_kiT[:c + 1], writes=btoks)

            def emit_relu(h, hv):
                c0h, c1h = halves[hv]
                W = c1h - c0h
                pbase = hv * 1024
                btoks = [t_bank[hv * 2], t_bank[hv * 2 + 1]][:(W + 511) // 512]
                rk = hv * 2 + (h % 2)
                relu = Rb[rk][:, 0:W]
                P.add("act", lambda e, relu=relu, pbase=pbase, W=W: e.activation(
                    out=relu, in_=ps[:, pbase:pbase + W], func=AF.Relu), reads=btoks, writes=[t_R[rk]])

            def emit_hs(h, hv):
                c0h, c1h = halves[hv]
                W = c1h - c0h
                dk = h % 2
                rk = hv * 2 + (h % 2)
                relu = Rb[rk][:, 0:W]
                atoks = [t_bank[4 + (c0h + p0) // 512] for p0 in range(0, W, 512)]

                def hs_fn(e, relu=relu, dk=dk, c0h=c0h, W=W, h=h):
                    ins = None
                    for p0 in range(0, W, 512):
                        w_ = min(512, W - p0)
                        ins = e.matmul(ps[:, 2048 + c0h + p0:2048 + c0h + p0 + w_], lhsT=dg[dk],
                                       rhs=relu[:, p0:p0 + w_], start=(h == 0), stop=(h == 15))
                    return ins

                P.add("pe", hs_fn, reads=[t_R[rk], t_dg[dk]], writes=atoks)

            emit_dots(0)
            for h in range(16):
                dk = h % 2
                P.add("pool", lambda e, dk=dk, b=b, h=h: e.affine_select(
                    out=dg[dk], in_=wis[:, b, h:h + 1].to_broadcast([128, 128]), pattern=[[-1, 128]],
                    compare_op=ALU.is_equal, fill=0.0, base=0, channel_multiplier=1),
                      reads=[t_wis], writes=[t_dg[dk], t_kn32])
                for hv in range(len(halves)):
                    emit_relu(h, hv)
                if h + 1 < 16:
                    emit_dots(h + 1)
                for hv in range(len(halves)):
                    emit_hs(h, hv)

        def idx_evac(b, c=c):
            L = (4 * c + b + 1) * 128
            P.add("act", lambda e, L=L: e.activation(out=acc[:, 0:L], in_=ps[:, 2048:2048 + L], func=AF.Copy),
                  reads=[t_bank[4 + k_] for k_ in range((L + 511) // 512)], writes=[t_h1[0]])

        def idx_bisect(b, c=c, LC=LC):
            jb = 4 * c + b
            L = (jb + 1) * 128
            P.add("dve", lambda e, L=L: e.tensor_tensor(out=acc[:, L - 128:L], in0=acc[:, L - 128:L], in1=trimask, op=ALU.add),
                  reads=[t_h1[0], t_const], writes=[t_h1[0]])
            if dbg and c == nchunks - 1 and b == 3:
                out_ops.append(P.add("sp", lambda e: e.dma_start(out=dbg_d["d_acc"], in_=acc), reads=[t_h1[0]], dma=True))
            if jb >= 2:
                P.add("dve", lambda e, L=L: e.tensor_reduce(out=thr, in_=acc[:, 0:L - 128], axis=mybir.AxisListType.X,
                                                            op=ALU.min), reads=[t_h1[0]], writes=[t_thr])
                P.add("dve", lambda e, L=L: e.tensor_reduce(out=bh0, in_=acc[:, 0:L], axis=mybir.AxisListType.X,
                                                            op=ALU.max), reads=[t_h1[0]], writes=[t_bh0])
                P.add("dve", lambda e: e.scalar_tensor_tensor(out=bh0, in0=bh0, scalar=1.0, in1=thr, op0=ALU.mult,
                                                              op1=ALU.subtract), reads=[t_bh0, t_thr], writes=[t_bh0])
                P.add("dve", lambda e: e.tensor_scalar(out=bh0, in0=bh0, scalar1=1.0009765625, scalar2=1e-30,
                                                       op0=ALU.mult, op1=ALU.add), reads=[t_bh0], writes=[t_bh0])
                for it in range(NBIS):
                    sc_ = 2.0 ** -(it + 1)
                    P.add("dve", lambda e, sc_=sc_: e.scalar_tensor_tensor(out=bmid, in0=bh0, scalar=sc_, in1=thr,
                                                                           op0=ALU.mult, op1=ALU.add),
                          reads=[t_bh0, t_thr], writes=[t_bmid])
                    P.add("dve", lambda e, L=L: e.tensor_scalar(out=maskrow[:, 0:L], in0=acc[:, 0:L], scalar1=bmid,
                                                                scalar2=None, op0=ALU.is_ge, op1=ALU.add, accum_out=bcnt),
                          reads=[t_h1[0], t_bmid], writes=[t_maskrow, t_bcnt])
                    P.add("dve", lambda e: e.tensor_scalar(out=bch, in0=bcnt, scalar1=255.5, scalar2=bh0,
                                                           op0=ALU.is_ge, op1=ALU.mult),
                          reads=[t_bcnt, t_bh0], writes=[t_bch])
                    P.add("dve", lambda e, sc_=sc_: e.scalar_tensor_tensor(out=thr, in0=bch, scalar=sc_, in1=thr,
                                                                           op0=ALU.mult, op1=ALU.add),
                          reads=[t_bch, t_thr], writes=[t_thr])
            else:
                P.add("dve", lambda e: e.memset(thr, -1.0e29), writes=[t_thr])
            P.add("dve", lambda e, L=L: e.tensor_scalar(out=maskrow[:, 0:L], in0=acc[:, 0:L], scalar1=thr, scalar2=MASKNEG,
                                                        op0=ALU.is_lt, op1=ALU.mult), reads=[t_h1[0], t_thr],
                  writes=[t_maskrow])
            if L < LC:
                P.add("dve", lambda e, L=L, LC=LC: e.memset(maskrow[:, L:LC], MASKNEG), writes=[t_maskrow])

        def idx_transposes(b, c=c, NKB=NKB):
            for i0 in range(0, NKB, 8):
                n8 = min(8, NKB - i0)
                bk = nbank()
                pb = bank_ap(bk).bitcast(BF16)

                def fn(e, i0=i0, n8=n8, pb=pb):
                    ins = None
                    for j in range(n8):
                        ins = e.transpose(out=pb[:, j * 128:(j + 1) * 128],
                                          in_=maskrow[:, (i0 + j) * 128:(i0 + j + 1) * 128], identity=ident)
                    return ins

                P.add("pe", fn, reads=[t_maskrow, t_const], writes=[t_bank[bk]])
                P.add("act", lambda e, i0=i0, n8=n8, pb=pb, b=b: e.activation(
                    out=big[:, 32 + i0:32 + i0 + n8, b * 128:(b + 1) * 128],
                    in_=pb[:, 0:n8 * 128].rearrange("p (j t) -> p j t", t=128), func=AF.Copy),
                      reads=[t_bank[bk]], writes=t_big[32 + i0:32 + i0 + n8])


        def lru_gen(n, c=c):
            for _once in range(1):
                wt, wtok, d = next_tile("lru")
                wt3 = wt[:, 0:4096].rearrange("p (k n) -> p k n", n=256)
                gb_ = gbuf[n % 2]
                gtok = t_gbuf[n % 2]
                P.add("pool", lambda e, gb_=gb_, wt=wt: e.tensor_copy(out=gb_, in_=wt[:, 4096:4352]), reads=[wtok], writes=[gtok])
                g4 = gb_.rearrange("p (a k) -> p a k", a=2)
                bx_, bg_, br, bi = [4 * (n % 2) + k_ for k_ in range(4)]
                proj_fm(wt3, wtok, 0, xT, t_xT, bx_)
                proj_fm(wt3, wtok, 128, xT, t_xT, bg_)
                if n % 2 == 0:
                    sl = [tmp[:, i, :] for i in range(6)]
                    tk = [t_tmp[i] for i in range(6)]
                else:
                    sl = [h1tm[:, 2 + i // 3, (i % 3) * 682:(i % 3) * 682 + 516] for i in range(6)]
                    tk = [t_B[i] for i in range(6)]
                xcb = xcbs[n % 2]
                t_xcb = t_xcbs[n % 2]
                xs, xc, ra, igu, hs, mu = sl
                t_xs, t_xc_, t_ra, t_igu, t_hs, t_mu = tk
                gg, t_gg = xs, t_xs
                P.add("dve", lambda e, n=n, xs=xs: e.tensor_copy(out=xs[:, 0:3], in_=lruc[:, n, :]), reads=[t_lruc],
                      writes=[t_xs])
                P.add("act", lambda e, xs=xs, bx_=bx_: e.activation(out=xs[:, 3:515], in_=bank_ap(bx_), func=AF.Copy),
                      reads=[t_bank[bx_]], writes=[t_xs])
                yield
                P.add("dve", lambda e, n=n, xs=xs: e.tensor_copy(out=lruc[:, n, :], in_=xs[:, 512:515]), reads=[t_xs],
                      writes=[t_lruc])
                P.add("dve", lambda e, n=n, xs=xs, xc=xc: e.tensor_scalar(
                    out=xc[:, 0:512], in0=xs[:, 3:515], scalar1=pvec[:, PV_LCW + n * 4 + 3:PV_LCW + n * 4 + 4],
                    scalar2=pvec[:, PV_LCB + n:PV_LCB + n + 1], op0=ALU.mult, op1=ALU.add),
                      reads=[t_xs, t_pvec], writes=[t_xc_])
                for j in range(3):
                    P.add("dve", lambda e, n=n, xs=xs, xc=xc, j=j: e.scalar_tensor_tensor(
                        out=xc[:, 0:512], in0=xs[:, j:j + 512], scalar=pvec[:, PV_LCW + n * 4 + j:PV_LCW + n * 4 + j + 1],
                        in1=xc[:, 0:512], op0=ALU.mult, op1=ALU.add), reads=[t_xs, t_xc_, t_pvec], writes=[t_xc_])
                yield
                P.add("act", lambda e, xc=xc, xcb=xcb: e.activation(out=xcb, in_=xc[:, 0:512], func=AF.Copy), reads=[t_xc_],
                      writes=[t_xcb])
                P.add("pe", lambda e, br=br, g4=g4, xcb=xcb: e.matmul(bank_ap(br), lhsT=g4[:, 0, :], rhs=xcb, start=True, stop=True),
                      reads=[gtok, t_xcb], writes=[t_bank[br]])
                P.add("pe", lambda e, bi=bi, g4=g4, xcb=xcb: e.matmul(bank_ap(bi), lhsT=g4[:, 1, :], rhs=xcb, start=True, stop=True),
                      reads=[gtok, t_xcb], writes=[t_bank[bi]])
                P.add("act", lambda e, n=n, br=br, ra=ra: e.activation(out=ra[:, 0:512], in_=bank_ap(br), func=AF.Sigmoid,
                                                                        bias=pvec[:, PV_BA + n:PV_BA + n + 1]),
                      reads=[t_bank[br], t_pvec], writes=[t_ra])
                P.add("act", lambda e, n=n, bi=bi, igu=igu: e.activation(out=igu[:, 0:512], in_=bank_ap(bi), func=AF.Sigmoid,
                                                                          bias=pvec[:, PV_BX + n:PV_BX + n + 1]),
                      reads=[t_bank[bi], t_pvec], writes=[t_igu])
                yield
                P.add("act", lambda e, n=n, ra=ra, mu=mu: e.activation(out=mu[:, 0:512], in_=ra[:, 0:512], func=AF.Exp,
                                                                        scale=cneg2[:, n:n + 1]),
                      reads=[t_ra, t_cneg], writes=[t_mu])
                P.add("act", lambda e, n=n, ra=ra: e.activation(out=ra[:, 0:512], in_=ra[:, 0:512], func=AF.Exp,
                                                                 scale=cneg[:, n:n + 1]),
                      reads=[t_ra, t_cneg], writes=[t_ra])
                yield
                P.add("dve", lambda e, mu=mu: e.tensor_scalar(out=mu[:, 0:512], in0=mu[:, 0:512], scalar1=-1.0, scalar2=1.0,
                                                              op0=ALU.mult, op1=ALU.add), reads=[t_mu], writes=[t_mu])
                P.add("act", lambda e, mu=mu: e.activation(out=mu[:, 0:512], in_=mu[:, 0:512], func=AF.Sqrt),
                      reads=[t_mu], writes=[t_mu])
                yield
                P.add("act", lambda e, bg_=bg_, gg=gg: e.activation(out=gg[:, 0:512], in_=bank_ap(bg_), func=AF.Gelu_apprx_tanh),
                      reads=[t_bank[bg_]], writes=[t_gg])
                P.add("dve", lambda e, igu=igu, xc=xc: e.tensor_tensor(out=igu[:, 0:512], in0=igu[:, 0:512], in1=xc[:, 0:512],
                                                                       op=ALU.mult),
                      reads=[t_igu, t_xc_], writes=[t_igu])
                yield
                if c == 0:
                    P.add("dve", lambda e, mu=mu: e.memset(mu[:, 0:1], 1.0), writes=[t_mu])
                P.add("dve", lambda e, mu=mu, igu=igu: e.tensor_tensor(out=igu[:, 0:512], in0=igu[:, 0:512], in1=mu[:, 0:512],
                                                                       op=ALU.mult),
                      reads=[t_igu, t_mu], writes=[t_igu])
                P.add("dve", lambda e, n=n, ra=ra, igu=igu, hs=hs: e.tensor_tensor_scan(
                    out=hs[:, 0:512], data0=ra[:, 0:512], data1=igu[:, 0:512], initial=hprev[:, n:n + 1],
                    op0=ALU.mult, op1=ALU.add), reads=[t_ra, t_igu, t_hprev], writes=[t_hs])
                yield
                P.add("dve", lambda e, n=n, hs=hs: e.tensor_copy(out=hprev[:, n:n + 1], in_=hs[:, 511:512]),
                      reads=[t_hs], writes=[t_hprev])
                P.add("dve", lambda e, n=n, gg=gg, hs=hs: e.tensor_tensor(out=big[:, n, :], in0=gg[:, 0:512], in1=hs[:, 0:512],
                                                                          op=ALU.mult),
                      reads=[t_gg, t_hs], writes=[t_big[n]])
                yield

        P.add("dve", lambda e: e.memset(bmid, 0.0), reads=[], writes=t_h1[1:4] + t_B + t_relu + [t_bmid])
        idx_heads(0)
        idx_evac(0)
        for b in range(4):
            if b + 1 < 4:
                idx_heads(b + 1)
            idx_bisect(b)
            if b + 1 < 4:
                idx_evac(b + 1)
            idx_transposes(b)
        for pr_ in range(8):
            gens = [lru_gen(2 * pr_), lru_gen(2 * pr_ + 1)]
            while gens:
                for g_ in list(gens):
                    try:
                        next(g_)
                    except StopIteration:
                        gens.remove(g_)
        phx["on"] = False
        if dbg and c == nchunks - 1:
            out_ops.append(P.add("sp", lambda e: e.dma_start(out=dbg_d["d_ylru"], in_=big[:, 0:16, :]),
                                 reads=t_big[0:16], dma=True))
            out_ops.append(P.add("sp", lambda e: e.dma_start(out=dbg_d["d_tmp"], in_=tmp), reads=t_tmp, dma=True))
            out_ops.append(P.add("sp", lambda e: e.dma_start(out=dbg_d["d_cneg"], in_=cneg), reads=[t_cneg], dma=True))

        for hp in range(8):
            wt, wtok, d = next_tile("fm")
            wt3 = wt[:, 0:4096].rearrange("p (k n) -> p k n", n=256)
            for j in range(2):
                h = 2 * hp + j
                lg_rr = [h % 4]

                def nbank4():
                    lg_rr[0] = (lg_rr[0] + 1) % 4
                    return lg_rr[0]

                bq = nbank4()
                proj_fm(wt3, wtok, j * 128, xT, t_xT, bq)
                P.add("act", lambda e, bq=bq, j=j: e.activation(out=qTh[j], in_=bank_ap(bq), func=AF.Copy),
                      reads=[t_bank[bq]], writes=[t_qTh[j]])
                bo, bd = (4, 5) if j == 0 else (6, 7)
                bls = {}

                def emit_front(i, j=j, h=h):
                    bl = nbank4()
                    bls[i] = bl
                    near = [(b, 4 * c + b - i) for b in range(4) if (4 * c + b - i) in (0, 1)]

                    def fn(e, bl=bl, i=i, j=j, h=h, near=near):
                        e.matmul(bank_ap(bl), lhsT=kT[:, i * 128:(i + 1) * 128], rhs=qTh[j], start=True, stop=False)
                        ins = e.matmul(bank_ap(bl), lhsT=ident, rhs=big[:, 32 + i, :], start=False, stop=(len(near) == 0))
                        for k_, (b, dd) in enumerate(near):
                            ins = e.matmul(bank_ap(bl)[:, b * 128:(b + 1) * 128], lhsT=ident, rhs=DBf[:, h, dd, :],
                                           start=False, stop=(k_ == len(near) - 1))
                        return ins

                    P.add("pe", fn, reads=[t_kT[i // 4], t_qTh[j], t_big[32 + i], t_const, t_DB], writes=[t_bank[bl]])

                emit_front(0)
                if NKB > 1:
                    emit_front(1)
                for i in range(NKB):
                    if i + 2 < NKB:
                        emit_front(i + 2)
                    bl = bls[i]
                    es = i % 2
                    P.add("act", lambda e, bl=bl, es=es, h=h: e.activation(out=Eb[es], in_=bank_ap(bl), func=AF.Exp,
                                                                            scale=ATT_SCALE, bias=rb31[:, h:h + 1]),
                          reads=[t_bank[bl], t_const], writes=[t_Eb[es]])

                    def pv(e, i=i, es=es, bo=bo, bd=bd, NKB=NKB):
                        e.matmul(bank_ap(bo), lhsT=Vb[:, i, :], rhs=Eb[es], start=(i == 0), stop=(i == NKB - 1))
                        return e.matmul(bank_ap(bd), lhsT=ones, rhs=Eb[es], start=(i == 0), stop=(i == NKB - 1))

                    P.add("pe", pv, reads=[t_V[i // 4], t_Eb[es], t_const], writes=[t_bank[bo], t_bank[bd]])
                P.add("dve", lambda e, bd=bd: e.reciprocal(out=rc, in_=bank_ap(bd)), reads=[t_bank[bd]], writes=[t_rc])
                P.add("dve", lambda e, bo=bo, h=h: e.tensor_tensor(out=big[:, 16 + h, :], in0=bank_ap(bo), in1=rc, op=ALU.mult),
                      reads=[t_bank[bo], t_rc], writes=[t_big[16 + h]])
        if dbg and c == nchunks - 1:
            out_ops.append(P.add("sp", lambda e: e.dma_start(out=dbg_d["d_yatt"], in_=big[:, 16:32, :]),
                                 reads=t_big[16:32], dma=True))

        for n in range(16):
            so = (n % 2) * 4
            s1, s2, m1, m2 = [tmp[:, so + i, :] for i in range(4)]
            ts1, ts2, tm1, tm2 = [t_tmp[so + i] for i in range(4)]
            wt, wtok, d = next_tile("fm")
            wt3 = wt[:, 0:4096].rearrange("p (k n) -> p k n", n=256)
            bA, bG1 = nbank(), nbank()
            proj_fm(wt3, wtok, 0, big[:, 0:16, :], t_big[0:16], bA)
            proj_fm(wt3, wtok, 128, xT, t_xT, bG1)
            wt, wtok, d = next_tile("fm")
            wt3 = wt[:, 0:4096].rearrange("p (k n) -> p k n", n=256)
            bB, bG2 = nbank(), nbank()
            proj_fm(wt3, wtok, 0, big[:, 16:32, :], t_big[16:32], bB)
            proj_fm(wt3, wtok, 128, xT, t_xT, bG2)
            P.add("act", lambda e, bG1=bG1, s1=s1: e.activation(out=s1[:, 0:512], in_=bank_ap(bG1), func=AF.Sigmoid),
                  reads=[t_bank[bG1]], writes=[ts1])
            P.add("act", lambda e, bG2=bG2, s2=s2: e.activation(out=s2[:, 0:512], in_=bank_ap(bG2), func=AF.Sigmoid),
                  reads=[t_bank[bG2]], writes=[ts2])
            P.add("dve", lambda e, bA=bA, s1=s1, m1=m1: e.tensor_tensor(out=m1[:, 0:512], in0=s1[:, 0:512], in1=bank_ap(bA),
                                                                        op=ALU.mult),
                  reads=[ts1, t_bank[bA]], writes=[tm1])
            P.add("dve", lambda e, bB=bB, s2=s2, m2=m2: e.tensor_tensor(out=m2[:, 0:512], in0=s2[:, 0:512], in1=bank_ap(bB),
                                                                        op=ALU.mult),
                  reads=[ts2, t_bank[bB]], writes=[tm2])
            P.add("dve", lambda e, n=n, m1=m1, m2=m2: e.tensor_tensor(out=big[:, 32 + n, :], in0=m1[:, 0:512],
                                                                      in1=m2[:, 0:512], op=ALU.add),
                  reads=[tm1, tm2], writes=[t_big[32 + n]])
        if dbg and c == nchunks - 1:
            out_ops.append(P.add("sp", lambda e: e.dma_start(out=dbg_d["d_merged"], in_=big[:, 32:48, :]),
                                 reads=t_big[32:48], dma=True))

        for b in range(4):
            P.add("sp", lambda e, b=b, t0=t0: e.dma_start(out=h1tm[:, b, :], in_=x_d[t0 + b * 128:t0 + (b + 1) * 128, :]),
                  writes=[t_h1[b]] + t_B + t_relu, dma=True)
        P.add("sp", lambda e: e.dma_start(out=lnp[:, 0, :], in_=lnp_d[0]), writes=[t_lnp[0]], dma=True)
        P.add("sp", lambda e: e.dma_start(out=lnp[:, 1, :], in_=lnp_d[1]), writes=[t_lnp[1]], dma=True)
        for nc4 in range(4):
            banks = [nbank() for _ in range(4)]
            for kh in range(2):
                wt, wtok, d = next_tile("kn")
                wt3 = wt[:, 0:4096].rearrange("p (k n) -> p k n", n=512)
                for b in range(4):
                    def fn(e, b=b, kh=kh, wt3=wt3, bk=banks[b]):
                        ins = None
                        for kc in range(8):
                            ins = e.matmul(bank_ap(bk), lhsT=big[:, 32 + kh * 8 + kc, b * 128:(b + 1) * 128],
                                           rhs=wt3[:, kc, :], start=(kh == 0 and kc == 0), stop=(kh == 1 and kc == 7))
                        return ins

                    P.add("pe", fn, reads=[wtok] + t_big[32 + kh * 8:32 + kh * 8 + 8], writes=[t_bank[banks[b]]])
            for b in range(4):
                zc = h1tm[:, b, nc4 * 512:(nc4 + 1) * 512]
                P.add("dve", lambda e, zc=zc, bk=banks[b]: e.scalar_tensor_tensor(
                    out=zc, in0=zc, scalar=ALPHA, in1=bank_ap(bk), op0=ALU.mult, op1=ALU.add),
                      reads=[t_h1[b], t_bank[banks[b]]], writes=[t_h1[b]])
        for b in range(4):
            layer_norm_inplace(h1tm[:, b, :], t_h1[b], 0)
            transpose_rows(h1tm[:, b, :], t_h1[b], xT, t_xT[b], b)
        if dbg and c == nchunks - 1:
            out_ops.append(P.add("sp", lambda e: e.dma_start(out=dbg_d["d_h1"], in_=h1tm[:, 3, :]), reads=[t_h1[3]], dma=True))

        for n in range(48):
            wt, wtok, d = next_tile("fm")
            wt3 = wt[:, 0:4096].rearrange("p (k n) -> p k n", n=256)
            bg_, bv_ = nbank(), nbank()
            proj_fm(wt3, wtok, 0, xT, t_xT, bg_)
            proj_fm(wt3, wtok, 128, xT, t_xT, bv_)
            so = (n % 2) * 5
            xg, xv, cg, cv, gl = [tmp[:, so + i, :] for i in range(5)]
            tg, tv, tcg, tcv, tgl = [t_tmp[so + i] for i in range(5)]
            for (xx, tx, bk_, ch, co, tco) in ((xg, tg, bg_, n, cg, tcg), (xv, tv, bv_, 48 + n, cv, tcv)):
                P.add("dve", lambda e, xx=xx, ch=ch: e.tensor_copy(out=xx[:, 0:2], in_=fcar[:, ch, :]), reads=[t_fcar],
                      writes=[tx])
                P.add("act", lambda e, xx=xx, bk_=bk_: e.activation(out=xx[:, 2:514], in_=bank_ap(bk_), func=AF.Copy),
                      reads=[t_bank[bk_]], writes=[tx])
                P.add("dve", lambda e, xx=xx, ch=ch: e.tensor_copy(out=fcar[:, ch, :], in_=xx[:, 512:514]), reads=[tx],
                      writes=[t_fcar])

                P.add("dve", lambda e, xx=xx, ch=ch, co=co: e.tensor_scalar(
                    out=co[:, 0:512], in0=xx[:, 2:514], scalar1=pvec[:, PV_FCW + ch * 3 + 2:PV_FCW + ch * 3 + 3],
                    scalar2=pvec[:, PV_FCB + ch:PV_FCB + ch + 1], op0=ALU.mult, op1=ALU.add),
                      reads=[tx, t_pvec], writes=[tco])
                for j in range(2):
                    P.add("dve", lambda e, xx=xx, ch=ch, co=co, j=j: e.scalar_tensor_tensor(
                        out=co[:, 0:512], in0=xx[:, j:j + 512], scalar=pvec[:, PV_FCW + ch * 3 + j:PV_FCW + ch * 3 + j + 1],
                        in1=co[:, 0:512], op0=ALU.mult, op1=ALU.add), reads=[tx, tco, t_pvec], writes=[tco])
            P.add("act", lambda e, cg=cg, gl=gl: e.activation(out=gl[:, 0:512], in_=cg[:, 0:512], func=AF.Gelu_apprx_tanh),
                  reads=[tcg], writes=[tgl])
            P.add("dve", lambda e, n=n, gl=gl, cv=cv: e.tensor_tensor(out=big[:, n, :], in0=gl[:, 0:512], in1=cv[:, 0:512],
                                                                      op=ALU.mult),
                  reads=[tgl, tcv], writes=[t_big[n]])
        if dbg and c == nchunks - 1:
            out_ops.append(P.add("sp", lambda e: e.dma_start(out=dbg_d["d_act"], in_=big), reads=t_big, dma=True))

        P.add("sp", lambda e: e.dma_start(out=lnp[:, 0, :], in_=lnp_d[2]), writes=[t_lnp[0]], dma=True)
        P.add("sp", lambda e: e.dma_start(out=lnp[:, 1, :], in_=lnp_d[3]), writes=[t_lnp[1]], dma=True)
        for nc4 in range(4):
            banks = [nbank() for _ in range(4)]
            for kg in range(6):
                wt, wtok, d = next_tile("kn")
                wt3 = wt[:, 0:4096].rearrange("p (k n) -> p k n", n=512)
                for b in range(4):
                    def fn(e, b=b, kg=kg, wt3=wt3, bk=banks[b]):
                        ins = None
                        for kc in range(8):
                            ins = e.matmul(bank_ap(bk), lhsT=big[:, kg * 8 + kc, b * 128:(b + 1) * 128],
                                           rhs=wt3[:, kc, :], start=(kg == 0 and kc == 0), stop=(kg == 5 and kc == 7))
                        return ins

                    P.add("pe", fn, reads=[wtok] + t_big[kg * 8:kg * 8 + 8], writes=[t_bank[banks[b]]])
            for b in range(4):
                zc = h1tm[:, b, nc4 * 512:(nc4 + 1) * 512]
                P.add("dve", lambda e, zc=zc, bk=banks[b]: e.scalar_tensor_tensor(
                    out=zc, in0=zc, scalar=ALPHA, in1=bank_ap(bk), op0=ALU.mult, op1=ALU.add),
                      reads=[t_h1[b], t_bank[banks[b]]], writes=[t_h1[b]])
            if c + 1 < nchunks:
                s1_block(c + 1, nc4)
        for b in range(4):
            layer_norm_inplace(h1tm[:, b, :], t_h1[b], 1)
            out_ops.append(P.add("sp", lambda e, b=b, t0=t0: e.dma_start(out=out_d[t0 + b * 128:t0 + (b + 1) * 128, :],
                                                                   in_=h1tm[:, b, :]), reads=[t_h1[b]], dma=True))

    print("sbuf bytes remaining", nc.sbuf_bytes_remaining)
    P.emit(out_dma_ops=out_ops)
    return nc


def host_pack(inputs):
    f = lambda k: np.ascontiguousarray(np.asarray(inputs[k], dtype=np.float32))
    mats = {"w_in": f("w_in")[0], "w_proj_lru": f("w_proj_lru")[0], "w_proj_attn": f("w_proj_attn")[0],
            "w_out": f("w_out")[0], "ffn_w_up": f("ffn_w_up")[0], "ffn_w_down": f("ffn_w_down")[0],
            "lru_gate_a_w": f("lru_gate_a_w")[0], "lru_gate_x_w": f("lru_gate_x_w")[0]}
    wstream = pack_wstream(mats)
    pvec = np.zeros((128, PV_N), np.float32)
    chan = lambda v: v.reshape(-1, 128).T
    lcw = f("lru_conv_w")[0]
    pvec[:, PV_LCW:PV_LCW + 64] = np.stack([chan(lcw[j]) for j in range(4)], axis=2).reshape(128, 64)
    pvec[:, PV_LCB:PV_LCB + 16] = chan(f("lru_conv_b")[0])
    pvec[:, PV_BA:PV_BA + 16] = chan(f("lru_gate_a_b")[0])
    pvec[:, PV_BX:PV_BX + 16] = chan(f("lru_gate_x_b")[0])
    pvec[:, PV_LAM:PV_LAM + 16] = chan(f("lru_lambda")[0])
    fcw = f("ffn_conv_w")[0]
    pvec[:, PV_FCW:PV_FCW + 288] = np.stack([chan(fcw[j]) for j in range(3)], axis=2).reshape(128, 288)
    pvec[:, PV_FCB:PV_FCB + 96] = chan(f("ffn_conv_b")[0])
    lnp = np.stack([np.broadcast_to(f(k)[0][None, :], (128, D)) for k in ("ln1_g", "ln1_b", "ln2_g", "ln2_b")], axis=0)
    lnp = np.ascontiguousarray(lnp)
    kg, kb = f("idx_knorm_g")[0], f("idx_knorm_b")[0]
    knp = np.zeros((128, 2, 128), np.float32)
    knp[:, 0, :] = np.concatenate([kg, kg])[None, :]
    knp[:, 1, :] = np.concatenate([kb, kb])[None, :]
    rel_bias = f("rel_bias")
    ss = np.arange(128)[:, None]
    tt = np.arange(128)[None, :]
    btab = np.zeros((128, 16, 2, 128), np.float32)
    for dd in range(2):
        rel = dd * 128 + tt - ss
        bkt = np.where(rel >= 0, t5_bucket_np(np.maximum(rel, 0)), 31)
        btab[:, :, dd, :] = rel_bias[bkt].transpose(0, 2, 1)
    rb31 = np.ascontiguousarray(np.broadcast_to(rel_bias[31][None, :], (128, 16)))
    return {"wstream": wstream, "pvec": pvec, "lnp": lnp, "knp": knp, "btab": btab, "rb31": rb31}


_CACHE = {}


def run(inputs, dbg=False, nchunks=NCHUNK, ncores=8, trace=False):
    key = (dbg, nchunks)
    shared = host_pack(inputs)
    x = np.asarray(inputs["x"], dtype=np.float32)
    nc = build_program(dbg=dbg, nchunks=nchunks)
    in_maps = []
    for b in range(ncores):
        m = dict(shared)
        m["x"] = np.ascontiguousarray(x[b])
        in_maps.append(m)
    res = run_bass_kernel_spmd(nc, in_maps, core_ids=list(range(ncores)), trace=trace)
    return res


def kernel(**inputs):
    res = run(inputs)
    out = np.stack([r["out"] for r in res.results], axis=0).astype(np.float32)
    return out
```
